# Optimizing a Trainium2 kernel written in Bass

```python
import jax, jax.numpy as jnp
from jax import lax
import numpy as np

D_MODEL = 1024
BATCH = 1
SEQ = 16384
DEPTH = 1
DEC_BATCH = 16
DEC_SEQ = 4096
PAST_LEN = 128

M_HEADS = 4
M_HEAD_DIM = 128
M_WIDTH = M_HEADS * M_HEAD_DIM
M_CHUNK = 128
A_HEADS = 8
A_KV_HEADS = 4
A_HEAD_DIM = 64
A_WIDTH = A_HEADS * A_HEAD_DIM
A_KV_WIDTH = A_KV_HEADS * A_HEAD_DIM
WINDOW = 128
A_BLOCK = 128
ROT_DIM = A_HEAD_DIM // 4
ROPE_THETA = 500000.0
D_FF = 2816
CONV_WIDTH = 3
EPS = 1e-6
D_IN = 4 * M_WIDTH + 4 * M_HEADS + A_WIDTH + 2 * A_KV_WIDTH + 2 * D_MODEL

kernel_name = "bidir_mlstm_swa_hybrid_encoder"


def _split_points():
    sizes = [M_WIDTH] * 4 + [2 * M_HEADS] * 2 + [A_WIDTH, A_KV_WIDTH, A_KV_WIDTH, D_MODEL, D_MODEL]
    return np.cumsum(sizes)[:-1].tolist()


def rmsnorm(x, w):
    xf = x.astype(jnp.float32)
    y = xf * lax.rsqrt(jnp.mean(xf * xf, axis=-1, keepdims=True) + EPS)
    return (y * w.astype(jnp.float32)).astype(x.dtype)


def mlstm_chunkwise(q, k, v, i_pre, f_pre):
    B, H, S, dh = q.shape
    L = M_CHUNK
    N = S // L
    q = q.reshape(B, H, N, L, dh)
    k = k.reshape(B, H, N, L, dh)
    v = v.reshape(B, H, N, L, dh)
    ig = i_pre.reshape(B, H, N, L)
    b = jnp.cumsum(jax.nn.log_sigmoid(f_pre).reshape(B, H, N, L), axis=-1)
    b_tot = b[..., -1]
    a = b_tot[..., None] - b + ig
    m_loc = jnp.max(a, axis=-1)
    wgt = jnp.exp(a - m_loc[..., None])
    C_loc = jnp.einsum('bhnlk,bhnlv->bhnkv', wgt[..., None] * k, v)
    n_loc = jnp.einsum('bhnl,bhnlk->bhnk', wgt, k)

    def step(carry, inp):
        C, n, m = carry
        bt, ml, Cl, nl = inp
        m_new = jnp.maximum(bt + m, ml)
        s_prev = jnp.exp(bt + m - m_new)
        s_loc = jnp.exp(ml - m_new)
        C_new = s_prev[..., None, None] * C + s_loc[..., None, None] * Cl
        n_new = s_prev[..., None] * n + s_loc[..., None] * nl
        return (C_new, n_new, m_new), (C, n, m)

    init = (jnp.zeros((B, H, dh, dh), jnp.float32), jnp.zeros((B, H, dh), jnp.float32),
            jnp.zeros((B, H), jnp.float32))
    xs = (jnp.moveaxis(b_tot, 2, 0), jnp.moveaxis(m_loc, 2, 0),
          jnp.moveaxis(C_loc, 2, 0), jnp.moveaxis(n_loc, 2, 0))
    _, (C_prev, n_prev, m_prev) = lax.scan(step, init, xs)
    C_prev = jnp.moveaxis(C_prev, 0, 2)
    n_prev = jnp.moveaxis(n_prev, 0, 2)
    m_prev = jnp.moveaxis(m_prev, 0, 2)

    D = b[..., :, None] - b[..., None, :] + ig[..., None, :]
    tril = jnp.tril(jnp.ones((L, L), dtype=bool))
    D = jnp.where(tril, D, -jnp.inf)
    inter_log = b + m_prev[..., None]
    m_t = jnp.maximum(inter_log, jnp.max(D, axis=-1))
    Dw = jnp.exp(D - m_t[..., None])
    inter_w = jnp.exp(inter_log - m_t)
    qk = jnp.einsum('bhntd,bhnsd->bhnts', q, k) * Dw
    num = jnp.einsum('bhnts,bhnsd->bhntd', qk, v) + inter_w[..., None] * jnp.einsum('bhntk,bhnkv->bhntv', q, C_prev)
    den = jnp.sum(qk, axis=-1) + inter_w * jnp.einsum('bhntk,bhnk->bhnt', q, n_prev)
    h = num / jnp.maximum(jnp.abs(den), jnp.exp(-m_t))[..., None]
    return h.reshape(B, H, S, dh)


def rope_partial(x, pos):
    half = ROT_DIM // 2
    inv = ROPE_THETA ** (-jnp.arange(half, dtype=jnp.float32) / half)
    ang = pos.astype(jnp.float32)[:, None] * inv[None, :]
    cos, sin = jnp.cos(ang), jnp.sin(ang)
    xr = x[..., :ROT_DIM].astype(jnp.float32)
    x1, x2 = xr[..., :half], xr[..., half:]
    rot = jnp.concatenate([x1 * cos - x2 * sin, x2 * cos + x1 * sin], axis=-1)
    return jnp.concatenate([rot.astype(x.dtype), x[..., ROT_DIM:]], axis=-1)


def window_attention(q, k, v, sinks):
    B, Hq, S, dh = q.shape
    Hkv = k.shape[1]
    G = Hq // Hkv
    L = A_BLOCK
    N = S // L
    qb = q.reshape(B, Hkv, G, N, L, dh)

    def band(t):
        tp = jnp.pad(t, ((0, 0), (0, 0), (L, L), (0, 0))).reshape(B, Hkv, N + 2, L, dh)
        return jnp.concatenate([tp[:, :, :-2], tp[:, :, 1:-1], tp[:, :, 2:]], axis=3)

    kb, vb = band(k), band(v)
    s = jnp.einsum('bhgnqd,bhnkd->bhgnqk', qb, kb).astype(jnp.float32) * (dh ** -0.5)
    qi = jnp.arange(L)
    kj = jnp.arange(3 * L)
    rel = kj[None, :] - L - qi[:, None]
    key_pos = jnp.arange(N)[:, None] * L - L + kj[None, :]
    mask = (jnp.abs(rel) <= WINDOW)[None, :, :] & ((key_pos >= 0) & (key_pos < S))[:, None, :]
    s = jnp.where(mask, s, -jnp.inf)
    sink = sinks.astype(jnp.float32).reshape(Hkv, G)[None, :, :, None, None, None]
    m = jnp.maximum(jnp.max(s, axis=-1, keepdims=True), sink)
    p = jnp.exp(s - m)
    p = p / (jnp.sum(p, axis=-1, keepdims=True) + jnp.exp(sink - m))
    o = jnp.einsum('bhgnqk,bhnkd->bhgnqd', p.astype(v.dtype), vb)
    return o.reshape(B, Hq, S, dh)


def hybrid_mixer(xn, w_in, i_bias, f_bias, mh_norm_w, attn_sink, w_proj_m, w_proj_a, w_out):
    B, S, _ = xn.shape
    proj = xn @ w_in
    mq, mk, mv, mo, mig, mfg, aq, ak, av, gm, ga = jnp.split(proj, _split_points(), axis=-1)

    def heads(t, H):
        return t.reshape(B, S, H, -1).transpose(0, 2, 1, 3)

    f32 = jnp.float32
    q = heads(mq, M_HEADS).astype(f32)
    k = heads(mk, M_HEADS).astype(f32) * (M_HEAD_DIM ** -0.5)
    v = heads(mv, M_HEADS).astype(f32)
    ig = (mig.astype(f32).reshape(B, S, 2, M_HEADS) + i_bias.astype(f32)).transpose(2, 0, 3, 1)
    fg = (mfg.astype(f32).reshape(B, S, 2, M_HEADS) + f_bias.astype(f32)).transpose(2, 0, 3, 1)
    flip = lambda t: jnp.flip(t, axis=2)
    h_fwd = mlstm_chunkwise(q, k, v, ig[0], fg[0])
    h_bwd = flip(mlstm_chunkwise(flip(q), flip(k), flip(v), flip(ig[1]), flip(fg[1])))
    h = h_fwd + h_bwd
    h = h - jnp.mean(h, axis=-1, keepdims=True)
    h = h * lax.rsqrt(jnp.mean(h * h, axis=-1, keepdims=True) + EPS)
    h = h.transpose(0, 2, 1, 3).reshape(B, S, M_WIDTH) * mh_norm_w.astype(f32)
    h = (jax.nn.sigmoid(mo.astype(f32)) * h).astype(xn.dtype)
    y_m = h @ w_proj_m

    pos = jnp.arange(S)
    qa = rope_partial(heads(aq, A_HEADS), pos)
    ka = rope_partial(heads(ak, A_KV_HEADS), pos)
    va = heads(av, A_KV_HEADS)
    oa = window_attention(qa, ka, va, attn_sink)
    y_a = oa.transpose(0, 2, 1, 3).reshape(B, S, A_WIDTH) @ w_proj_a

    y = jax.nn.sigmoid(gm) * y_m + jax.nn.sigmoid(ga) * y_a
    return y @ w_out


def conv_ffn(xn, w_up, conv_w, conv_b, w_down):
    S = xn.shape[1]
    u = xn @ w_up
    pad = CONV_WIDTH // 2
    up = jnp.pad(u, ((0, 0), (pad, pad), (0, 0)))
    u = sum(up[:, j:j + S] * conv_w[j] for j in range(CONV_WIDTH)) + conv_b
    a, b = jnp.split(u, 2, axis=-1)
    return (jax.nn.silu(a) * b) @ w_down


def encoder(x, norm1_w, w_in, i_bias, f_bias, mh_norm_w, attn_sink, w_proj_m, w_proj_a, w_out,
            norm2_w, w_up, conv_w, conv_b, w_down, norm_f_w):
    for l in range(DEPTH):
        x = x + hybrid_mixer(rmsnorm(x, norm1_w[l]), w_in[l], i_bias[l], f_bias[l], mh_norm_w[l],
                             attn_sink[l], w_proj_m[l], w_proj_a[l], w_out[l])
        x = x + conv_ffn(rmsnorm(x, norm2_w[l]), w_up[l], conv_w[l], conv_b[l], w_down[l])
    return rmsnorm(x, norm_f_w)


def setup_inputs(seed: int = 0) -> dict:
    key = jax.random.key(seed)
    ks = jax.random.split(key, 18)
    f32 = jnp.float32
    nrm = lambda k, shape: jax.random.normal(k, shape, f32)
    lin = lambda k, shape, fan_in: nrm(k, shape) * (fan_in ** -0.5)
    return {
        "x_prompt": nrm(ks[0], (BATCH, SEQ, D_MODEL)),
        "x_sample": nrm(ks[1], (DEC_BATCH, DEC_SEQ, D_MODEL)),
        "norm1_w": 1.0 + 0.02 * nrm(ks[2], (DEPTH, D_MODEL)),
        "w_in": lin(ks[3], (DEPTH, D_MODEL, D_IN), D_MODEL),
        "i_bias": 0.1 * nrm(ks[4], (DEPTH, 2, M_HEADS)),
        "f_bias": jax.random.uniform(ks[5], (DEPTH, 2, M_HEADS), f32, 3.0, 6.0),
        "mh_norm_w": 1.0 + 0.02 * nrm(ks[6], (DEPTH, M_WIDTH)),
        "attn_sink": 0.5 * nrm(ks[7], (DEPTH, A_HEADS)),
        "w_proj_m": lin(ks[8], (DEPTH, M_WIDTH, D_MODEL), M_WIDTH),
        "w_proj_a": lin(ks[9], (DEPTH, A_WIDTH, D_MODEL), A_WIDTH),
        "w_out": lin(ks[10], (DEPTH, D_MODEL, D_MODEL), D_MODEL),
        "norm2_w": 1.0 + 0.02 * nrm(ks[11], (DEPTH, D_MODEL)),
        "w_up": lin(ks[12], (DEPTH, D_MODEL, 2 * D_FF), D_MODEL),
        "conv_w": lin(ks[13], (DEPTH, CONV_WIDTH, 2 * D_FF), CONV_WIDTH),
        "conv_b": 0.02 * nrm(ks[14], (DEPTH, 2 * D_FF)),
        "w_down": lin(ks[15], (DEPTH, D_FF, D_MODEL), D_FF),
        "norm_f_w": 1.0 + 0.02 * nrm(ks[16], (D_MODEL,)),
    }


def reference(x_prompt, x_sample, norm1_w, w_in, i_bias, f_bias, mh_norm_w, attn_sink, w_proj_m, w_proj_a,
              w_out, norm2_w, w_up, conv_w, conv_b, w_down, norm_f_w):
    y_prompt = encoder(x_prompt, norm1_w, w_in, i_bias, f_bias, mh_norm_w, attn_sink, w_proj_m, w_proj_a,
                       w_out, norm2_w, w_up, conv_w, conv_b, w_down, norm_f_w)
    y_sample = encoder(x_sample, norm1_w, w_in, i_bias, f_bias, mh_norm_w, attn_sink, w_proj_m, w_proj_a,
                       w_out, norm2_w, w_up, conv_w, conv_b, w_down, norm_f_w)
    return (y_prompt, y_sample)
```

```python
import contextlib
import numpy as np
import ml_dtypes
import concourse.bass as bass
import concourse.mybir as mybir
from concourse.bass_utils import run_bass_kernel_spmd

F32 = mybir.dt.float32
BF16 = mybir.dt.bfloat16
AF = mybir.ActivationFunctionType
ALU = mybir.AluOpType

NCORES = 8
D = 1024
KC = 8
DIN = 5136
DFF = 2816
NFC = 22
MQ, MK, MV, MO, GI, AQ, AK, AV, GM, GA = 0, 512, 1024, 1536, 2048, 2064, 2576, 2832, 3088, 4112
EPS = 1e-6
NEG = -30000.0
DBG_STOP = None
DBG_PHASE = None


class _Stop(Exception):
    pass
ROPE_THETA = 500000.0


class Buf:
    __slots__ = ("name", "writers", "readers", "psum")

    def __init__(self, name, psum=False):
        self.name = name
        self.writers = []
        self.readers = []
        self.psum = psum


class Prog:
    ENG = ("pe", "act", "dve", "pool", "sp")
    HANDLE = {"pe": "tensor", "act": "scalar", "dve": "vector", "pool": "gpsimd", "sp": "sync"}

    def __init__(self, nc, stack):
        self.nc = nc
        self.stack = stack
        self.streams = {e: [] for e in self.ENG}
        self.sems = {}
        self.count = {}
        self.known = {e: {} for e in self.ENG}
        for e in self.ENG:
            self._sem(e)

    def _sem(self, key):
        if key not in self.sems:
            self.sems[key] = self.stack.enter_context(self.nc.semaphore("s_" + key))
            self.count[key] = 0
        return self.sems[key]

    @staticmethod
    def _deps(reads, writes, pwrites, eng=None):
        deps = {}

        def add(kv):
            if deps.get(kv[0], 0) < kv[1]:
                deps[kv[0]] = kv[1]
        ps = [b for b in list(reads) + list(writes) + list(pwrites) if b.psum]
        for b in ps:
            for w in b.writers:
                if w[0] != eng:
                    add(w)
        reads = [b for b in reads if not b.psum]
        writes = [b for b in writes if not b.psum]
        pwrites = [b for b in pwrites if not b.psum]
        for b in reads:
            for w in b.writers:
                add(w)
        for b in writes:
            for w in b.writers:
                add(w)
            for r in b.readers:
                add(r)
        for b in pwrites:
            for r in b.readers:
                add(r)
            if b.readers:
                for w in b.writers:
                    add(w)
        return deps

    def _waits(self, eng, deps):
        kn = self.known[eng]
        for k, v in sorted(deps.items()):
            if k == eng and eng == "pe":
                continue
            if kn.get(k, 0) >= v:
                continue
            kn[k] = v
            self.streams[eng].append(("wait", k, v))

    @staticmethod
    def _update(me, reads, writes, pwrites):
        for b in list(reads) + list(writes) + list(pwrites):
            if b.psum:
                b.writers = [me]
        reads = [b for b in reads if not b.psum]
        writes = [b for b in writes if not b.psum]
        pwrites = [b for b in pwrites if not b.psum]
        for b in reads:
            b.readers.append(me)
        for b in writes:
            b.writers = [me]
            b.readers = []
        for b in pwrites:
            if b.readers:
                b.writers = [me]
                b.readers = []
            else:
                b.writers.append(me)

    def op(self, eng, fn, reads=(), writes=(), pwrites=()):
        self._waits(eng, self._deps(reads, writes, pwrites, eng))
        self.count[eng] += 1
        self.streams[eng].append(("op", fn, eng, 1))
        self._update((eng, self.count[eng]), reads, writes, pwrites)

    def dma(self, slot, fns, reads=(), writes=(), queue="sp"):
        self._sem(slot)
        self._waits(queue, self._deps(reads, writes, (), queue))
        for fn in fns:
            self.count[slot] += 16
            self.streams[queue].append(("op", fn, slot, 16))
        self._update((slot, self.count[slot]), reads, writes, ())

    def finish(self, queue="sp"):
        for k, v in self.count.items():
            if v > 0:
                self.streams[queue].append(("wait", k, v))

    def replay(self, block):
        for e in self.ENG:
            stream = self.streams[e]
            if not stream:
                continue

            def body(h, stream=stream):
                for item in stream:
                    if item[0] == "wait":
                        h.wait_ge(self.sems[item[1]], item[2])
                    else:
                        item[1](h).then_inc(self.sems[item[2]], item[3])
            getattr(block, self.HANDLE[e])(body)


def build_program(PC, SC, NS):
    NPB = PC + 4
    LCTX = 7 * PC - 1
    NTAB = NPB + SC
    nc = bass.Bass("TRN2", target_bir_lowering=False)

    def din(name, shape, dt=F32):
        return nc.dram_tensor(name, list(shape), dt, kind="ExternalInput").ap()

    x_own = din("x_own", [NPB * 128, D])
    x_ctx = din("x_ctx", [LCTX * 128, D])
    c_ctxf = din("c_ctxf", [LCTX, 128, 16])
    x_s = din("x_s", [NS * SC * 128, D])
    w_in = din("w_in", [D, DIN])
    w_pm = din("w_pm", [512, D])
    w_pa = din("w_pa", [512, D])
    w_out = din("w_out", [D, D])
    w_up = din("w_up", [D, 2 * DFF])
    w_dn = din("w_dn", [DFF, D])
    norm1 = din("norm1", [D])
    norm2 = din("norm2", [D])
    normf = din("normf", [D])
    mhw = din("mhw", [512])
    gbias = din("gbias", [16])
    sink = din("sink", [8])
    convw = din("convw", [3, 2 * DFF])
    convb = din("convb", [2 * DFF])
    c_ident = din("c_ident", [128, 128], BF16)
    c_tri = din("c_tri", [128, 512])
    c_mb = din("c_mb", [128, 512], BF16)
    c_rope = din("c_rope", [NTAB, 128, 16])
    c_flags = din("c_flags", [128, 4])
    y_own = nc.dram_tensor("y_own", [PC * 128, D], F32, kind="ExternalOutput").ap()
    y_s = nc.dram_tensor("y_s", [NS * SC * 128, D], F32, kind="ExternalOutput").ap()
    wup_s = nc.dram_tensor("wup_s", [NFC, 128, KC, 2, 128], BF16).ap()
    wdn_s = nc.dram_tensor("wdn_s", [NFC, 128, D], BF16).ap()
    NCD = max(NPB, SC)
    cdb_s = nc.dram_tensor("cdb_s", [NCD, 128, 516], BF16).ap()

    with contextlib.ExitStack() as st:
        P = Prog(nc, st)
        bufs = {}

        def B(name):
            if name not in bufs:
                bufs[name] = Buf(name)
            return bufs[name]

        def sb(name, shape, dt=F32):
            return st.enter_context(nc.sbuf_tensor(name, list(shape), dt))

        win = sb("win", [128, KC, DIN], BF16)
        wpm = sb("wpm", [128, 4, D], BF16)
        wpa = sb("wpa", [128, 4, D], BF16)
        wout = sb("wout", [128, KC, D], BF16)
        ident = sb("ident", [128, 128], BF16)
        tri = sb("tri", [128, 512])
        mbias = sb("mbias", [128, 512], BF16)
        flags = sb("flags", [128, 4])
        gb = sb("gb", [128, 16])
        esink = sb("esink", [128, 8])
        n1c = sb("n1c", [128, KC])
        n2c = sb("n2c", [128, KC])
        mhc = sb("mhc", [128, 4])
        cw = sb("cw", [128, 3, 2 * NFC])
        cb = sb("cb", [128, 2 * NFC])
        nfw = sb("nfw", [128, D])
        ropes = [sb(f"rope{i}", [128, 16]) for i in range(3)]
        flt = [sb(f"flt{i}", [128, 16]) for i in range(2)]
        xs = [sb(f"x{i}", [128, D]) for i in range(1)]
        xn = sb("xn", [128, D], BF16)
        xnT = [sb(f"xnT{i}", [128, KC, 128], BF16) for i in range(2)]
        sm = sb("sm", [128, 64])
        kb = sb("kb", [128, 512], BF16)
        kw = sb("kw", [128, 512], BF16)
        vp = sb("vp", [128, 4, 129], BF16)
        sgo = sb("sgo", [128, 512])
        rowst = sgo
        G = sb("G", [128, 16])
        E8 = sb("E8", [128, 8])
        LF = sb("LF", [128, 8])
        NB = sb("NB", [128, 16])
        RW = sb("RW", [128, 16])
        RW2 = sb("RW2", [128, 16])
        Dd = sb("Dd", [128, 8])
        Dd2 = sb("Dd2", [128, 8])
        qT = sb("qT", [128, 4, 128], BF16)
        kT = sb("kT", [128, 4, 128], BF16)
        aqr = sb("aqr", [128, 512], BF16)
        akd = sb("akd", [128, 4, 64], BF16)
        rt = sb("rt", [128, 4, 8, 8])
        aqT = sb("aqT", [128, 8, 128], BF16)
        akT = [sb(f"akT{i}", [128, 4, 128], BF16) for i in range(3)]
        avp = [sb(f"avp{i}", [128, 4, 65], BF16) for i in range(3)]
        PT = sb("PT", [128, 8, 128], BF16)
        Cst = sb("Cst", [128, 4, 129])
        Cd = sb("Cd", [128, 4, 129], BF16)
        Cdb = [sb(f"Cdb{i}", [128, 4, 129], BF16) for i in range(1)]
        hh = sb("hh", [128, 512])
        bst = sb("bst", [128, 4, 6])
        mv = sb("mv", [128, 4, 2])
        hn = sb("hn", [128, 512], BF16)
        oa = kw
        qb = hn
        hT = sb("hT", [128, 4, 128], BF16)
        PTa = [sb(f"PTa{i}", [128, 256], BF16) for i in range(4)]
        oaT = sb("oaT", [128, 4, 128], BF16)
        sgm = sb("sgm", [128, 512], BF16)
        sga = sb("sga", [128, 512], BF16)
        NH = 3
        hres = [sb(f"hres{i}", [128, D]) for i in range(NH)]
        xn2 = sb("xn2", [128, D], BF16)
        ybf = xn2
        yT = xn[:].rearrange("p (k t) -> p k t", t=128)
        NU = 2
        unit = [sb(f"unit{i}", [128, KC, 258], BF16) for i in range(NU)]
        NR = 3
        wupr = [sb(f"wupr{i}", [128, KC, 2, 128], BF16) for i in range(NR)]
        wdnr = [sb(f"wdnr{i}", [128, D], BF16) for i in range(NR)]
        cab = [sb(f"cab{i}", [128, 2, 256]) for i in range(2)]
        gt = [sb(f"gt{i}", [128, 256], BF16) for i in range(2)]
        stg = [hres[0], hres[1]]
        stgb = [xn2, xn]
        pb = [st.enter_context(nc.psum_tensor(f"pb{i}", [128, 512], F32)) for i in range(8)]
        pbb = [p[:].bitcast(BF16) for p in pb]
        BP = [Buf(f"pb{i}", psum=True) for i in range(8)]

        U = tri[:, 0:128]
        UT = tri[:, 128:256]
        ONES = tri[:, 256:384]
        IDF = tri[:, 384:512]

        def load(slot, out, in_, wbuf, reads=()):
            P.dma(slot, [lambda e: e.dma_start(out=out, in_=in_)], reads=list(reads), writes=[wbuf])

        def act(out, in_, func, reads, writes=(), pwrites=(), **kw_):
            P.op("act", lambda e: e.activation(out=out, in_=in_, func=func, **kw_), reads=reads,
                 writes=writes, pwrites=pwrites)

        def ts(eng, out, in0, s1, s2, op0, op1=None, reads=(), writes=(), pwrites=()):
            if op1 is None:
                P.op(eng, lambda e: e.tensor_scalar(out=out, in0=in0, scalar1=s1, scalar2=None, op0=op0),
                     reads=reads, writes=writes, pwrites=pwrites)
            else:
                P.op(eng, lambda e: e.tensor_scalar(out=out, in0=in0, scalar1=s1, scalar2=s2, op0=op0, op1=op1),
                     reads=reads, writes=writes, pwrites=pwrites)

        def tt(eng, out, in0, in1, op, reads=(), writes=(), pwrites=()):
            P.op(eng, lambda e: e.tensor_tensor(out=out, in0=in0, in1=in1, op=op), reads=reads,
                 writes=writes, pwrites=pwrites)

        def stt(eng, out, in0, scalar, in1, op0, op1, reads=(), writes=(), pwrites=()):
            P.op(eng, lambda e: e.scalar_tensor_tensor(out=out, in0=in0, scalar=scalar, in1=in1, op0=op0, op1=op1),
                 reads=reads, writes=writes, pwrites=pwrites)

        def cp(eng, out, in_, reads=(), writes=(), pwrites=()):
            if eng == "act":
                act(out, in_, AF.Copy, reads, writes, pwrites)
            else:
                P.op(eng, lambda e: e.tensor_copy(out=out, in_=in_), reads=reads, writes=writes, pwrites=pwrites)

        def mm_group(mms, reads, writes=(), pwrites=()):
            def fn(e):
                ins = None
                for m in mms:
                    o, l, r, s0, s1 = m[:5]
                    if len(m) > 5:
                        ins = e.matmul(o, lhsT=l, rhs=r, start=s0, stop=s1, skip_group_check=True)
                    else:
                        ins = e.matmul(o, lhsT=l, rhs=r, start=s0, stop=s1)
                return ins
            P.op("pe", fn, reads=reads, writes=writes, pwrites=pwrites)

        def transposes(outs_ins, reads, writes=(), pwrites=()):
            def fn(e):
                ins = None
                for (o, i) in outs_ins:
                    ins = e.transpose(out=o, in_=i, identity=ident[:])
                return ins
            P.op("pe", fn, reads=list(reads) + [B("ident")], writes=writes, pwrites=pwrites)

        try:
            def ckpt(n):
                if DBG_PHASE == n:
                    raise _Stop()

            load("c0", ident[:], c_ident, B("ident"))
            load("c1", tri[:], c_tri, B("tri"))
            load("c2", mbias[:], c_mb, B("mbias"))
            load("c3", flags[:], c_flags, B("flags"))
            load("c4", gb[:], gbias.partition_broadcast(128), B("gb"))
            load("c5", esink[:], sink.partition_broadcast(128), B("esink"))
            load("c6", nfw[:], normf.partition_broadcast(128), B("nfw"))

            def load_T(dst, src_rows, n, wbuf, first=True):
                P.dma("c7", [lambda e: e.dma_start(out=rowst[0:n, 0:128], in_=src_rows)], writes=[B("sgo")])
                mm_group([(pb[0][:, 0:n], rowst[0:n, 0:128], IDF[0:n, 0:n], True, True)], [B("sgo"), B("tri")], writes=[BP[0]])
                cp("dve", dst, pb[0][:, 0:n], reads=[BP[0]], writes=[wbuf] if first else (), pwrites=() if first else [wbuf])
            load_T(n1c[:], norm1.rearrange("(k p) -> k p", p=128), KC, B("n1c"))
            load_T(n2c[:], norm2.rearrange("(k p) -> k p", p=128), KC, B("n2c"))
            load_T(mhc[:], mhw.rearrange("(k p) -> k p", p=128), 4, B("mhc"))
            for j in range(3):
                load_T(cw[:, j, :], convw[j].rearrange("(c p) -> c p", p=128), 2 * NFC, B("cw"), first=(j == 0))
            load_T(cb[:], convb.rearrange("(c p) -> c p", p=128), 2 * NFC, B("cb"))
            act(esink[:], esink[:], AF.Exp, [B("esink")], [B("esink")])
            P.op("pool", lambda e: e.memset(vp[:], 1.0), writes=[B("vp")])
            for i in range(3):
                P.op("pool", lambda e, i=i: e.memset(avp[i][:], 1.0), writes=[B(f"avp{i}")])

            PIECE = 1408
            wcount = [0]

            bcount = [0]

            def prep2(src_rows, ncols, normcol, consume, scale_ranges=(), direct=None, bdirect=None, bg=False):
                c0 = 0
                while c0 < ncols:
                    c1 = min(ncols, c0 + D)
                    i = 1 if bg else wcount[0] % 3
                    wcount[0] += 1
                    sf, bsf = hres[i], B(f"hres{i}")
                    w = c1 - c0
                    load(f"wl{i}", sf[:, 0:w], src_rows[:, c0:c1], bsf)
                    if direct is None:
                        if bg:
                            ib, sbf, bsb = 0, PT[:].rearrange("p k t -> p (k t)"), B("PT")
                        else:
                            ib = bcount[0] % 2
                            bcount[0] += 1
                            sbf, bsb = stgb[ib], B("xn2" if ib == 0 else "xn")
                    cuts = sorted(set([c0, c1] + [c for (a, b, s) in scale_ranges for c in (a, b) if c0 < c < c1]))
                    eng = "pool" if bg else ("dve" if wcount[0] % 2 == 0 else "pool")
                    first = True
                    for a, b in zip(cuts[:-1], cuts[1:]):
                        s = 1.0
                        for (ra, rb, rs) in scale_ranges:
                            if ra <= a and b <= rb:
                                s = rs
                        sc1 = normcol if normcol is not None else 1.0
                        rd = [bsf] + ([] if normcol is None else [B("ncols")])
                        if direct is not None:
                            ts(eng, direct(a, b), sf[:, a - c0:b - c0], sc1, float(s), ALU.mult, ALU.mult, reads=rd,
                               pwrites=[bdirect])
                        else:
                            ts(eng, sbf[:, a - c0:b - c0], sf[:, a - c0:b - c0], sc1, float(s), ALU.mult, ALU.mult, reads=rd,
                               writes=[bsb] if first else (), pwrites=() if first else [bsb])
                        first = False
                    if direct is None:
                        consume(c0, c1, sbf, bsb, ib)
                    c0 = c1
                    if bg:
                        yield

            bufs["ncols"] = Buf("ncols")
            for nm in ("n1c", "n2c", "mhc"):
                for w_ in B(nm).writers:
                    bufs["ncols"].writers.append(w_)

            sr_in = [(MK, MK + 512, 128.0 ** -0.5), (AQ, AQ + 512, 0.125)]
            for kc in range(KC):
                for _ in prep2(w_in[kc * 128:(kc + 1) * 128, :], DIN, n1c[:, kc:kc + 1], None, sr_in,
                      direct=lambda a, b, kc=kc: win[:, kc, a:b], bdirect=B("win")):
                    pass
            for kc in range(4):
                for _ in prep2(w_pm[kc * 128:(kc + 1) * 128, :], D, mhc[:, kc:kc + 1], None,
                      direct=lambda a, b, kc=kc: wpm[:, kc, a:b], bdirect=B("wpm")):
                    pass
            for kc in range(4):
                for _ in prep2(w_pa[kc * 128:(kc + 1) * 128, :], D, None, None,
                      direct=lambda a, b, kc=kc: wpa[:, kc, a:b], bdirect=B("wpa")):
                    pass
            for kc in range(KC):
                for _ in prep2(w_out[kc * 128:(kc + 1) * 128, :], D, None, None,
                      direct=lambda a, b, kc=kc: wout[:, kc, a:b], bdirect=B("wout")):
                    pass
            def ffn_weight_prep():
                wup_v = wup_s.rearrange("j p k a c -> p j k a c")
                for kc in range(KC):
                    def consume(c0, c1, sbf, bsb, i, kc=kc):
                        fns = []
                        c = c0
                        while c < c1:
                            ab = 0 if c < DFF else 1
                            lim = DFF if ab == 0 else 2 * DFF
                            ce = min(c1, lim)
                            j0 = (c - ab * DFF) // 128
                            nj = (ce - c) // 128
                            src = sbf[:, c - c0:ce - c0].rearrange("p (j c) -> p j c", c=128)
                            dst = wup_v[:, j0:j0 + nj, kc, ab, :]
                            fns.append(lambda e, src=src, dst=dst: e.dma_start(out=dst, in_=src))
                            c = ce
                        P.dma(f"ws{i}", fns, reads=[bsb], writes=[B("wup_s")] if False else [], )
                        B("wup_s").writers.append((f"ws{i}", P.count[f"ws{i}"]))
                        bsb.readers.append((f"ws{i}", P.count[f"ws{i}"]))
                    yield from prep2(w_up[kc * 128:(kc + 1) * 128, :], 2 * DFF, n2c[:, kc:kc + 1], consume, bg=True)
                for j in range(NFC):
                    def consume(c0, c1, sbf, bsb, i, j=j):
                        P.dma(f"ws{i}", [lambda e: e.dma_start(out=wdn_s[j], in_=sbf[:, 0:D])], reads=[bsb])
                        B("wdn_s").writers.append((f"ws{i}", P.count[f"ws{i}"]))
                    yield from prep2(w_dn[j * 128:(j + 1) * 128, :], D, None, consume, bg=True)


            state = {"bg": ffn_weight_prep()}
            ckpt(1)
            state.update({"xi": 0, "ri": 0})

            def rmsnorm_T(xt, bx, dstT, bdst, psum_i=3, xn_=None, bxn=None):
                if xn_ is None:
                    xn_, bxn = xn, B("xn")
                act(xn_[:], xt, AF.Square, [bx], [bxn, B("sm0")], accum_out=sm[:, 0:1])
                act(sm[:, 1:2], sm[:, 0:1], AF.Ln, [B("sm0")], [B("sm1")], scale=1.0 / D, bias=EPS)
                act(sm[:, 2:3], sm[:, 1:2], AF.Exp, [B("sm1")], [B("sm2")], scale=-0.5)
                act(xn_[:], xt, AF.Copy, [bx, B("sm2")], [bxn], scale=sm[:, 2:3])
                transposes([(pbb[psum_i][:, k * 128:(k + 1) * 128], xn_[:, k * 128:(k + 1) * 128]) for k in range(KC)],
                           [bxn], writes=[BP[psum_i]])
                cp("dve", dstT.rearrange("p k t -> p (k t)"), pbb[psum_i][:, 0:D], reads=[BP[psum_i]], writes=[bdst])

            def proj(xT, bxT, c0, n, bank, lo=0):
                mm_group([(pb[bank][:, lo:lo + n], xT[:, k, :], win[:, k, c0:c0 + n], k == 0, k == KC - 1) for k in range(KC)],
                         [bxT, B("win")], writes=[BP[bank]])

            SL = lambda: None
            def slot(i):
                if i == 0:
                    return dict(RW=RW, bRW=B("RW"), Dd=Dd, bDd=B("Dd"), kb=kb, bkb=B("kb"), vp=vp, bvp=B("vp"))
                return dict(RW=RW2, bRW=B("RW2"), Dd=Dd2, bDd=B("Dd2"), kb=aqr, bkb=B("aqr"), vp=Cdb[0], bvp=B("Cdb0"))

            def gates(si=0, cb_=3):
                S_ = slot(si)
                RW_, bRW, Dd_, bDd = S_["RW"], S_["bRW"], S_["Dd"], S_["bDd"]
                act(E8[:], G[:, 8:16], AF.Exp, [B("G")], [B("E8")], scale=-1.0)
                act(LF[:], E8[:], AF.Ln, [B("E8")], [B("LF")], bias=1.0)
                mm_group([(pb[cb_][:, 0:4], U, LF[:, 0:4], True, True),
                          (pb[cb_][:, 4:8], UT, LF[:, 4:8], True, True),
                          (pb[cb_][:, 8:16], ONES, LF[:, 0:8], True, True)], [B("tri"), B("LF")], writes=[BP[cb_]])
                cp("dve", NB[:], pb[cb_][:, 0:16], reads=[BP[cb_]], writes=[B("NB")])
                tt("dve", RW_[:, 0:8], NB[:, 0:8], NB[:, 8:16], ALU.subtract, reads=[B("NB")], writes=[bRW])
                tt("dve", RW_[:, 8:16], RW_[:, 0:8], G[:, 0:8], ALU.add, reads=[bRW, B("G")], writes=[bRW])
                act(RW_[:], RW_[:], AF.Exp, [bRW], [bRW])
                act(Dd_[:], NB[:, 8:16], AF.Exp, [B("NB")], [bDd], scale=-1.0)

            def state_update(d, si=0):
                S_ = slot(si)
                for h in range(4):
                    act(Cd[:, h, :], Cst[:, h, :], AF.Copy, [B("Cst"), S_["bDd"]], writes=[B("Cd")] if h == 0 else (),
                        pwrites=() if h == 0 else [B("Cd")], scale=S_["Dd"][:, d * 4 + h:d * 4 + h + 1])

            def cloc_and_update(d, si=0):
                S_ = slot(si)
                RW_, Dd_, kb_, vp_ = S_["RW"], S_["Dd"], S_["kb"], S_["vp"]
                for h in range(4):
                    ts("dve", kw[:, h * 128:(h + 1) * 128], kb_[:, h * 128:(h + 1) * 128], RW_[:, 8 + d * 4 + h:9 + d * 4 + h],
                       None, ALU.mult, reads=[S_["bkb"], S_["bRW"]], writes=[B("kw")] if h == 0 else (),
                       pwrites=() if h == 0 else [B("kw")])
                regs = [(7, 0), (7, 129), (7, 258), (6, 258)]
                mm_group([(pb[regs[h][0]][:, regs[h][1]:regs[h][1] + 129], kw[:, h * 128:(h + 1) * 128], vp_[:, h, :], True, True)
                          for h in range(3)], [B("kw"), S_["bvp"]], writes=[BP[7]])
                mm_group([(pb[6][:, 258:387], kw[:, 384:512], vp_[:, 3, :], True, True)], [B("kw"), S_["bvp"]], pwrites=[BP[6]])
                for h in range(4):
                    bk, c0 = regs[h]
                    stt("dve", Cst[:, h, :], Cst[:, h, :], Dd_[:, d * 4 + h:d * 4 + h + 1], pb[bk][:, c0:c0 + 129],
                        ALU.mult, ALU.add, reads=[B("Cst"), S_["bDd"], BP[bk]], writes=[B("Cst")] if h == 0 else (),
                        pwrites=() if h == 0 else [B("Cst")])

            def load_x(src, row0):
                i = 0
                state["xi"] += 1
                load(f"xl{i}", xs[i][:], src[row0:row0 + 128, :], B(f"x{i}"))
                return i

            def zero_state():
                P.op("pool", lambda e: e.memset(Cst[:], 0.0), writes=[B("Cst")])

            XS = [(xs[0], "x0"), (hres[0], "hres0")]

            XN = [(xn, "xn"), (xn2, "xn2")]

            def scan(src, chunk_rows, d, store_slots=None, ctx_flags=None):
                P.op("dve", lambda e: e.memset(Cdb[0][:, :, 128:129], 1.0), writes=[B("Cdb0")])
                n = len(chunk_rows)

                def s0_act(i):
                    si = i % 2
                    xt, bxn = XS[si]
                    bx = B(bxn)
                    xn_, bn_ = XN[si][0], B(XN[si][1])
                    load(f"xl{si}", xt[:], src[chunk_rows[i]:chunk_rows[i] + 128, :], bx)
                    if ctx_flags is not None:
                        load(f"fl{si}", flt[si][:], ctx_flags[i], B(f"flt{si}"))
                    act(xn_[:], xt[:], AF.Square, [bx], [bn_, B("sm0")], accum_out=sm[:, 0:1])
                    act(sm[:, 1:2], sm[:, 0:1], AF.Ln, [B("sm0")], [B("sm1")], scale=1.0 / D, bias=EPS)
                    act(sm[:, 2:3], sm[:, 1:2], AF.Exp, [B("sm1")], [B("sm2")], scale=-0.5)
                    act(xn_[:], xt[:], AF.Copy, [bx, B("sm2")], [bn_], scale=sm[:, 2:3])

                def s0_pe(i):
                    si = i % 2
                    xn_, bn_ = XN[si][0], B(XN[si][1])
                    transposes([(pbb[3][:, k * 128:(k + 1) * 128], xn_[:, k * 128:(k + 1) * 128]) for k in range(KC)],
                               [bn_], writes=[BP[3]])
                    cp("dve", xnT[si][:].rearrange("p k t -> p (k t)"), pbb[3][:, 0:D], reads=[BP[3]], writes=[B(f"xnT{si}")])

                def s1_pe(i):
                    si = i % 2
                    bxT = B(f"xnT{si}")
                    proj(xnT[si], bxT, GI, 16, 2)
                    proj(xnT[si], bxT, MK, 512, 0)
                    proj(xnT[si], bxT, MV, 512, 1)

                def s1_ev(i):
                    S_ = slot(i % 2)
                    tt("dve", G[:], pb[2][:, 0:16], gb[:], ALU.add, reads=[BP[2], B("gb")], writes=[B("G")])
                    cp("act", S_["kb"][:], pb[0][:, :], reads=[BP[0]], writes=[S_["bkb"]])
                    cp("act", S_["vp"][:, :, 0:128], pb[1][:, :].rearrange("p (h c) -> p h c", h=4), reads=[BP[1]],
                       pwrites=[S_["bvp"]])

                def s1_g1(i):
                    act(E8[:], G[:, 8:16], AF.Exp, [B("G")], [B("E8")], scale=-1.0)
                    act(LF[:], E8[:], AF.Ln, [B("E8")], [B("LF")], bias=1.0)
                    mm_group([(pb[5][:, 0:4], U, LF[:, 0:4], True, True),
                              (pb[5][:, 4:8], UT, LF[:, 4:8], True, True),
                              (pb[5][:, 8:16], ONES, LF[:, 0:8], True, True)], [B("tri"), B("LF")], writes=[BP[5]])

                def s1_g2(i):
                    S_ = slot(i % 2)
                    RW_, bRW, Dd_, bDd = S_["RW"], S_["bRW"], S_["Dd"], S_["bDd"]
                    cp("dve", NB[:], pb[5][:, 0:16], reads=[BP[5]], writes=[B("NB")])
                    tt("dve", RW_[:, 0:8], NB[:, 0:8], NB[:, 8:16], ALU.subtract, reads=[B("NB")], writes=[bRW])
                    tt("dve", RW_[:, 8:16], RW_[:, 0:8], G[:, 0:8], ALU.add, reads=[bRW, B("G")], writes=[bRW])
                    act(RW_[:], RW_[:], AF.Exp, [bRW], [bRW])
                    act(Dd_[:], NB[:, 8:16], AF.Exp, [B("NB")], [bDd], scale=-1.0)
                    if ctx_flags is not None:
                        fi = i % 2
                        bfl = B(f"flt{fi}")
                        tt("dve", RW_[:, 8:16], RW_[:, 8:16], flt[fi][:, 0:8], ALU.mult, reads=[bRW, bfl], writes=[bRW])
                        stt("dve", Dd_[:], Dd_[:], -1.0, flt[fi][:, 8:16], ALU.add, ALU.mult, reads=[bDd, bfl], writes=[bDd])
                        ts("dve", Dd_[:], Dd_[:], 1.0, None, ALU.add, reads=[bDd], writes=[bDd])

                REGS = [(7, 0), (7, 129), (7, 258), (6, 258)]
                REGS_B = [(4, 0), (4, 129), (4, 258), (6, 0)]
                fstate = hres[2][:, 0:516].rearrange("p (h c) -> p h c", c=129)

                def s2_a(i):
                    si = i % 2
                    S_ = slot(si)
                    if ctx_flags is not None:
                        for dd, kwt, bkw, regs in ((0, kw, B("kw"), REGS), (1, hn, B("hn"), REGS_B)):
                            for h in range(4):
                                ts("pool", kwt[:, h * 128:(h + 1) * 128], S_["kb"][:, h * 128:(h + 1) * 128],
                                   S_["RW"][:, 8 + dd * 4 + h:9 + dd * 4 + h], 1.0, ALU.mult, ALU.mult,
                                   reads=[S_["bkb"], S_["bRW"]], writes=[bkw] if h == 0 else (), pwrites=() if h == 0 else [bkw])
                            mm_group([(pb[regs[h][0]][:, regs[h][1]:regs[h][1] + 129], kwt[:, h * 128:(h + 1) * 128],
                                       S_["vp"][:, h, :], True, True) for h in range(3)], [bkw, S_["bvp"]], writes=[BP[regs[0][0]]])
                            mm_group([(pb[6][:, regs[3][1]:regs[3][1] + 129], kwt[:, 384:512], S_["vp"][:, 3, :], True, True)],
                                     [bkw, S_["bvp"]], writes=[BP[6]])
                        return
                    if store_slots is not None:
                        state_update(d, si)
                        ss = store_slots[i]
                        P.dma("cds", [lambda e: e.dma_start(out=cdb_s[ss], in_=Cd[:].rearrange("p h c -> p (h c)"))],
                              reads=[B("Cd")], writes=[B(f"cdb_s{ss}")])
                    for h in range(4):
                        ts("dve", kw[:, h * 128:(h + 1) * 128], S_["kb"][:, h * 128:(h + 1) * 128],
                           S_["RW"][:, 8 + d * 4 + h:9 + d * 4 + h], None, ALU.mult, reads=[S_["bkb"], S_["bRW"]],
                           writes=[B("kw")] if h == 0 else (), pwrites=() if h == 0 else [B("kw")])
                    mm_group([(pb[REGS[h][0]][:, REGS[h][1]:REGS[h][1] + 129], kw[:, h * 128:(h + 1) * 128], S_["vp"][:, h, :],
                               True, True) for h in range(3)], [B("kw"), S_["bvp"]], writes=[BP[7]])
                    mm_group([(pb[6][:, 258:387], kw[:, 384:512], S_["vp"][:, 3, :], True, True)], [B("kw"), S_["bvp"]],
                             writes=[BP[6]])

                def s2_b(i):
                    S_ = slot(i % 2)
                    if ctx_flags is not None:
                        for dd, stt_, bst_, regs in ((0, fstate, B("hres2"), REGS), (1, Cst, B("Cst"), REGS_B)):
                            for h in range(4):
                                bk, c0 = regs[h]
                                stt("dve", stt_[:, h, :], stt_[:, h, :], S_["Dd"][:, dd * 4 + h:dd * 4 + h + 1],
                                    pb[bk][:, c0:c0 + 129], ALU.mult, ALU.add, reads=[bst_, S_["bDd"], BP[bk]],
                                    writes=[bst_] if h == 0 else (), pwrites=() if h == 0 else [bst_])
                        return
                    for h in range(4):
                        bk, c0 = REGS[h]
                        stt("dve", Cst[:, h, :], Cst[:, h, :], S_["Dd"][:, d * 4 + h:d * 4 + h + 1], pb[bk][:, c0:c0 + 129],
                            ALU.mult, ALU.add, reads=[B("Cst"), S_["bDd"], BP[bk]], writes=[B("Cst")] if h == 0 else (),
                            pwrites=() if h == 0 else [B("Cst")])

                s0_act(0)
                s0_pe(0)
                if n > 1:
                    s0_act(1)
                    s0_pe(1)
                s1_pe(0)
                s1_ev(0)
                s1_g1(0)
                s1_g2(0)
                for idx in range(n):
                    if idx + 1 < n:
                        s1_pe(idx + 1)
                    s2_a(idx)
                    if idx + 2 < n:
                        s0_act(idx + 2)
                    if idx + 1 < n:
                        s1_ev(idx + 1)
                        s1_g1(idx + 1)
                    s2_b(idx)
                    if idx + 1 < n:
                        s1_g2(idx + 1)
                    if idx + 2 < n:
                        s0_pe(idx + 2)
                    if state.get("bg") is not None:
                        if next(state["bg"], "done") == "done":
                            state["bg"] = None

            def rope_apply(src_psum, bsrc, nh, dst_view, bdst, rope_t, brope, first_write):
                sv = src_psum.rearrange("p (h c) -> p h c", c=64)
                cos = rope_t[:, 0:8].unsqueeze(1).to_broadcast([128, nh, 8])
                sin = rope_t[:, 8:16].unsqueeze(1).to_broadcast([128, nh, 8])
                x1, x2 = sv[:, :, 0:8], sv[:, :, 8:16]
                ta, tb, tc, td = (rt[:, i, 0:nh, :] for i in range(4))
                tt("dve", ta, x1, cos, ALU.mult, reads=[bsrc, brope], writes=[B("rt")])
                tt("dve", tb, x2, sin, ALU.mult, reads=[bsrc, brope], pwrites=[B("rt")])
                tt("dve", tc, x2, cos, ALU.mult, reads=[bsrc, brope], pwrites=[B("rt")])
                tt("dve", td, x1, sin, ALU.mult, reads=[bsrc, brope], pwrites=[B("rt")])
                w0 = [bdst] if first_write else ()
                p0 = () if first_write else [bdst]
                tt("dve", dst_view(0, 8), ta, tb, ALU.subtract, reads=[B("rt")], writes=w0, pwrites=p0)
                tt("dve", dst_view(8, 16), tc, td, ALU.add, reads=[B("rt")], pwrites=[bdst])
                cp("act", dst_view(16, 64), sv[:, :, 16:64], reads=[bsrc], pwrites=[bdst])

            def s1_kv(job, c):
                xi = load_x(job["x"], c * 128)
                r = c % 3
                load(f"rl{r}", ropes[r][:], c_rope[job["tab0"] + c], B(f"rope{r}"))
                j = c % 2
                xt, bx = xs[xi][:], B(f"x{xi}")
                act(xn[:], xt, AF.Square, [bx], [B("xn"), B("sm0")], accum_out=sm[:, 0:1])
                act(sm[:, 1:2], sm[:, 0:1], AF.Ln, [B("sm0")], [B("sm1")], scale=1.0 / D, bias=EPS)
                act(sm[:, 2:3], sm[:, 1:2], AF.Exp, [B("sm1")], [B("sm2")], scale=-0.5)
                ts("pool", xn[:], xt, sm[:, 2:3], 1.0, ALU.mult, ALU.mult, reads=[bx, B("sm2")], writes=[B("xn")])
                yield
                transposes([(pbb[3][:, k * 128:(k + 1) * 128], xn[:, k * 128:(k + 1) * 128]) for k in range(KC)],
                           [B("xn")], writes=[BP[3]])
                cp("dve", xnT[j][:].rearrange("p k t -> p (k t)"), pbb[3][:, 0:D], reads=[BP[3]], writes=[B(f"xnT{j}")])
                yield
                proj(xnT[j], B(f"xnT{j}"), AK, 512, 1)
                s_ = c % 3
                rope_apply(pb[1][:, 0:256], BP[1], 4, lambda a, b_: akd[:, :, a:b_], B("akd"), ropes[r], B(f"rope{r}"), True)
                cp("act", avp[s_][:, :, 0:64], pb[1][:, 256:512].rearrange("p (h c) -> p h c", h=4), reads=[BP[1]],
                   pwrites=[B(f"avp{s_}")])
                yield
                transposes([(pbb[3][0:64, k * 128:(k + 1) * 128], akd[:, k, :]) for k in range(4)],
                           [B("akd")], writes=[BP[3]])
                cp("dve", akT[s_][0:64, :, :].rearrange("p k t -> p (k t)"), pbb[3][0:64, 0:512], reads=[BP[3]], writes=[B(f"akT{s_}")])
                yield

            def s1_rest(job, c):
                j = c % 2
                r = c % 3
                bxT = B(f"xnT{j}")
                proj(xnT[j], bxT, GI, 16, 2)
                proj(xnT[j], bxT, MQ, 512, 0)
                proj(xnT[j], bxT, MK, 512, 1)
                tt("dve", G[:], pb[2][:, 0:16], gb[:], ALU.add, reads=[BP[2], B("gb")], writes=[B("G")])
                cp("dve", qb[:], pb[0][:, :], reads=[BP[0]], writes=[B("hn")])
                cp("dve", kb[:], pb[1][:, :], reads=[BP[1]], writes=[B("kb")])
                yield
                proj(xnT[j], bxT, MV, 512, 2)
                proj(xnT[j], bxT, MO, 512, 0)
                proj(xnT[j], bxT, AQ, 512, 1)
                cp("act", vp[:, :, 0:128], pb[2][:, :].rearrange("p (h c) -> p h c", h=4), reads=[BP[2]], pwrites=[B("vp")])
                act(sgo[:], pb[0][:, :], AF.Sigmoid, [BP[0]], [B("sgo")])
                yield
                rope_apply(pb[1][:, :], BP[1], 8, lambda a, b_: aqr[:].rearrange("p (h c) -> p h c", c=64)[:, :, a:b_],
                           B("aqr"), ropes[r], B(f"rope{r}"), True)
                yield
                transposes([(pbb[2][:, k * 128:(k + 1) * 128], qb[:, k * 128:(k + 1) * 128]) for k in range(4)] +
                           [(pbb[2][:, 512 + k * 128:512 + (k + 1) * 128], kb[:, k * 128:(k + 1) * 128]) for k in range(4)],
                           [B("hn"), B("kb")], writes=[BP[2]])
                gates(0)
                cp("dve", qT[:].rearrange("p k t -> p (k t)"), pbb[2][:, 0:512], reads=[BP[2]], writes=[B("qT")])
                cp("dve", kT[:].rearrange("p k t -> p (k t)"), pbb[2][:, 512:1024], reads=[BP[2]], writes=[B("kT")])
                yield
                transposes([(pbb[0][0:64, k * 128:(k + 1) * 128], aqr[:, k * 64:(k + 1) * 64]) for k in range(8)],
                           [B("aqr")], writes=[BP[0]])
                cp("dve", aqT[0:64, :, :].rearrange("p k t -> p (k t)"), pbb[0][0:64, 0:1024], reads=[BP[0]], writes=[B("aqT")])
                yield

            def interleave(*gens):
                gens = [g for g in gens if g is not None]
                while gens:
                    for g in list(gens):
                        try:
                            next(g)
                        except StopIteration:
                            gens.remove(g)

            ACC = [(4 + (i // 3), (i % 3) * 129) for i in range(8)]

            def s2_mlstm(job, c, first, cdb_slot):
                mm_group([(pb[3][:, h * 128:(h + 1) * 128], kT[:, h, :], qT[:, h, :], True, True) for h in range(4)],
                         [B("kT"), B("qT")], writes=[BP[3]])
                for h in range(4):
                    for d in range(2):
                        i = h * 2 + d
                        stt("dve", PT[:, i, :], pb[3][:, h * 128:(h + 1) * 128], RW[:, 8 + d * 4 + h:9 + d * 4 + h],
                            U if d == 0 else UT, ALU.mult, ALU.mult, reads=[BP[3], B("RW"), B("tri")],
                            writes=[B("PT")] if i == 0 else (), pwrites=() if i == 0 else [B("PT")])
                if first:
                    P.op("pool", lambda e: e.memset(Cd[:], 0.0), writes=[B("Cd")])
                else:
                    state_update(0)
                bcd = B(f"Cdb{cdb_slot}") if cdb_slot is not None else None
                for bank in (4, 5, 6):
                    mms = []
                    rd = [B("PT"), B("vp"), B("qT"), B("Cd")] + ([bcd] if bcd is not None else [])
                    for i in range(8):
                        if ACC[i][0] != bank:
                            continue
                        h, d = i // 2, i % 2
                        o = pb[bank][:, ACC[i][1]:ACC[i][1] + 129]
                        inter = Cd[:, h, :] if d == 0 else (Cdb[cdb_slot][:, h, :] if cdb_slot is not None else None)
                        if inter is None:
                            mms.append((o, PT[:, i, :], vp[:, h, :], True, True))
                        else:
                            mms.append((o, PT[:, i, :], vp[:, h, :], True, False))
                            mms.append((o, qT[:, h, :], inter, False, True))
                    mm_group(mms, rd, writes=[BP[bank]])
                cloc_and_update(0)
                yield
                for bank in (4, 5, 6):
                    n = 3 if bank < 6 else 2
                    i0 = (bank - 4) * 3
                    act(sm[:, 8 + i0:8 + i0 + n], pb[bank][:, 128:128 + 129 * (n - 1) + 1:129], AF.Abs, [BP[bank]],
                        writes=[B("sm8")] if i0 == 0 else (), pwrites=() if i0 == 0 else [B("sm8")])
                rinv = RW[:, 0:8].rearrange("p (d h) -> p h d", d=2)
                tt("dve", sm[:, 16:24].rearrange("p (h d) -> p h d", d=2), sm[:, 8:16].rearrange("p (h d) -> p h d", d=2),
                   rinv, ALU.max, reads=[B("sm8"), B("RW")], writes=[B("sm16")])
                P.op("dve", lambda e: e.reciprocal(out=sm[:, 24:32], in_=sm[:, 16:24]), reads=[B("sm16")], writes=[B("sm24")])
                yield
                for h in range(4):
                    bf_, cf_ = ACC[2 * h]
                    bb_, cb_ = ACC[2 * h + 1]
                    act(hh[:, h * 128:(h + 1) * 128], pb[bf_][:, cf_:cf_ + 128], AF.Copy, [BP[bf_], B("sm24")],
                        writes=[B("hh")] if h == 0 else (), pwrites=() if h == 0 else [B("hh")], scale=sm[:, 24 + 2 * h:25 + 2 * h])
                    if h % 2 == 1:
                        yield
                for h in range(4):
                    bb_, cb_ = ACC[2 * h + 1]
                    stt("dve", hh[:, h * 128:(h + 1) * 128], pb[bb_][:, cb_:cb_ + 128], sm[:, 25 + 2 * h:26 + 2 * h],
                        hh[:, h * 128:(h + 1) * 128], ALU.mult, ALU.add, reads=[BP[bb_], B("sm24"), B("hh")],
                        writes=[B("hh")] if h == 0 else (), pwrites=() if h == 0 else [B("hh")])
                    if h % 2 == 1:
                        yield
                for h in range(4):
                    P.op("dve", lambda e, h=h: e.bn_stats(out=bst[:, h, :], in_=hh[:, h * 128:(h + 1) * 128]), reads=[B("hh")],
                         writes=[B("bst")] if h == 0 else (), pwrites=() if h == 0 else [B("bst")])
                    if h % 2 == 1:
                        yield
                for h in range(4):
                    P.op("dve", lambda e, h=h: e.bn_aggr(out=mv[:, h, :], in_=bst[:, h, :]), reads=[B("bst")],
                         writes=[B("mv")] if h == 0 else (), pwrites=() if h == 0 else [B("mv")])
                act(sm[:, 32:36], mv[:, :, 1], AF.Ln, [B("mv")], [B("sm32")], bias=EPS)
                act(sm[:, 36:40], sm[:, 32:36], AF.Exp, [B("sm32")], [B("sm36")], scale=-0.5)
                yield
                for h in range(4):
                    stt("dve", hh[:, h * 128:(h + 1) * 128], hh[:, h * 128:(h + 1) * 128], mv[:, h, 0:1],
                        sgo[:, h * 128:(h + 1) * 128], ALU.subtract, ALU.mult, reads=[B("hh"), B("mv"), B("sgo")],
                        writes=[B("hh")] if h == 0 else (), pwrites=() if h == 0 else [B("hh")])
                    if h % 2 == 1:
                        yield
                for h in range(4):
                    act(hn[:, h * 128:(h + 1) * 128], hh[:, h * 128:(h + 1) * 128], AF.Copy, [B("hh"), B("sm36")],
                        writes=[B("hn")] if h == 0 else (), pwrites=() if h == 0 else [B("hn")], scale=sm[:, 36 + h:37 + h])
                yield

            def s2_mlstm_T():
                transposes([(pbb[3][:, k * 128:(k + 1) * 128], hn[:, k * 128:(k + 1) * 128]) for k in range(4)],
                           [B("hn")], writes=[BP[3]])
                cp("dve", hT[:].rearrange("p k t -> p (k t)"), pbb[3][:, 0:512], reads=[BP[3]], writes=[B("hT")])

            def s2_attn(job, c, blocks):
                nb = len(blocks)
                steps = [(j, bi) for j in range(4) for bi in range(nb)]

                def st(k):
                    j, bi = steps[k]
                    which, cc, kcol = blocks[bi]
                    s_ = cc % 3
                    reg = k % 4
                    bank, lo = reg % 2, (reg // 2) * 256
                    o = pb[bank][:, lo:lo + 256]
                    mms = []
                    if which != "cur":
                        mb = mbias[:, 0:256] if which == "prev" else mbias[:, 256:512]
                        mms.append((o, ident[:], mb, True, False))
                    nm = which == "cur"
                    mms.append((o, akT[s_][0:64, j, :], aqT[0:64, 2 * j:2 * j + 2, :].rearrange("p h t -> p (h t)"), nm, True))
                    mm_group(mms, [B(f"akT{s_}"), B("aqT"), B("mbias"), B("ident")], pwrites=[BP[bank]])

                def ex_pv(k):
                    j, bi = steps[k]
                    which, cc, kcol = blocks[bi]
                    s_ = cc % 3
                    reg = k % 4
                    bank, lo = reg % 2, (reg // 2) * 256
                    o = pb[bank][:, lo:lo + 256]
                    pt = PTa[reg]
                    if kcol is None:
                        act(pt[:], o, AF.Exp, [BP[bank]], [B(f"PTa{reg}")])
                    else:
                        act(pt[:], o, AF.Exp, [BP[bank], B("flags")], [B(f"PTa{reg}")], bias=flags[:, kcol:kcol + 1])
                    for hp in range(2):
                        hd = 2 * j + hp
                        ob, oc = (2, hd * 65) if hd < 4 else (3, (hd - 4) * 65)
                        first_in_bank = (bi == 0 and hp == 0 and j in (0, 2))
                        mm_group([(pb[ob][:, oc:oc + 65], pt[:, hp * 128:(hp + 1) * 128], avp[s_][:, j, :], first_in_bank, False, True)],
                                 [B(f"PTa{reg}"), B(f"avp{s_}")], pwrites=[BP[ob]])

                st(0)
                for k in range(len(steps)):
                    if k + 1 < len(steps):
                        st(k + 1)
                    ex_pv(k)
                    if k % 2 == 1:
                        yield
                for half in range(2):
                    ob = 2 + half
                    den = pb[ob][:, 0:260].rearrange("p (h c) -> p h c", c=65)[:, :, 64]
                    tt("dve", sm[:, 40 + 4 * half:44 + 4 * half], den, esink[:, 4 * half:4 * half + 4], ALU.add,
                       reads=[BP[ob], B("esink")], writes=[B(f"sm40{half}")])
                    P.op("dve", lambda e, half=half: e.reciprocal(out=sm[:, 48 + 4 * half:52 + 4 * half],
                                                                   in_=sm[:, 40 + 4 * half:44 + 4 * half]),
                         reads=[B(f"sm40{half}")], writes=[B(f"sm48{half}")])
                    for hq in range(4):
                        hd = half * 4 + hq
                        act(oa[:, hd * 64:(hd + 1) * 64], pb[ob][:, hq * 65:hq * 65 + 64], AF.Copy, [BP[ob], B(f"sm48{half}")],
                            writes=[B("kw")] if hd == 0 else (), pwrites=() if hd == 0 else [B("kw")],
                            scale=sm[:, 48 + hd:49 + hd])
                yield

            def s2_attn_T():
                transposes([(pbb[0][:, k * 128:(k + 1) * 128], oa[:, k * 128:(k + 1) * 128]) for k in range(4)],
                           [B("kw")], writes=[BP[0]])
                cp("dve", oaT[:].rearrange("p k t -> p (k t)"), pbb[0][:, 0:512], reads=[BP[0]], writes=[B("oaT")])

            def s2_merge_gates(c, half):
                j = c % 2
                bxT = B(f"xnT{j}")
                proj(xnT[j], bxT, GM + half * 512, 512, 4)
                act(sgm[:], pb[4][:, :], AF.Sigmoid, [BP[4]], [B("sgm")])
                proj(xnT[j], bxT, GA + half * 512, 512, 5)
                act(sga[:], pb[5][:, :], AF.Sigmoid, [BP[5]], [B("sga")])

            def s2_merge(job, c, hslot):
                j = c % 2
                bxT = B(f"xnT{j}")
                bh = B(f"hres{hslot}")
                for half in range(2):
                    hs = slice(half * 512, (half + 1) * 512)
                    if half == 1:
                        s2_merge_gates(c, 1)
                    mm_group([(pb[6][:, :], hT[:, k, :], wpm[:, k, hs], k == 0, k == 3) for k in range(4)],
                             [B("hT"), B("wpm")], writes=[BP[6]])
                    mm_group([(pb[7][:, :], oaT[:, k, :], wpa[:, k, hs], k == 0, k == 3) for k in range(4)],
                             [B("oaT"), B("wpa")], writes=[BP[7]])
                    tt("dve", hh[:], pb[6][:, :], sgm[:], ALU.mult, reads=[BP[6], B("sgm")], writes=[B("hh")])
                    tt("dve", sgo[:], pb[7][:, :], sga[:], ALU.mult, reads=[BP[7], B("sga")], writes=[B("sgo")])
                    tt("dve", ybf[:, hs], hh[:], sgo[:], ALU.add, reads=[B("hh"), B("sgo")],
                       writes=[B("xn2")] if half == 0 else (), pwrites=() if half == 0 else [B("xn2")])
                transposes([(pbb[0][:, k * 128:(k + 1) * 128], ybf[:, k * 128:(k + 1) * 128]) for k in range(KC)],
                           [B("xn2")], writes=[BP[0]])
                cp("act", xn[:], pbb[0][:, 0:D], reads=[BP[0]], writes=[B("xn")])
                for half in range(2):
                    hs = slice(half * 512, (half + 1) * 512)
                    bank = 1 + half
                    mm_group([(pb[bank][:, :], yT[:, k, :], wout[:, k, hs], k == 0, k == KC - 1) for k in range(KC)],
                             [B("xn"), B("wout")], writes=[BP[bank]])
                    tt("dve", hres[hslot][:, hs], hres[hslot][:, hs], pb[bank][:, :], ALU.add, reads=[bh, BP[bank]],
                       writes=[bh] if half == 0 else (), pwrites=() if half == 0 else [bh])

            def s2_norm2(job, c, hslot, targets):
                bh = B(f"hres{hslot}")
                act(xn2[:], hres[hslot][:], AF.Square, [bh], [B("xn2"), B("sm4")], accum_out=sm[:, 4:5])
                act(sm[:, 5:6], sm[:, 4:5], AF.Ln, [B("sm4")], [B("sm5")], scale=1.0 / D, bias=EPS)
                act(sm[:, 6:7], sm[:, 5:6], AF.Exp, [B("sm5")], [B("sm6")], scale=-0.5)
                yield
                ts("pool", xn2[:], hres[hslot][:], sm[:, 6:7], 1.0, ALU.mult, ALU.mult, reads=[bh, B("sm6")], writes=[B("xn2")])
                yield
                transposes([(pbb[3][:, k * 128:(k + 1) * 128], xn2[:, k * 128:(k + 1) * 128]) for k in range(KC)],
                           [B("xn2")], writes=[BP[3]])
                src = pbb[3][:, 0:D].rearrange("p (k t) -> p k t", t=128)
                for (us, d0, s0, n, fcol) in targets:
                    bu = B(f"unit{us}")
                    if fcol is None:
                        cp("dve", unit[us][:, :, d0:d0 + n], src[:, :, s0:s0 + n], reads=[BP[3]], pwrites=[bu])
                    else:
                        ts("dve", unit[us][:, :, d0:d0 + n], src[:, :, s0:s0 + n], flags[:, fcol:fcol + 1], None, ALU.mult,
                           reads=[BP[3], B("flags")], pwrites=[bu])
                yield

            def s3_ffn(job, us, hslots, out_rows):
                bu = B(f"unit{us}")

                def load_wu(j):
                    r = j % NR
                    P.dma(f"fwu{r}", [lambda e: e.dma_start(out=wupr[r][:], in_=wup_s[j])],
                          reads=[B("wup_s")], writes=[B(f"fwu{r}")])

                def load_wd(j):
                    r = j % NR
                    P.dma(f"fwd{r}", [lambda e: e.dma_start(out=wdnr[r][:], in_=wdn_s[j])],
                          reads=[B("wdn_s")], writes=[B(f"fwd{r}")])

                def up(j):
                    r = j % NR
                    for ab in range(2):
                        bank = ab + 2 * (j % 2)
                        mm_group([(pb[bank][:, 0:258], wupr[r][:, k, ab, :], unit[us][:, k, :], k == 0, k == KC - 1)
                                  for k in range(KC)], [B(f"fwu{r}"), bu], writes=[BP[bank]])

                def conv(j):
                    cslot = j % 2
                    bc = B(f"cab{cslot}")
                    for ab in range(2):
                        bank = ab + 2 * (j % 2)
                        fidx = ab * NFC + j
                        cv = cab[cslot][:, ab, :]
                        act(cv, pb[bank][:, 1:257], AF.Identity, [BP[bank], B("cw"), B("cb")],
                            writes=[bc] if ab == 0 else (), pwrites=() if ab == 0 else [bc],
                            scale=cw[:, 1, fidx:fidx + 1], bias=cb[:, fidx:fidx + 1])
                        stt("dve", cv, pb[bank][:, 0:256], cw[:, 0, fidx:fidx + 1], cv, ALU.mult, ALU.add,
                            reads=[BP[bank], B("cw"), bc], pwrites=[bc])
                        stt("dve", cv, pb[bank][:, 2:258], cw[:, 2, fidx:fidx + 1], cv, ALU.mult, ALU.add,
                            reads=[BP[bank], B("cw"), bc], pwrites=[bc])
                    act(cab[cslot][:, 0, :], cab[cslot][:, 0, :], AF.Silu, [bc], [bc])
                    gs = j % 2
                    tt("pool", gt[gs][:], cab[cslot][:, 0, :], cab[cslot][:, 1, :], ALU.mult, reads=[bc], writes=[B(f"gt{gs}")])

                def down(j):
                    r = j % NR
                    gs = j % 2
                    for ti in range(2):
                        for half in range(2):
                            bank = 4 + ti * 2 + half
                            mm_group([(pb[bank][:, :], gt[gs][:, ti * 128:(ti + 1) * 128], wdnr[r][:, half * 512:(half + 1) * 512],
                                       j == 0, j == NFC - 1)], [B(f"gt{gs}"), B(f"fwd{r}")], pwrites=[BP[bank]])

                if not state.get("ffn_pre"):
                    for j in range(min(NR, NFC)):
                        load_wu(j)
                        load_wd(j)
                    state["ffn_pre"] = True
                up(0)
                for j in range(NFC):
                    if j + 1 < NFC:
                        up(j + 1)
                    if j + NR < NFC:
                        load_wu(j + NR)
                    conv(j)
                    down(j)
                    if j + NR < NFC:
                        load_wd(j + NR)
                for j in range(min(NR, NFC)):
                    load_wu(j)
                    load_wd(j)
                ckpt(12)
                for ti in range(2):
                    hs_ = hslots[ti]
                    bh = B(f"hres{hs_}")
                    for half in range(2):
                        bank = 4 + ti * 2 + half
                        hsl = slice(half * 512, (half + 1) * 512)
                        tt("dve", hres[hs_][:, hsl], hres[hs_][:, hsl], pb[bank][:, :], ALU.add, reads=[bh, BP[bank]],
                           writes=[bh] if half == 0 else (), pwrites=() if half == 0 else [bh])
                for ti in range(2):
                    hs_ = hslots[ti]
                    bh = B(f"hres{hs_}")
                    c0_ = 56 + 3 * ti
                    act(xn2[:], hres[hs_][:], AF.Square, [bh], [B("xn2"), B(f"smf{ti}a")], accum_out=sm[:, c0_:c0_ + 1])
                    act(sm[:, c0_ + 1:c0_ + 2], sm[:, c0_:c0_ + 1], AF.Ln, [B(f"smf{ti}a")], [B(f"smf{ti}b")], scale=1.0 / D, bias=EPS)
                    act(sm[:, c0_ + 2:c0_ + 3], sm[:, c0_ + 1:c0_ + 2], AF.Exp, [B(f"smf{ti}b")], [B(f"smf{ti}c")], scale=-0.5)
                    stt("dve", hres[hs_][:], hres[hs_][:], sm[:, c0_ + 2:c0_ + 3], nfw[:], ALU.mult, ALU.mult,
                        reads=[bh, B(f"smf{ti}c"), B("nfw")], writes=[bh])
                    dst, r0 = out_rows[ti]
                    P.dma(f"st{hs_}", [lambda e, dst=dst, r0=r0, hs_=hs_: e.dma_start(out=dst[r0:r0 + 128, :], in_=hres[hs_][:])],
                          reads=[bh], queue="pool")

            def passB(job):
                n = job["n"]
                lo2, hi2 = job["s2"]
                lo3, hi3 = job["s3"]
                dyn = job["dyn"]
                hctr = [0]
                hslot_of = {}
                unit_of = {}

                def unit_slot(u):
                    return u % NU

                interleave(s1_kv(job, 0))
                ckpt(5)
                pend = {"n": None, "f": None}

                def run_ffn():
                    if pend["f"] is not None:
                        u = pend["f"]
                        pend["f"] = None
                        ca_, cb_ = lo3 + 2 * u, lo3 + 2 * u + 1
                        s3_ffn(job, unit_slot(u), [hslot_of[ca_], hslot_of[cb_]],
                               [(job["y"], (ca_ - lo3) * 128), (job["y"], (cb_ - lo3) * 128)])
                        ckpt(11)

                for c in range(n):
                    gD = s1_kv(job, c + 1) if c + 1 < n else None
                    gN, pend["n"] = pend["n"], None
                    if not (lo2 <= c < hi2):
                        interleave(gD, gN)
                        run_ffn()
                        continue
                    interleave(s1_rest(job, c), gD, gN)
                    ckpt(6)
                    if job["has_cdb"](c):
                        cs = 0
                        load(f"cl{cs}", Cdb[cs][:].rearrange("p h c -> p (h c)"), cdb_s[c], B(f"Cdb{cs}"),
                             reads=[B(f"cdb_s{c}")])
                    else:
                        cs = None
                    run_ffn()
                    hs_ = hctr[0] % NH
                    hctr[0] += 1
                    hslot_of[c] = hs_
                    load(f"hl{hs_}", hres[hs_][:], job["x"][c * 128:(c + 1) * 128, :], B(f"hres{hs_}"))
                    blocks = []
                    if c - 1 >= 0:
                        blocks.append(("prev", c - 1, job["kcol"](c - 1)))
                    blocks.append(("cur", c, job["kcol"](c)))
                    if c + 1 < n:
                        blocks.append(("next", c + 1, job["kcol"](c + 1)))
                    interleave(s2_mlstm(job, c, job["fwd_zero"] and c == lo2, cs), s2_attn(job, c, blocks))
                    s2_merge_gates(c, 0)
                    s2_mlstm_T()
                    s2_attn_T()
                    ckpt(8)
                    s2_merge(job, c, hs_)
                    ckpt(9)
                    if DBG_STOP == (job["tab0"], c):
                        raise _Stop()
                    targets = []
                    if lo3 <= c < hi3:
                        u = (c - lo3) // 2
                        pos = (c - lo3) % 2
                        targets.append((unit_slot(u), 1 + pos * 128, 0, 128, None))
                        if pos == 0 and c - 1 >= lo3:
                            targets.append((unit_slot(u - 1), 257, 0, 1, None))
                        if pos == 1 and c + 1 < hi3:
                            targets.append((unit_slot(u + 1), 0, 127, 1, None))
                    elif c == lo3 - 1:
                        targets.append((unit_slot(0), 0, 127, 1, 2 if dyn else None))
                    elif c == hi3:
                        targets.append((unit_slot((hi3 - lo3) // 2 - 1), 257, 0, 1, 3 if dyn else None))
                    if targets:
                        pend["n"] = s2_norm2(job, c, hs_, targets)
                    if lo3 <= c < hi3 and not dyn:
                        if c == lo3:
                            P.op("dve", lambda e, us=unit_slot(0): e.memset(unit[us][:, :, 0:1], 0.0), pwrites=[B(f"unit{unit_slot(0)}")])
                        if c == hi3 - 1:
                            ul = unit_slot((hi3 - lo3) // 2 - 1)
                            P.op("dve", lambda e, us=ul: e.memset(unit[us][:, :, 257:258], 0.0), pwrites=[B(f"unit{ul}")])
                    if c >= lo3 and c - 1 >= lo3 and (c - 1 - lo3) % 2 == 1 and c - 1 < hi3:
                        pend["f"] = (c - 1 - lo3) // 2
                interleave(pend["n"])
                pend["n"] = None
                run_ffn()
                if hi3 == n:
                    pend["f"] = (hi3 - lo3) // 2 - 1
                    run_ffn()

            zero_state()
            fsave = hres[2][:, 0:516].rearrange("p (h c) -> p h c", c=129)
            P.op("pool", lambda e: e.memset(hres[2][:, 0:516], 0.0), writes=[B("hres2")])
            scan(x_ctx, [i * 128 for i in range(LCTX)], 0, ctx_flags=c_ctxf)
            ckpt(2)
            rows = [c * 128 for c in range(NPB - 2, 0, -1)]
            scan(x_own, rows, 1, store_slots=list(range(NPB - 2, 0, -1)))
            cp("pool", Cst[:], fsave, reads=[B("hres2")], writes=[B("Cst")])
            ckpt(3)
            if state.get("bg") is not None:
                for _ in state["bg"]:
                    pass
                state["bg"] = None
            pj = dict(x=x_own, n=NPB, s2=(1, NPB - 1), s3=(2, NPB - 2), dyn=True, tab0=0, y=y_own, fwd_zero=False,
                      has_cdb=lambda c: True,
                      kcol=lambda c: 0 if c < 2 else (1 if c >= NPB - 2 else None))
            passB(pj)
            ckpt(4)
            for s in range(NS):
                xs_ = x_s[s * SC * 128:(s + 1) * SC * 128, :]
                ys_ = y_s[s * SC * 128:(s + 1) * SC * 128, :]
                zero_state()
                scan(xs_, [c * 128 for c in range(SC - 1, -1, -1)], 1, store_slots=list(range(SC - 1, -1, -1)))
                sj = dict(x=xs_, n=SC, s2=(0, SC), s3=(0, SC), dyn=False, tab0=NPB, y=ys_, fwd_zero=True,
                          has_cdb=lambda c: True, kcol=lambda c: None)
                zero_state()
                passB(sj)

        except _Stop:
            pass
        P.finish()
        with nc.Block() as block:
            P.replay(block)
    return nc


def _consts():
    s = np.arange(128)[:, None]
    t = np.arange(128)[None, :]
    U = (s <= t).astype(np.float32)
    UT = (s >= t).astype(np.float32)
    tri = np.concatenate([U, UT, np.ones((128, 128), np.float32), np.eye(128, dtype=np.float32)], axis=1)
    mprev = np.where(s >= t, 0.0, NEG).astype(np.float32)
    mnext = np.where(s <= t, 0.0, NEG).astype(np.float32)
    mb = np.concatenate([mprev, mprev, mnext, mnext], axis=1).astype(ml_dtypes.bfloat16)
    ident = np.eye(128, dtype=np.float32).astype(ml_dtypes.bfloat16)
    return ident, tri, mb


def _rope_rows(pos):
    inv = (np.float32(ROPE_THETA) ** (-np.arange(8, dtype=np.float32) / np.float32(8))).astype(np.float32)
    ang = (pos.astype(np.float32)[:, None] * inv[None, :]).astype(np.float32)
    return np.concatenate([np.cos(ang.astype(np.float64)), np.sin(ang.astype(np.float64))], axis=1).astype(np.float32)


_PROGS = {}


def run_layer(inputs, PC, SC, NS):
    xp = np.ascontiguousarray(inputs["x_prompt"][0], dtype=np.float32)
    xsamp = np.ascontiguousarray(inputs["x_sample"], dtype=np.float32)
    NPB, LCTX = PC + 4, 7 * PC - 1
    nchunks = xp.shape[0] // 128
    assert nchunks == NCORES * PC and xsamp.shape[0] == NCORES * NS and xsamp.shape[1] == SC * 128

    def take(g0, n):
        out = np.zeros((n * 128, D), np.float32)
        lo, hi = max(g0, 0), min(g0 + n, nchunks)
        if hi > lo:
            out[(lo - g0) * 128:(hi - g0) * 128] = xp[lo * 128:hi * 128]
        return out

    ident, tri, mb = _consts()
    f32 = lambda a: np.ascontiguousarray(a, dtype=np.float32)
    shared = dict(
        w_in=f32(inputs["w_in"][0]), w_pm=f32(inputs["w_proj_m"][0]), w_pa=f32(inputs["w_proj_a"][0]),
        w_out=f32(inputs["w_out"][0]), w_up=f32(inputs["w_up"][0]), w_dn=f32(inputs["w_down"][0]),
        norm1=f32(inputs["norm1_w"][0]), norm2=f32(inputs["norm2_w"][0]), normf=f32(inputs["norm_f_w"]),
        mhw=f32(inputs["mh_norm_w"][0]),
        gbias=f32(np.concatenate([inputs["i_bias"][0].reshape(-1), inputs["f_bias"][0].reshape(-1)])),
        sink=f32(inputs["attn_sink"][0]), convw=f32(inputs["conv_w"][0]), convb=f32(inputs["conv_b"][0]),
        c_ident=ident, c_tri=tri, c_mb=mb)
    p = np.arange(128)
    in_maps = []
    for i in range(NCORES):
        g0 = i * PC - 2
        rope = np.stack([_rope_rows((g0 + l) * 128 + p) for l in range(NPB)] +
                        [_rope_rows(c * 128 + p) for c in range(SC)], axis=0)
        fl = np.zeros((128, 4), np.float32)
        fl[:, 0] = NEG if i == 0 else 0.0
        fl[:, 1] = NEG if i == NCORES - 1 else 0.0
        fl[:, 2] = 0.0 if i == 0 else 1.0
        fl[:, 3] = 0.0 if i == NCORES - 1 else 1.0
        m = dict(shared)
        nF, nB = max(i * PC - 1, 0), max((NCORES - 1 - i) * PC - 1, 0)
        npad = LCTX - nF - nB
        order = list(range(0, nF)) + list(range(nchunks - 1, nchunks - 1 - nB, -1))
        x_ctx = np.zeros((LCTX * 128, D), np.float32)
        for k, gch in enumerate(order):
            x_ctx[(npad + k) * 128:(npad + k + 1) * 128] = xp[gch * 128:(gch + 1) * 128]
        f0 = np.array([1.0] * (npad + nF) + [0.0] * nB, np.float32)
        cf = np.zeros((LCTX, 128, 16), np.float32)
        cf[:, :, 0:4] = f0[:, None, None]
        cf[:, :, 4:8] = 1.0 - f0[:, None, None]
        cf[:, :, 8:12] = f0[:, None, None]
        cf[:, :, 12:16] = 1.0
        m.update(x_own=take(g0, NPB), x_ctx=x_ctx, c_ctxf=cf,
                 x_s=np.ascontiguousarray(xsamp[i * NS:(i + 1) * NS].reshape(NS * SC * 128, D)),
                 c_rope=rope, c_flags=fl)
        in_maps.append(m)
    key = (PC, SC, NS)
    if key not in _PROGS:
        _PROGS[key] = build_program(PC, SC, NS)
    res = run_bass_kernel_spmd(_PROGS[key], in_maps, core_ids=list(range(NCORES)))
    y_p = np.concatenate([r["y_own"] for r in res.results], axis=0)[None]
    y_s = np.stack([r["y_s"].reshape(NS, SC * 128, D) for r in res.results], axis=0).reshape(NCORES * NS, SC * 128, D)
    return y_p.astype(np.float32), y_s.astype(np.float32)


def kernel(**inputs):
    return run_layer(inputs, PC=16, SC=32, NS=2)
```

```python
import contextlib
import numpy as np
import ml_dtypes
import concourse.bass as bass
import concourse.mybir as mybir
from concourse.bass_utils import run_bass_kernel_spmd

F32 = mybir.dt.float32
BF16 = mybir.dt.bfloat16
AF = mybir.ActivationFunctionType
ALU = mybir.AluOpType

NCORES = 8
D = 1024
KC = 8
DIN = 5136
DFF = 2816
NFC = 22
MQ, MK, MV, MO, GI, AQ, AK, AV, GM, GA = 0, 512, 1024, 1536, 2048, 2064, 2576, 2832, 3088, 4112
EPS = 1e-6
NEG = -30000.0
DBG_STOP = None
DBG_PHASE = None


class _Stop(Exception):
    pass
ROPE_THETA = 500000.0


class Buf:
    __slots__ = ("name", "writers", "readers", "psum")

    def __init__(self, name, psum=False):
        self.name = name
        self.writers = []
        self.readers = []
        self.psum = psum


class Prog:
    ENG = ("pe", "act", "dve", "pool", "sp")
    HANDLE = {"pe": "tensor", "act": "scalar", "dve": "vector", "pool": "gpsimd", "sp": "sync"}

    def __init__(self, nc, stack):
        self.nc = nc
        self.stack = stack
        self.streams = {e: [] for e in self.ENG}
        self.sems = {}
        self.count = {}
        self.known = {e: {} for e in self.ENG}
        for e in self.ENG:
            self._sem(e)

    def _sem(self, key):
        if key not in self.sems:
            self.sems[key] = self.stack.enter_context(self.nc.semaphore("s_" + key))
            self.count[key] = 0
        return self.sems[key]

    @staticmethod
    def _deps(reads, writes, pwrites, eng=None):
        deps = {}

        def add(kv):
            if deps.get(kv[0], 0) < kv[1]:
                deps[kv[0]] = kv[1]
        ps = [b for b in list(reads) + list(writes) + list(pwrites) if b.psum]
        for b in ps:
            for w in b.writers:
                if w[0] != eng:
                    add(w)
        reads = [b for b in reads if not b.psum]
        writes = [b for b in writes if not b.psum]
        pwrites = [b for b in pwrites if not b.psum]
        for b in reads:
            for w in b.writers:
                add(w)
        for b in writes:
            for w in b.writers:
                add(w)
            for r in b.readers:
                add(r)
        for b in pwrites:
            for r in b.readers:
                add(r)
            if b.readers:
                for w in b.writers:
                    add(w)
        return deps

    def _waits(self, eng, deps):
        kn = self.known[eng]
        for k, v in sorted(deps.items()):
            if k == eng and eng == "pe":
                continue
            if kn.get(k, 0) >= v:
                continue
            kn[k] = v
            self.streams[eng].append(("wait", k, v))

    @staticmethod
    def _update(me, reads, writes, pwrites):
        for b in list(reads) + list(writes) + list(pwrites):
            if b.psum:
                b.writers = [me]
        reads = [b for b in reads if not b.psum]
        writes = [b for b in writes if not b.psum]
        pwrites = [b for b in pwrites if not b.psum]
        for b in reads:
            b.readers.append(me)
        for b in writes:
            b.writers = [me]
            b.readers = []
        for b in pwrites:
            if b.readers:
                b.writers = [me]
                b.readers = []
            else:
                b.writers.append(me)

    def op(self, eng, fn, reads=(), writes=(), pwrites=()):
        self._waits(eng, self._deps(reads, writes, pwrites, eng))
        self.count[eng] += 1
        self.streams[eng].append(("op", fn, eng, 1))
        self._update((eng, self.count[eng]), reads, writes, pwrites)

    def dma(self, slot, fns, reads=(), writes=(), queue="sp"):
        self._sem(slot)
        self._waits(queue, self._deps(reads, writes, (), queue))
        for fn in fns:
            self.count[slot] += 16
            self.streams[queue].append(("op", fn, slot, 16))
        self._update((slot, self.count[slot]), reads, writes, ())

    def finish(self, queue="sp"):
        for k, v in self.count.items():
            if v > 0:
                self.streams[queue].append(("wait", k, v))

    def replay(self, block):
        for e in self.ENG:
            stream = self.streams[e]
            if not stream:
                continue

            def body(h, stream=stream):
                for item in stream:
                    if item[0] == "wait":
                        h.wait_ge(self.sems[item[1]], item[2])
                    else:
                        item[1](h).then_inc(self.sems[item[2]], item[3])
            getattr(block, self.HANDLE[e])(body)


def build_program(PC, SC, NS):
    NPB = PC + 4
    LCTX = 7 * PC - 1
    NTAB = NPB + SC
    nc = bass.Bass("TRN2", target_bir_lowering=False)

    def din(name, shape, dt=F32):
        return nc.dram_tensor(name, list(shape), dt, kind="ExternalInput").ap()

    x_own = din("x_own", [NPB * 128, D])
    x_ctx = din("x_ctx", [LCTX * 128, D])
    c_ctxf = din("c_ctxf", [LCTX, 128, 16])
    x_s = din("x_s", [NS * SC * 128, D])
    w_in = din("w_in", [D, DIN])
    w_pm = din("w_pm", [512, D])
    w_pa = din("w_pa", [512, D])
    w_out = din("w_out", [D, D])
    w_up = din("w_up", [D, 2 * DFF])
    w_dn = din("w_dn", [DFF, D])
    norm1 = din("norm1", [D])
    norm2 = din("norm2", [D])
    normf = din("normf", [D])
    mhw = din("mhw", [512])
    gbias = din("gbias", [16])
    sink = din("sink", [8])
    convw = din("convw", [3, 2 * DFF])
    convb = din("convb", [2 * DFF])
    c_ident = din("c_ident", [128, 128], BF16)
    c_tri = din("c_tri", [128, 512])
    c_mb = din("c_mb", [128, 512], BF16)
    c_rope = din("c_rope", [NTAB, 128, 16])
    c_flags = din("c_flags", [128, 4])
    y_own = nc.dram_tensor("y_own", [PC * 128, D], F32, kind="ExternalOutput").ap()
    y_s = nc.dram_tensor("y_s", [NS * SC * 128, D], F32, kind="ExternalOutput").ap()
    wup_s = nc.dram_tensor("wup_s", [NFC, 128, KC, 2, 128], BF16).ap()
    wdn_s = nc.dram_tensor("wdn_s", [NFC, 128, D], BF16).ap()
    NCD = max(NPB, SC)
    cdb_s = nc.dram_tensor("cdb_s", [NCD, 128, 516], BF16).ap()

    with contextlib.ExitStack() as st:
        P = Prog(nc, st)
        bufs = {}

        def B(name):
            if name not in bufs:
                bufs[name] = Buf(name)
            return bufs[name]

        def sb(name, shape, dt=F32):
            return st.enter_context(nc.sbuf_tensor(name, list(shape), dt))

        win = sb("win", [128, KC, DIN], BF16)
        wpm = sb("wpm", [128, 4, D], BF16)
        wpa = sb("wpa", [128, 4, D], BF16)
        wout = sb("wout", [128, KC, D], BF16)
        ident = sb("ident", [128, 128], BF16)
        tri = sb("tri", [128, 512])
        mbias = sb("mbias", [128, 512], BF16)
        flags = sb("flags", [128, 4])
        gb = sb("gb", [128, 16])
        esink = sb("esink", [128, 8])
        n1c = sb("n1c", [128, KC])
        n2c = sb("n2c", [128, KC])
        mhc = sb("mhc", [128, 4])
        cw = sb("cw", [128, 3, 2 * NFC])
        cb = sb("cb", [128, 2 * NFC])
        nfw = sb("nfw", [128, D])
        ropes = [sb(f"rope{i}", [128, 16]) for i in range(3)]
        flt = [sb(f"flt{i}", [128, 16]) for i in range(2)]
        xs = [sb(f"x{i}", [128, D]) for i in range(1)]
        xn = sb("xn", [128, D], BF16)
        xnT = [sb(f"xnT{i}", [128, KC, 128], BF16) for i in range(2)]
        sm = sb("sm", [128, 64])
        kb = sb("kb", [128, 512], BF16)
        kw = sb("kw", [128, 512], BF16)
        vp = sb("vp", [128, 4, 129], BF16)
        sgo = sb("sgo", [128, 512])
        rowst = sgo
        G = sb("G", [128, 16])
        E8 = sb("E8", [128, 8])
        LF = sb("LF", [128, 8])
        NB = sb("NB", [128, 16])
        RW = sb("RW", [128, 16])
        RW2 = sb("RW2", [128, 16])
        Dd = sb("Dd", [128, 8])
        Dd2 = sb("Dd2", [128, 8])
        qT = sb("qT", [128, 4, 128], BF16)
        kT = sb("kT", [128, 4, 128], BF16)
        aqr = sb("aqr", [128, 512], BF16)
        akd = sb("akd", [128, 4, 64], BF16)
        rt = sb("rt", [128, 4, 8, 8])
        aqT = sb("aqT", [128, 8, 128], BF16)
        akT = [sb(f"akT{i}", [128, 4, 128], BF16) for i in range(3)]
        avp = [sb(f"avp{i}", [128, 4, 65], BF16) for i in range(3)]
        PT = sb("PT", [128, 8, 128], BF16)
        Cst = sb("Cst", [128, 4, 129])
        Cd = sb("Cd", [128, 4, 129], BF16)
        Cdb = [sb(f"Cdb{i}", [128, 4, 129], BF16) for i in range(1)]
        hh = sb("hh", [128, 512])
        bst = sb("bst", [128, 4, 6])
        mv = sb("mv", [128, 4, 2])
        hn = sb("hn", [128, 512], BF16)
        oa = kw
        qb = hn
        hT = sb("hT", [128, 4, 128], BF16)
        PTa = [sb(f"PTa{i}", [128, 256], BF16) for i in range(4)]
        oaT = sb("oaT", [128, 4, 128], BF16)
        sgm = sb("sgm", [128, 512], BF16)
        sga = sb("sga", [128, 512], BF16)
        NH = 3
        hres = [sb(f"hres{i}", [128, D]) for i in range(NH)]
        xn2 = sb("xn2", [128, D], BF16)
        ybf = xn2
        yT = xn[:].rearrange("p (k t) -> p k t", t=128)
        NU = 2
        unit = [sb(f"unit{i}", [128, KC, 258], BF16) for i in range(NU)]
        NR = 3
        wupr = [sb(f"wupr{i}", [128, KC, 2, 128], BF16) for i in range(NR)]
        wdnr = [sb(f"wdnr{i}", [128, D], BF16) for i in range(NR)]
        cab = [sb(f"cab{i}", [128, 2, 256]) for i in range(2)]
        gt = [sb(f"gt{i}", [128, 256], BF16) for i in range(2)]
        stg = [hres[0], hres[1]]
        stgb = [xn2, xn]
        pb = [st.enter_context(nc.psum_tensor(f"pb{i}", [128, 512], F32)) for i in range(8)]
        pbb = [p[:].bitcast(BF16) for p in pb]
        BP = [Buf(f"pb{i}", psum=True) for i in range(8)]

        U = tri[:, 0:128]
        UT = tri[:, 128:256]
        ONES = tri[:, 256:384]
        IDF = tri[:, 384:512]

        def load(slot, out, in_, wbuf, reads=()):
            P.dma(slot, [lambda e: e.dma_start(out=out, in_=in_)], reads=list(reads), writes=[wbuf])

        def act(out, in_, func, reads, writes=(), pwrites=(), **kw_):
            P.op("act", lambda e: e.activation(out=out, in_=in_, func=func, **kw_), reads=reads,
                 writes=writes, pwrites=pwrites)

        def ts(eng, out, in0, s1, s2, op0, op1=None, reads=(), writes=(), pwrites=()):
            if op1 is None:
                P.op(eng, lambda e: e.tensor_scalar(out=out, in0=in0, scalar1=s1, scalar2=None, op0=op0),
                     reads=reads, writes=writes, pwrites=pwrites)
            else:
                P.op(eng, lambda e: e.tensor_scalar(out=out, in0=in0, scalar1=s1, scalar2=s2, op0=op0, op1=op1),
                     reads=reads, writes=writes, pwrites=pwrites)

        def tt(eng, out, in0, in1, op, reads=(), writes=(), pwrites=()):
            P.op(eng, lambda e: e.tensor_tensor(out=out, in0=in0, in1=in1, op=op), reads=reads,
                 writes=writes, pwrites=pwrites)

        def stt(eng, out, in0, scalar, in1, op0, op1, reads=(), writes=(), pwrites=()):
            P.op(eng, lambda e: e.scalar_tensor_tensor(out=out, in0=in0, scalar=scalar, in1=in1, op0=op0, op1=op1),
                 reads=reads, writes=writes, pwrites=pwrites)

        def cp(eng, out, in_, reads=(), writes=(), pwrites=()):
            if eng == "act":
                act(out, in_, AF.Copy, reads, writes, pwrites)
            else:
                P.op(eng, lambda e: e.tensor_copy(out=out, in_=in_), reads=reads, writes=writes, pwrites=pwrites)

        def mm_group(mms, reads, writes=(), pwrites=()):
            def fn(e):
                ins = None
                for m in mms:
                    o, l, r, s0, s1 = m[:5]
                    if len(m) > 5:
                        ins = e.matmul(o, lhsT=l, rhs=r, start=s0, stop=s1, skip_group_check=True)
                    else:
                        ins = e.matmul(o, lhsT=l, rhs=r, start=s0, stop=s1)
                return ins
            P.op("pe", fn, reads=reads, writes=writes, pwrites=pwrites)

        def transposes(outs_ins, reads, writes=(), pwrites=()):
            def fn(e):
                ins = None
                for (o, i) in outs_ins:
                    ins = e.transpose(out=o, in_=i, identity=ident[:])
                return ins
            P.op("pe", fn, reads=list(reads) + [B("ident")], writes=writes, pwrites=pwrites)

        try:
            def ckpt(n):
                if DBG_PHASE == n:
                    raise _Stop()

            load("c0", ident[:], c_ident, B("ident"))
            load("c1", tri[:], c_tri, B("tri"))
            load("c2", mbias[:], c_mb, B("mbias"))
            load("c3", flags[:], c_flags, B("flags"))
            load("c4", gb[:], gbias.partition_broadcast(128), B("gb"))
            load("c5", esink[:], sink.partition_broadcast(128), B("esink"))
            load("c6", nfw[:], normf.partition_broadcast(128), B("nfw"))

            def load_T(dst, src_rows, n, wbuf, first=True):
                P.dma("c7", [lambda e: e.dma_start(out=rowst[0:n, 0:128], in_=src_rows)], writes=[B("sgo")])
                mm_group([(pb[0][:, 0:n], rowst[0:n, 0:128], IDF[0:n, 0:n], True, True)], [B("sgo"), B("tri")], writes=[BP[0]])
                cp("dve", dst, pb[0][:, 0:n], reads=[BP[0]], writes=[wbuf] if first else (), pwrites=() if first else [wbuf])
            load_T(n1c[:], norm1.rearrange("(k p) -> k p", p=128), KC, B("n1c"))
            load_T(n2c[:], norm2.rearrange("(k p) -> k p", p=128), KC, B("n2c"))
            load_T(mhc[:], mhw.rearrange("(k p) -> k p", p=128), 4, B("mhc"))
            for j in range(3):
                load_T(cw[:, j, :], convw[j].rearrange("(c p) -> c p", p=128), 2 * NFC, B("cw"), first=(j == 0))
            load_T(cb[:], convb.rearrange("(c p) -> c p", p=128), 2 * NFC, B("cb"))
            act(esink[:], esink[:], AF.Exp, [B("esink")], [B("esink")])
            P.op("pool", lambda e: e.memset(vp[:], 1.0), writes=[B("vp")])
            for i in range(3):
                P.op("pool", lambda e, i=i: e.memset(avp[i][:], 1.0), writes=[B(f"avp{i}")])

            PIECE = 1408
            wcount = [0]

            bcount = [0]

            def prep2(src_rows, ncols, normcol, consume, scale_ranges=(), direct=None, bdirect=None, bg=False):
                c0 = 0
                while c0 < ncols:
                    c1 = min(ncols, c0 + D)
                    i = 1 if bg else wcount[0] % 3
                    wcount[0] += 1
                    sf, bsf = hres[i], B(f"hres{i}")
                    w = c1 - c0
                    load(f"wl{i}", sf[:, 0:w], src_rows[:, c0:c1], bsf)
                    if direct is None:
                        if bg:
                            ib, sbf, bsb = 0, PT[:].rearrange("p k t -> p (k t)"), B("PT")
                        else:
                            ib = bcount[0] % 2
                            bcount[0] += 1
                            sbf, bsb = stgb[ib], B("xn2" if ib == 0 else "xn")
                    cuts = sorted(set([c0, c1] + [c for (a, b, s) in scale_ranges for c in (a, b) if c0 < c < c1]))
                    eng = "pool" if bg else ("dve" if wcount[0] % 2 == 0 else "pool")
                    first = True
                    for a, b in zip(cuts[:-1], cuts[1:]):
                        s = 1.0
                        for (ra, rb, rs) in scale_ranges:
                            if ra <= a and b <= rb:
                                s = rs
                        sc1 = normcol if normcol is not None else 1.0
                        rd = [bsf] + ([] if normcol is None else [B("ncols")])
                        if direct is not None:
                            ts(eng, direct(a, b), sf[:, a - c0:b - c0], sc1, float(s), ALU.mult, ALU.mult, reads=rd,
                               pwrites=[bdirect])
                        else:
                            ts(eng, sbf[:, a - c0:b - c0], sf[:, a - c0:b - c0], sc1, float(s), ALU.mult, ALU.mult, reads=rd,
                               writes=[bsb] if first else (), pwrites=() if first else [bsb])
                        first = False
                    if direct is None:
                        consume(c0, c1, sbf, bsb, ib)
                    c0 = c1
                    if bg:
                        yield

            bufs["ncols"] = Buf("ncols")
            for nm in ("n1c", "n2c", "mhc"):
                for w_ in B(nm).writers:
                    bufs["ncols"].writers.append(w_)

            sr_in = [(MK, MK + 512, 128.0 ** -0.5), (AQ, AQ + 512, 0.125)]
            for kc in range(KC):
                for _ in prep2(w_in[kc * 128:(kc + 1) * 128, :], DIN, n1c[:, kc:kc + 1], None, sr_in,
                      direct=lambda a, b, kc=kc: win[:, kc, a:b], bdirect=B("win")):
                    pass
            for kc in range(4):
                for _ in prep2(w_pm[kc * 128:(kc + 1) * 128, :], D, mhc[:, kc:kc + 1], None,
                      direct=lambda a, b, kc=kc: wpm[:, kc, a:b], bdirect=B("wpm")):
                    pass
            for kc in range(4):
                for _ in prep2(w_pa[kc * 128:(kc + 1) * 128, :], D, None, None,
                      direct=lambda a, b, kc=kc: wpa[:, kc, a:b], bdirect=B("wpa")):
                    pass
            for kc in range(KC):
                for _ in prep2(w_out[kc * 128:(kc + 1) * 128, :], D, None, None,
                      direct=lambda a, b, kc=kc: wout[:, kc, a:b], bdirect=B("wout")):
                    pass
            def ffn_weight_prep():
                wup_v = wup_s.rearrange("j p k a c -> p j k a c")
                for kc in range(KC):
                    def consume(c0, c1, sbf, bsb, i, kc=kc):
                        fns = []
                        c = c0
                        while c < c1:
                            ab = 0 if c < DFF else 1
                            lim = DFF if ab == 0 else 2 * DFF
                            ce = min(c1, lim)
                            j0 = (c - ab * DFF) // 128
                            nj = (ce - c) // 128
                            src = sbf[:, c - c0:ce - c0].rearrange("p (j c) -> p j c", c=128)
                            dst = wup_v[:, j0:j0 + nj, kc, ab, :]
                            fns.append(lambda e, src=src, dst=dst: e.dma_start(out=dst, in_=src))
                            c = ce
                        P.dma(f"ws{i}", fns, reads=[bsb], writes=[B("wup_s")] if False else [], )
                        B("wup_s").writers.append((f"ws{i}", P.count[f"ws{i}"]))
                        bsb.readers.append((f"ws{i}", P.count[f"ws{i}"]))
                    yield from prep2(w_up[kc * 128:(kc + 1) * 128, :], 2 * DFF, n2c[:, kc:kc + 1], consume, bg=True)
                for j in range(NFC):
                    def consume(c0, c1, sbf, bsb, i, j=j):
                        P.dma(f"ws{i}", [lambda e: e.dma_start(out=wdn_s[j], in_=sbf[:, 0:D])], reads=[bsb])
                        B("wdn_s").writers.append((f"ws{i}", P.count[f"ws{i}"]))
                    yield from prep2(w_dn[j * 128:(j + 1) * 128, :], D, None, consume, bg=True)


            state = {"bg": ffn_weight_prep()}
            ckpt(1)
            state.update({"xi": 0, "ri": 0})

            def rmsnorm_T(xt, bx, dstT, bdst, psum_i=3, xn_=None, bxn=None):
                if xn_ is None:
                    xn_, bxn = xn, B("xn")
                act(xn_[:], xt, AF.Square, [bx], [bxn, B("sm0")], accum_out=sm[:, 0:1])
                act(sm[:, 1:2], sm[:, 0:1], AF.Ln, [B("sm0")], [B("sm1")], scale=1.0 / D, bias=EPS)
                act(sm[:, 2:3], sm[:, 1:2], AF.Exp, [B("sm1")], [B("sm2")], scale=-0.5)
                act(xn_[:], xt, AF.Copy, [bx, B("sm2")], [bxn], scale=sm[:, 2:3])
                transposes([(pbb[psum_i][:, k * 128:(k + 1) * 128], xn_[:, k * 128:(k + 1) * 128]) for k in range(KC)],
                           [bxn], writes=[BP[psum_i]])
                cp("dve", dstT.rearrange("p k t -> p (k t)"), pbb[psum_i][:, 0:D], reads=[BP[psum_i]], writes=[bdst])

            def proj(xT, bxT, c0, n, bank, lo=0):
                mm_group([(pb[bank][:, lo:lo + n], xT[:, k, :], win[:, k, c0:c0 + n], k == 0, k == KC - 1) for k in range(KC)],
                         [bxT, B("win")], writes=[BP[bank]])

            SL = lambda: None
            def slot(i):
                if i == 0:
                    return dict(RW=RW, bRW=B("RW"), Dd=Dd, bDd=B("Dd"), kb=kb, bkb=B("kb"), vp=vp, bvp=B("vp"))
                return dict(RW=RW2, bRW=B("RW2"), Dd=Dd2, bDd=B("Dd2"), kb=aqr, bkb=B("aqr"), vp=Cdb[0], bvp=B("Cdb0"))

            def gates(si=0, cb_=3, part=None):
                S_ = slot(si)
                RW_, bRW, Dd_, bDd = S_["RW"], S_["bRW"], S_["Dd"], S_["bDd"]
                if part in (None, "a"):
                    act(E8[:], G[:, 8:16], AF.Exp, [B("G")], [B("E8")], scale=-1.0)
                    act(LF[:], E8[:], AF.Ln, [B("E8")], [B("LF")], bias=1.0)
                if part in (None, "b"):
                    mm_group([(pb[cb_][:, 0:4], U, LF[:, 0:4], True, True),
                              (pb[cb_][:, 4:8], UT, LF[:, 4:8], True, True),
                              (pb[cb_][:, 8:16], ONES, LF[:, 0:8], True, True)], [B("tri"), B("LF")], writes=[BP[cb_]])
                if part in (None, "c"):
                    cp("dve", NB[:], pb[cb_][:, 0:16], reads=[BP[cb_]], writes=[B("NB")])
                    tt("dve", RW_[:, 0:8], NB[:, 0:8], NB[:, 8:16], ALU.subtract, reads=[B("NB")], writes=[bRW])
                    tt("dve", RW_[:, 8:16], RW_[:, 0:8], G[:, 0:8], ALU.add, reads=[bRW, B("G")], writes=[bRW])
                    act(RW_[:], RW_[:], AF.Exp, [bRW], [bRW])
                    act(Dd_[:], NB[:, 8:16], AF.Exp, [B("NB")], [bDd], scale=-1.0)

            def state_update(d, si=0):
                S_ = slot(si)
                for h in range(4):
                    act(Cd[:, h, :], Cst[:, h, :], AF.Copy, [B("Cst"), S_["bDd"]], writes=[B("Cd")] if h == 0 else (),
                        pwrites=() if h == 0 else [B("Cd")], scale=S_["Dd"][:, d * 4 + h:d * 4 + h + 1])

            def cloc_and_update(d, si=0):
                S_ = slot(si)
                RW_, Dd_, kb_, vp_ = S_["RW"], S_["Dd"], S_["kb"], S_["vp"]
                for h in range(4):
                    ts("dve", kw[:, h * 128:(h + 1) * 128], kb_[:, h * 128:(h + 1) * 128], RW_[:, 8 + d * 4 + h:9 + d * 4 + h],
                       None, ALU.mult, reads=[S_["bkb"], S_["bRW"]], writes=[B("kw")] if h == 0 else (),
                       pwrites=() if h == 0 else [B("kw")])
                regs = [(7, 0), (7, 129), (7, 258), (6, 258)]
                mm_group([(pb[regs[h][0]][:, regs[h][1]:regs[h][1] + 129], kw[:, h * 128:(h + 1) * 128], vp_[:, h, :], True, True)
                          for h in range(3)], [B("kw"), S_["bvp"]], writes=[BP[7]])
                mm_group([(pb[6][:, 258:387], kw[:, 384:512], vp_[:, 3, :], True, True)], [B("kw"), S_["bvp"]], pwrites=[BP[6]])
                for h in range(4):
                    bk, c0 = regs[h]
                    stt("dve", Cst[:, h, :], Cst[:, h, :], Dd_[:, d * 4 + h:d * 4 + h + 1], pb[bk][:, c0:c0 + 129],
                        ALU.mult, ALU.add, reads=[B("Cst"), S_["bDd"], BP[bk]], writes=[B("Cst")] if h == 0 else (),
                        pwrites=() if h == 0 else [B("Cst")])

            def load_x(src, row0):
                i = 0
                state["xi"] += 1
                load(f"xl{i}", xs[i][:], src[row0:row0 + 128, :], B(f"x{i}"))
                return i

            def zero_state():
                P.op("pool", lambda e: e.memset(Cst[:], 0.0), writes=[B("Cst")])

            XS = [(xs[0], "x0"), (hres[0], "hres0")]

            XN = [(xn, "xn"), (xn2, "xn2")]

            def scan(src, chunk_rows, d, store_slots=None, ctx_flags=None):
                P.op("dve", lambda e: e.memset(Cdb[0][:, :, 128:129], 1.0), writes=[B("Cdb0")])
                n = len(chunk_rows)

                def s0_act(i):
                    si = i % 2
                    xt, bxn = XS[si]
                    bx = B(bxn)
                    xn_, bn_ = XN[si][0], B(XN[si][1])
                    load(f"xl{si}", xt[:], src[chunk_rows[i]:chunk_rows[i] + 128, :], bx)
                    if ctx_flags is not None:
                        load(f"fl{si}", flt[si][:], ctx_flags[i], B(f"flt{si}"))
                    act(xn_[:], xt[:], AF.Square, [bx], [bn_, B("sm0")], accum_out=sm[:, 0:1])
                    act(sm[:, 1:2], sm[:, 0:1], AF.Ln, [B("sm0")], [B("sm1")], scale=1.0 / D, bias=EPS)
                    act(sm[:, 2:3], sm[:, 1:2], AF.Exp, [B("sm1")], [B("sm2")], scale=-0.5)
                    act(xn_[:], xt[:], AF.Copy, [bx, B("sm2")], [bn_], scale=sm[:, 2:3])

                def s0_pe(i):
                    si = i % 2
                    xn_, bn_ = XN[si][0], B(XN[si][1])
                    transposes([(pbb[3][:, k * 128:(k + 1) * 128], xn_[:, k * 128:(k + 1) * 128]) for k in range(KC)],
                               [bn_], writes=[BP[3]])
                    cp("dve", xnT[si][:].rearrange("p k t -> p (k t)"), pbb[3][:, 0:D], reads=[BP[3]], writes=[B(f"xnT{si}")])

                def s1_pe(i):
                    si = i % 2
                    bxT = B(f"xnT{si}")
                    proj(xnT[si], bxT, GI, 16, 2)
                    proj(xnT[si], bxT, MK, 512, 0)
                    proj(xnT[si], bxT, MV, 512, 1)

                def s1_ev(i):
                    S_ = slot(i % 2)
                    tt("dve", G[:], pb[2][:, 0:16], gb[:], ALU.add, reads=[BP[2], B("gb")], writes=[B("G")])
                    cp("act", S_["kb"][:], pb[0][:, :], reads=[BP[0]], writes=[S_["bkb"]])
                    cp("act", S_["vp"][:, :, 0:128], pb[1][:, :].rearrange("p (h c) -> p h c", h=4), reads=[BP[1]],
                       pwrites=[S_["bvp"]])

                def s1_g1(i):
                    act(E8[:], G[:, 8:16], AF.Exp, [B("G")], [B("E8")], scale=-1.0)
                    act(LF[:], E8[:], AF.Ln, [B("E8")], [B("LF")], bias=1.0)
                    mm_group([(pb[5][:, 0:4], U, LF[:, 0:4], True, True),
                              (pb[5][:, 4:8], UT, LF[:, 4:8], True, True),
                              (pb[5][:, 8:16], ONES, LF[:, 0:8], True, True)], [B("tri"), B("LF")], writes=[BP[5]])

                def s1_g2(i):
                    S_ = slot(i % 2)
                    RW_, bRW, Dd_, bDd = S_["RW"], S_["bRW"], S_["Dd"], S_["bDd"]
                    cp("dve", NB[:], pb[5][:, 0:16], reads=[BP[5]], writes=[B("NB")])
                    tt("dve", RW_[:, 0:8], NB[:, 0:8], NB[:, 8:16], ALU.subtract, reads=[B("NB")], writes=[bRW])
                    tt("dve", RW_[:, 8:16], RW_[:, 0:8], G[:, 0:8], ALU.add, reads=[bRW, B("G")], writes=[bRW])
                    act(RW_[:], RW_[:], AF.Exp, [bRW], [bRW])
                    act(Dd_[:], NB[:, 8:16], AF.Exp, [B("NB")], [bDd], scale=-1.0)
                    if ctx_flags is not None:
                        fi = i % 2
                        bfl = B(f"flt{fi}")
                        tt("dve", RW_[:, 8:16], RW_[:, 8:16], flt[fi][:, 0:8], ALU.mult, reads=[bRW, bfl], writes=[bRW])
                        stt("dve", Dd_[:], Dd_[:], -1.0, flt[fi][:, 8:16], ALU.add, ALU.mult, reads=[bDd, bfl], writes=[bDd])
                        ts("dve", Dd_[:], Dd_[:], 1.0, None, ALU.add, reads=[bDd], writes=[bDd])

                REGS = [(7, 0), (7, 129), (7, 258), (6, 258)]
                REGS_B = [(4, 0), (4, 129), (4, 258), (6, 0)]
                fstate = hres[2][:, 0:516].rearrange("p (h c) -> p h c", c=129)

                def s2_a(i):
                    si = i % 2
                    S_ = slot(si)
                    if ctx_flags is not None:
                        for dd, kwt, bkw, regs in ((0, kw, B("kw"), REGS), (1, hn, B("hn"), REGS_B)):
                            for h in range(4):
                                ts("pool", kwt[:, h * 128:(h + 1) * 128], S_["kb"][:, h * 128:(h + 1) * 128],
                                   S_["RW"][:, 8 + dd * 4 + h:9 + dd * 4 + h], 1.0, ALU.mult, ALU.mult,
                                   reads=[S_["bkb"], S_["bRW"]], writes=[bkw] if h == 0 else (), pwrites=() if h == 0 else [bkw])
                            mm_group([(pb[regs[h][0]][:, regs[h][1]:regs[h][1] + 129], kwt[:, h * 128:(h + 1) * 128],
                                       S_["vp"][:, h, :], True, True) for h in range(3)], [bkw, S_["bvp"]], writes=[BP[regs[0][0]]])
                            mm_group([(pb[6][:, regs[3][1]:regs[3][1] + 129], kwt[:, 384:512], S_["vp"][:, 3, :], True, True)],
                                     [bkw, S_["bvp"]], writes=[BP[6]])
                        return
                    if store_slots is not None:
                        state_update(d, si)
                        ss = store_slots[i]
                        P.dma("cds", [lambda e: e.dma_start(out=cdb_s[ss], in_=Cd[:].rearrange("p h c -> p (h c)"))],
                              reads=[B("Cd")], writes=[B(f"cdb_s{ss}")])
                    for h in range(4):
                        ts("dve", kw[:, h * 128:(h + 1) * 128], S_["kb"][:, h * 128:(h + 1) * 128],
                           S_["RW"][:, 8 + d * 4 + h:9 + d * 4 + h], None, ALU.mult, reads=[S_["bkb"], S_["bRW"]],
                           writes=[B("kw")] if h == 0 else (), pwrites=() if h == 0 else [B("kw")])
                    mm_group([(pb[REGS[h][0]][:, REGS[h][1]:REGS[h][1] + 129], kw[:, h * 128:(h + 1) * 128], S_["vp"][:, h, :],
                               True, True) for h in range(3)], [B("kw"), S_["bvp"]], writes=[BP[7]])
                    mm_group([(pb[6][:, 258:387], kw[:, 384:512], S_["vp"][:, 3, :], True, True)], [B("kw"), S_["bvp"]],
                             writes=[BP[6]])

                def s2_b(i):
                    S_ = slot(i % 2)
                    if ctx_flags is not None:
                        for dd, stt_, bst_, regs in ((0, fstate, B("hres2"), REGS), (1, Cst, B("Cst"), REGS_B)):
                            for h in range(4):
                                bk, c0 = regs[h]
                                stt("dve", stt_[:, h, :], stt_[:, h, :], S_["Dd"][:, dd * 4 + h:dd * 4 + h + 1],
                                    pb[bk][:, c0:c0 + 129], ALU.mult, ALU.add, reads=[bst_, S_["bDd"], BP[bk]],
                                    writes=[bst_] if h == 0 else (), pwrites=() if h == 0 else [bst_])
                        return
                    for h in range(4):
                        bk, c0 = REGS[h]
                        stt("dve", Cst[:, h, :], Cst[:, h, :], S_["Dd"][:, d * 4 + h:d * 4 + h + 1], pb[bk][:, c0:c0 + 129],
                            ALU.mult, ALU.add, reads=[B("Cst"), S_["bDd"], BP[bk]], writes=[B("Cst")] if h == 0 else (),
                            pwrites=() if h == 0 else [B("Cst")])

                s0_act(0)
                s0_pe(0)
                if n > 1:
                    s0_act(1)
                    s0_pe(1)
                s1_pe(0)
                s1_ev(0)
                s1_g1(0)
                s1_g2(0)
                for idx in range(n):
                    if idx + 1 < n:
                        s1_pe(idx + 1)
                    s2_a(idx)
                    if idx + 2 < n:
                        s0_act(idx + 2)
                    if idx + 1 < n:
                        s1_ev(idx + 1)
                        s1_g1(idx + 1)
                    s2_b(idx)
                    if idx + 1 < n:
                        s1_g2(idx + 1)
                    if idx + 2 < n:
                        s0_pe(idx + 2)
                    if state.get("bg") is not None:
                        if next(state["bg"], "done") == "done":
                            state["bg"] = None

            def rope_apply(src_psum, bsrc, nh, dst_view, bdst, rope_t, brope, first_write):
                sv = src_psum.rearrange("p (h c) -> p h c", c=64)
                cos = rope_t[:, 0:8].unsqueeze(1).to_broadcast([128, nh, 8])
                sin = rope_t[:, 8:16].unsqueeze(1).to_broadcast([128, nh, 8])
                x1, x2 = sv[:, :, 0:8], sv[:, :, 8:16]
                ta, tb, tc, td = (rt[:, i, 0:nh, :] for i in range(4))
                tt("dve", ta, x1, cos, ALU.mult, reads=[bsrc, brope], writes=[B("rt")])
                tt("dve", tb, x2, sin, ALU.mult, reads=[bsrc, brope], pwrites=[B("rt")])
                tt("dve", tc, x2, cos, ALU.mult, reads=[bsrc, brope], pwrites=[B("rt")])
                tt("dve", td, x1, sin, ALU.mult, reads=[bsrc, brope], pwrites=[B("rt")])
                w0 = [bdst] if first_write else ()
                p0 = () if first_write else [bdst]
                tt("dve", dst_view(0, 8), ta, tb, ALU.subtract, reads=[B("rt")], writes=w0, pwrites=p0)
                tt("dve", dst_view(8, 16), tc, td, ALU.add, reads=[B("rt")], pwrites=[bdst])
                cp("act", dst_view(16, 64), sv[:, :, 16:64], reads=[bsrc], pwrites=[bdst])

            def s1_kv(job, c):
                xi = load_x(job["x"], c * 128)
                r = c % 3
                load(f"rl{r}", ropes[r][:], c_rope[job["tab0"] + c], B(f"rope{r}"))
                j = c % 2
                xt, bx = xs[xi][:], B(f"x{xi}")
                act(xn[:], xt, AF.Square, [bx], [B("xn"), B("sm0")], accum_out=sm[:, 0:1])
                act(sm[:, 1:2], sm[:, 0:1], AF.Ln, [B("sm0")], [B("sm1")], scale=1.0 / D, bias=EPS)
                act(sm[:, 2:3], sm[:, 1:2], AF.Exp, [B("sm1")], [B("sm2")], scale=-0.5)
                ts("pool", xn[:], xt, sm[:, 2:3], 1.0, ALU.mult, ALU.mult, reads=[bx, B("sm2")], writes=[B("xn")])
                yield
                transposes([(pbb[3][:, k * 128:(k + 1) * 128], xn[:, k * 128:(k + 1) * 128]) for k in range(KC)],
                           [B("xn")], writes=[BP[3]])
                cp("dve", xnT[j][:].rearrange("p k t -> p (k t)"), pbb[3][:, 0:D], reads=[BP[3]], writes=[B(f"xnT{j}")])
                yield
                proj(xnT[j], B(f"xnT{j}"), AK, 512, 1)
                s_ = c % 3
                rope_apply(pb[1][:, 0:256], BP[1], 4, lambda a, b_: akd[:, :, a:b_], B("akd"), ropes[r], B(f"rope{r}"), True)
                cp("act", avp[s_][:, :, 0:64], pb[1][:, 256:512].rearrange("p (h c) -> p h c", h=4), reads=[BP[1]],
                   pwrites=[B(f"avp{s_}")])
                yield
                transposes([(pbb[3][0:64, k * 128:(k + 1) * 128], akd[:, k, :]) for k in range(4)],
                           [B("akd")], writes=[BP[3]])
                cp("dve", akT[s_][0:64, :, :].rearrange("p k t -> p (k t)"), pbb[3][0:64, 0:512], reads=[BP[3]], writes=[B(f"akT{s_}")])
                yield

            def s1_rest(job, c):
                j = c % 2
                r = c % 3
                bxT = B(f"xnT{j}")
                proj(xnT[j], bxT, GI, 16, 2)
                proj(xnT[j], bxT, MQ, 512, 0)
                proj(xnT[j], bxT, MK, 512, 1)
                tt("dve", G[:], pb[2][:, 0:16], gb[:], ALU.add, reads=[BP[2], B("gb")], writes=[B("G")])
                gates(0, part="a")
                cp("dve", qb[:], pb[0][:, :], reads=[BP[0]], writes=[B("hn")])
                cp("dve", kb[:], pb[1][:, :], reads=[BP[1]], writes=[B("kb")])
                yield
                proj(xnT[j], bxT, MV, 512, 2)
                proj(xnT[j], bxT, MO, 512, 0)
                proj(xnT[j], bxT, AQ, 512, 1)
                gates(0, part="b")
                gates(0, part="c")
                cp("act", vp[:, :, 0:128], pb[2][:, :].rearrange("p (h c) -> p h c", h=4), reads=[BP[2]], pwrites=[B("vp")])
                act(sgo[:], pb[0][:, :], AF.Sigmoid, [BP[0]], [B("sgo")])
                yield
                rope_apply(pb[1][:, :], BP[1], 8, lambda a, b_: aqr[:].rearrange("p (h c) -> p h c", c=64)[:, :, a:b_],
                           B("aqr"), ropes[r], B(f"rope{r}"), True)
                yield
                transposes([(pbb[2][:, k * 128:(k + 1) * 128], qb[:, k * 128:(k + 1) * 128]) for k in range(4)] +
                           [(pbb[2][:, 512 + k * 128:512 + (k + 1) * 128], kb[:, k * 128:(k + 1) * 128]) for k in range(4)],
                           [B("hn"), B("kb")], writes=[BP[2]])
                cp("dve", qT[:].rearrange("p k t -> p (k t)"), pbb[2][:, 0:512], reads=[BP[2]], writes=[B("qT")])
                cp("dve", kT[:].rearrange("p k t -> p (k t)"), pbb[2][:, 512:1024], reads=[BP[2]], writes=[B("kT")])
                yield
                transposes([(pbb[0][0:64, k * 128:(k + 1) * 128], aqr[:, k * 64:(k + 1) * 64]) for k in range(8)],
                           [B("aqr")], writes=[BP[0]])
                cp("dve", aqT[0:64, :, :].rearrange("p k t -> p (k t)"), pbb[0][0:64, 0:1024], reads=[BP[0]], writes=[B("aqT")])
                yield

            def interleave(*gens):
                gens = [g for g in gens if g is not None]
                while gens:
                    for g in list(gens):
                        try:
                            next(g)
                        except StopIteration:
                            gens.remove(g)

            ACC = [(4 + (i // 3), (i % 3) * 129) for i in range(8)]

            def s2_mlstm(job, c, first, cdb_slot):
                mm_group([(pb[3][:, h * 128:(h + 1) * 128], kT[:, h, :], qT[:, h, :], True, True) for h in range(4)],
                         [B("kT"), B("qT")], writes=[BP[3]])
                for h in range(4):
                    for d in range(2):
                        i = h * 2 + d
                        stt("dve", PT[:, i, :], pb[3][:, h * 128:(h + 1) * 128], RW[:, 8 + d * 4 + h:9 + d * 4 + h],
                            U if d == 0 else UT, ALU.mult, ALU.mult, reads=[BP[3], B("RW"), B("tri")],
                            writes=[B("PT")] if i == 0 else (), pwrites=() if i == 0 else [B("PT")])
                if first:
                    P.op("pool", lambda e: e.memset(Cd[:], 0.0), writes=[B("Cd")])
                else:
                    state_update(0)
                bcd = B(f"Cdb{cdb_slot}") if cdb_slot is not None else None
                for bank in (4, 5, 6):
                    mms = []
                    rd = [B("PT"), B("vp"), B("qT"), B("Cd")] + ([bcd] if bcd is not None else [])
                    for i in range(8):
                        if ACC[i][0] != bank:
                            continue
                        h, d = i // 2, i % 2
                        o = pb[bank][:, ACC[i][1]:ACC[i][1] + 129]
                        inter = Cd[:, h, :] if d == 0 else (Cdb[cdb_slot][:, h, :] if cdb_slot is not None else None)
                        if inter is None:
                            mms.append((o, PT[:, i, :], vp[:, h, :], True, True))
                        else:
                            mms.append((o, PT[:, i, :], vp[:, h, :], True, False))
                            mms.append((o, qT[:, h, :], inter, False, True))
                    mm_group(mms, rd, writes=[BP[bank]])
                cloc_and_update(0)
                yield
                for bank in (4, 5, 6):
                    n = 3 if bank < 6 else 2
                    i0 = (bank - 4) * 3
                    act(sm[:, 8 + i0:8 + i0 + n], pb[bank][:, 128:128 + 129 * (n - 1) + 1:129], AF.Abs, [BP[bank]],
                        writes=[B("sm8")] if i0 == 0 else (), pwrites=() if i0 == 0 else [B("sm8")])
                rinv = RW[:, 0:8].rearrange("p (d h) -> p h d", d=2)
                tt("dve", sm[:, 16:24].rearrange("p (h d) -> p h d", d=2), sm[:, 8:16].rearrange("p (h d) -> p h d", d=2),
                   rinv, ALU.max, reads=[B("sm8"), B("RW")], writes=[B("sm16")])
                P.op("dve", lambda e: e.reciprocal(out=sm[:, 24:32], in_=sm[:, 16:24]), reads=[B("sm16")], writes=[B("sm24")])
                yield
                for h in range(4):
                    bf_, cf_ = ACC[2 * h]
                    bb_, cb_ = ACC[2 * h + 1]
                    act(hh[:, h * 128:(h + 1) * 128], pb[bf_][:, cf_:cf_ + 128], AF.Copy, [BP[bf_], B("sm24")],
                        writes=[B("hh")] if h == 0 else (), pwrites=() if h == 0 else [B("hh")], scale=sm[:, 24 + 2 * h:25 + 2 * h])
                    if h % 2 == 1:
                        yield
                for h in range(4):
                    bb_, cb_ = ACC[2 * h + 1]
                    stt("dve", hh[:, h * 128:(h + 1) * 128], pb[bb_][:, cb_:cb_ + 128], sm[:, 25 + 2 * h:26 + 2 * h],
                        hh[:, h * 128:(h + 1) * 128], ALU.mult, ALU.add, reads=[BP[bb_], B("sm24"), B("hh")],
                        writes=[B("hh")] if h == 0 else (), pwrites=() if h == 0 else [B("hh")])
                    if h % 2 == 1:
                        yield
                for h in range(4):
                    P.op("dve", lambda e, h=h: e.bn_stats(out=bst[:, h, :], in_=hh[:, h * 128:(h + 1) * 128]), reads=[B("hh")],
                         writes=[B("bst")] if h == 0 else (), pwrites=() if h == 0 else [B("bst")])
                    if h % 2 == 1:
                        yield
                for h in range(4):
                    P.op("dve", lambda e, h=h: e.bn_aggr(out=mv[:, h, :], in_=bst[:, h, :]), reads=[B("bst")],
                         writes=[B("mv")] if h == 0 else (), pwrites=() if h == 0 else [B("mv")])
                act(sm[:, 32:36], mv[:, :, 1], AF.Ln, [B("mv")], [B("sm32")], bias=EPS)
                act(sm[:, 36:40], sm[:, 32:36], AF.Exp, [B("sm32")], [B("sm36")], scale=-0.5)
                yield
                for h in range(4):
                    stt("dve", hh[:, h * 128:(h + 1) * 128], hh[:, h * 128:(h + 1) * 128], mv[:, h, 0:1],
                        sgo[:, h * 128:(h + 1) * 128], ALU.subtract, ALU.mult, reads=[B("hh"), B("mv"), B("sgo")],
                        writes=[B("hh")] if h == 0 else (), pwrites=() if h == 0 else [B("hh")])
                    if h % 2 == 1:
                        yield
                for h in range(4):
                    act(hn[:, h * 128:(h + 1) * 128], hh[:, h * 128:(h + 1) * 128], AF.Copy, [B("hh"), B("sm36")],
                        writes=[B("hn")] if h == 0 else (), pwrites=() if h == 0 else [B("hn")], scale=sm[:, 36 + h:37 + h])
                yield

            def s2_mlstm_T():
                transposes([(pbb[3][:, k * 128:(k + 1) * 128], hn[:, k * 128:(k + 1) * 128]) for k in range(4)],
                           [B("hn")], writes=[BP[3]])
                cp("dve", hT[:].rearrange("p k t -> p (k t)"), pbb[3][:, 0:512], reads=[BP[3]], writes=[B("hT")])

            def s2_attn(job, c, blocks):
                nb = len(blocks)
                steps = [(j, bi) for j in range(4) for bi in range(nb)]

                def st(k):
                    j, bi = steps[k]
                    which, cc, kcol = blocks[bi]
                    s_ = cc % 3
                    reg = k % 4
                    bank, lo = reg % 2, (reg // 2) * 256
                    o = pb[bank][:, lo:lo + 256]
                    mms = []
                    if which != "cur":
                        mb = mbias[:, 0:256] if which == "prev" else mbias[:, 256:512]
                        mms.append((o, ident[:], mb, True, False))
                    nm = which == "cur"
                    mms.append((o, akT[s_][0:64, j, :], aqT[0:64, 2 * j:2 * j + 2, :].rearrange("p h t -> p (h t)"), nm, True))
                    mm_group(mms, [B(f"akT{s_}"), B("aqT"), B("mbias"), B("ident")], pwrites=[BP[bank]])

                def ex_pv(k):
                    j, bi = steps[k]
                    which, cc, kcol = blocks[bi]
                    s_ = cc % 3
                    reg = k % 4
                    bank, lo = reg % 2, (reg // 2) * 256
                    o = pb[bank][:, lo:lo + 256]
                    pt = PTa[reg]
                    if kcol is None:
                        act(pt[:], o, AF.Exp, [BP[bank]], [B(f"PTa{reg}")])
                    else:
                        act(pt[:], o, AF.Exp, [BP[bank], B("flags")], [B(f"PTa{reg}")], bias=flags[:, kcol:kcol + 1])
                    for hp in range(2):
                        hd = 2 * j + hp
                        ob, oc = (2, hd * 65) if hd < 4 else (3, (hd - 4) * 65)
                        first_in_bank = (bi == 0 and hp == 0 and j in (0, 2))
                        mm_group([(pb[ob][:, oc:oc + 65], pt[:, hp * 128:(hp + 1) * 128], avp[s_][:, j, :], first_in_bank, False, True)],
                                 [B(f"PTa{reg}"), B(f"avp{s_}")], pwrites=[BP[ob]])

                st(0)
                for k in range(len(steps)):
                    if k + 1 < len(steps):
                        st(k + 1)
                    ex_pv(k)
                    if k % 2 == 1:
                        yield
                for half in range(2):
                    ob = 2 + half
                    den = pb[ob][:, 0:260].rearrange("p (h c) -> p h c", c=65)[:, :, 64]
                    tt("dve", sm[:, 40 + 4 * half:44 + 4 * half], den, esink[:, 4 * half:4 * half + 4], ALU.add,
                       reads=[BP[ob], B("esink")], writes=[B(f"sm40{half}")])
                    P.op("dve", lambda e, half=half: e.reciprocal(out=sm[:, 48 + 4 * half:52 + 4 * half],
                                                                   in_=sm[:, 40 + 4 * half:44 + 4 * half]),
                         reads=[B(f"sm40{half}")], writes=[B(f"sm48{half}")])
                    for hq in range(4):
                        hd = half * 4 + hq
                        act(oa[:, hd * 64:(hd + 1) * 64], pb[ob][:, hq * 65:hq * 65 + 64], AF.Copy, [BP[ob], B(f"sm48{half}")],
                            writes=[B("kw")] if hd == 0 else (), pwrites=() if hd == 0 else [B("kw")],
                            scale=sm[:, 48 + hd:49 + hd])
                yield

            def s2_attn_T():
                transposes([(pbb[0][:, k * 128:(k + 1) * 128], oa[:, k * 128:(k + 1) * 128]) for k in range(4)],
                           [B("kw")], writes=[BP[0]])
                cp("dve", oaT[:].rearrange("p k t -> p (k t)"), pbb[0][:, 0:512], reads=[BP[0]], writes=[B("oaT")])

            def s2_merge_gates(c, half):
                j = c % 2
                bxT = B(f"xnT{j}")
                proj(xnT[j], bxT, GM + half * 512, 512, 4)
                act(sgm[:], pb[4][:, :], AF.Sigmoid, [BP[4]], [B("sgm")])
                proj(xnT[j], bxT, GA + half * 512, 512, 5)
                act(sga[:], pb[5][:, :], AF.Sigmoid, [BP[5]], [B("sga")])

            def s2_merge(job, c, hslot):
                j = c % 2
                bxT = B(f"xnT{j}")
                bh = B(f"hres{hslot}")
                for half in range(2):
                    hs = slice(half * 512, (half + 1) * 512)
                    if half == 1:
                        s2_merge_gates(c, 1)
                    mm_group([(pb[6][:, :], hT[:, k, :], wpm[:, k, hs], k == 0, k == 3) for k in range(4)],
                             [B("hT"), B("wpm")], writes=[BP[6]])
                    mm_group([(pb[7][:, :], oaT[:, k, :], wpa[:, k, hs], k == 0, k == 3) for k in range(4)],
                             [B("oaT"), B("wpa")], writes=[BP[7]])
                    tt("dve", hh[:], pb[6][:, :], sgm[:], ALU.mult, reads=[BP[6], B("sgm")], writes=[B("hh")])
                    tt("dve", sgo[:], pb[7][:, :], sga[:], ALU.mult, reads=[BP[7], B("sga")], writes=[B("sgo")])
                    tt("dve", ybf[:, hs], hh[:], sgo[:], ALU.add, reads=[B("hh"), B("sgo")],
                       writes=[B("xn2")] if half == 0 else (), pwrites=() if half == 0 else [B("xn2")])
                transposes([(pbb[0][:, k * 128:(k + 1) * 128], ybf[:, k * 128:(k + 1) * 128]) for k in range(KC)],
                           [B("xn2")], writes=[BP[0]])
                cp("act", xn[:], pbb[0][:, 0:D], reads=[BP[0]], writes=[B("xn")])
                for half in range(2):
                    hs = slice(half * 512, (half + 1) * 512)
                    bank = 1 + half
                    mm_group([(pb[bank][:, :], yT[:, k, :], wout[:, k, hs], k == 0, k == KC - 1) for k in range(KC)],
                             [B("xn"), B("wout")], writes=[BP[bank]])
                    tt("dve", hres[hslot][:, hs], hres[hslot][:, hs], pb[bank][:, :], ALU.add, reads=[bh, BP[bank]],
                       writes=[bh] if half == 0 else (), pwrites=() if half == 0 else [bh])

            def s2_norm2(job, c, hslot, targets):
                bh = B(f"hres{hslot}")
                act(xn2[:], hres[hslot][:], AF.Square, [bh], [B("xn2"), B("sm4")], accum_out=sm[:, 4:5])
                act(sm[:, 5:6], sm[:, 4:5], AF.Ln, [B("sm4")], [B("sm5")], scale=1.0 / D, bias=EPS)
                act(sm[:, 6:7], sm[:, 5:6], AF.Exp, [B("sm5")], [B("sm6")], scale=-0.5)
                yield
                ts("pool", xn2[:], hres[hslot][:], sm[:, 6:7], 1.0, ALU.mult, ALU.mult, reads=[bh, B("sm6")], writes=[B("xn2")])
                yield
                transposes([(pbb[3][:, k * 128:(k + 1) * 128], xn2[:, k * 128:(k + 1) * 128]) for k in range(KC)],
                           [B("xn2")], writes=[BP[3]])
                src = pbb[3][:, 0:D].rearrange("p (k t) -> p k t", t=128)
                for (us, d0, s0, n, fcol) in targets:
                    bu = B(f"unit{us}")
                    if fcol is None:
                        cp("dve", unit[us][:, :, d0:d0 + n], src[:, :, s0:s0 + n], reads=[BP[3]], pwrites=[bu])
                    else:
                        ts("dve", unit[us][:, :, d0:d0 + n], src[:, :, s0:s0 + n], flags[:, fcol:fcol + 1], None, ALU.mult,
                           reads=[BP[3], B("flags")], pwrites=[bu])
                yield

            def s3_ffn(job, us, hslots, out_rows):
                bu = B(f"unit{us}")

                def load_wu(j):
                    r = j % NR
                    P.dma(f"fwu{r}", [lambda e: e.dma_start(out=wupr[r][:], in_=wup_s[j])],
                          reads=[B("wup_s")], writes=[B(f"fwu{r}")])

                def load_wd(j):
                    r = j % NR
                    P.dma(f"fwd{r}", [lambda e: e.dma_start(out=wdnr[r][:], in_=wdn_s[j])],
                          reads=[B("wdn_s")], writes=[B(f"fwd{r}")])

                def up(j):
                    r = j % NR
                    for ab in range(2):
                        bank = ab + 2 * (j % 2)
                        mm_group([(pb[bank][:, 0:258], wupr[r][:, k, ab, :], unit[us][:, k, :], k == 0, k == KC - 1)
                                  for k in range(KC)], [B(f"fwu{r}"), bu], writes=[BP[bank]])

                def conv(j):
                    cslot = j % 2
                    bc = B(f"cab{cslot}")
                    for ab in range(2):
                        bank = ab + 2 * (j % 2)
                        fidx = ab * NFC + j
                        cv = cab[cslot][:, ab, :]
                        act(cv, pb[bank][:, 1:257], AF.Identity, [BP[bank], B("cw"), B("cb")],
                            writes=[bc] if ab == 0 else (), pwrites=() if ab == 0 else [bc],
                            scale=cw[:, 1, fidx:fidx + 1], bias=cb[:, fidx:fidx + 1])
                        stt("dve", cv, pb[bank][:, 0:256], cw[:, 0, fidx:fidx + 1], cv, ALU.mult, ALU.add,
                            reads=[BP[bank], B("cw"), bc], pwrites=[bc])
                        stt("dve", cv, pb[bank][:, 2:258], cw[:, 2, fidx:fidx + 1], cv, ALU.mult, ALU.add,
                            reads=[BP[bank], B("cw"), bc], pwrites=[bc])
                    act(cab[cslot][:, 0, :], cab[cslot][:, 0, :], AF.Silu, [bc], [bc])
                    gs = j % 2
                    tt("pool", gt[gs][:], cab[cslot][:, 0, :], cab[cslot][:, 1, :], ALU.mult, reads=[bc], writes=[B(f"gt{gs}")])

                def down(j):
                    r = j % NR
                    gs = j % 2
                    for ti in range(2):
                        for half in range(2):
                            bank = 4 + ti * 2 + half
                            mm_group([(pb[bank][:, :], gt[gs][:, ti * 128:(ti + 1) * 128], wdnr[r][:, half * 512:(half + 1) * 512],
                                       j == 0, j == NFC - 1)], [B(f"gt{gs}"), B(f"fwd{r}")], pwrites=[BP[bank]])

                if not state.get("ffn_pre"):
                    for j in range(min(NR, NFC)):
                        load_wu(j)
                        load_wd(j)
                    state["ffn_pre"] = True
                up(0)
                for j in range(NFC):
                    if j + 1 < NFC:
                        up(j + 1)
                    if j + NR < NFC:
                        load_wu(j + NR)
                    conv(j)
                    down(j)
                    if j + NR < NFC:
                        load_wd(j + NR)
                for j in range(min(NR, NFC)):
                    load_wu(j)
                    load_wd(j)
                ckpt(12)
                for ti in range(2):
                    hs_ = hslots[ti]
                    bh = B(f"hres{hs_}")
                    for half in range(2):
                        bank = 4 + ti * 2 + half
                        hsl = slice(half * 512, (half + 1) * 512)
                        tt("dve", hres[hs_][:, hsl], hres[hs_][:, hsl], pb[bank][:, :], ALU.add, reads=[bh, BP[bank]],
                           writes=[bh] if half == 0 else (), pwrites=() if half == 0 else [bh])
                for ti in range(2):
                    hs_ = hslots[ti]
                    bh = B(f"hres{hs_}")
                    c0_ = 56 + 3 * ti
                    act(xn2[:], hres[hs_][:], AF.Square, [bh], [B("xn2"), B(f"smf{ti}a")], accum_out=sm[:, c0_:c0_ + 1])
                    act(sm[:, c0_ + 1:c0_ + 2], sm[:, c0_:c0_ + 1], AF.Ln, [B(f"smf{ti}a")], [B(f"smf{ti}b")], scale=1.0 / D, bias=EPS)
                    act(sm[:, c0_ + 2:c0_ + 3], sm[:, c0_ + 1:c0_ + 2], AF.Exp, [B(f"smf{ti}b")], [B(f"smf{ti}c")], scale=-0.5)
                    stt("dve", hres[hs_][:], hres[hs_][:], sm[:, c0_ + 2:c0_ + 3], nfw[:], ALU.mult, ALU.mult,
                        reads=[bh, B(f"smf{ti}c"), B("nfw")], writes=[bh])
                    dst, r0 = out_rows[ti]
                    P.dma(f"st{hs_}", [lambda e, dst=dst, r0=r0, hs_=hs_: e.dma_start(out=dst[r0:r0 + 128, :], in_=hres[hs_][:])],
                          reads=[bh], queue="pool")

            def passB(job):
                n = job["n"]
                lo2, hi2 = job["s2"]
                lo3, hi3 = job["s3"]
                dyn = job["dyn"]
                hctr = [0]
                hslot_of = {}
                unit_of = {}

                def unit_slot(u):
                    return u % NU

                interleave(s1_kv(job, 0))
                ckpt(5)
                pend = {"n": None, "f": None}

                def run_ffn():
                    if pend["f"] is not None:
                        u = pend["f"]
                        pend["f"] = None
                        ca_, cb_ = lo3 + 2 * u, lo3 + 2 * u + 1
                        s3_ffn(job, unit_slot(u), [hslot_of[ca_], hslot_of[cb_]],
                               [(job["y"], (ca_ - lo3) * 128), (job["y"], (cb_ - lo3) * 128)])
                        ckpt(11)

                for c in range(n):
                    gD = s1_kv(job, c + 1) if c + 1 < n else None
                    gN, pend["n"] = pend["n"], None
                    if not (lo2 <= c < hi2):
                        interleave(gD, gN)
                        run_ffn()
                        continue
                    interleave(s1_rest(job, c), gD, gN)
                    ckpt(6)
                    if job["has_cdb"](c):
                        cs = 0
                        load(f"cl{cs}", Cdb[cs][:].rearrange("p h c -> p (h c)"), cdb_s[c], B(f"Cdb{cs}"),
                             reads=[B(f"cdb_s{c}")])
                    else:
                        cs = None
                    run_ffn()
                    hs_ = hctr[0] % NH
                    hctr[0] += 1
                    hslot_of[c] = hs_
                    load(f"hl{hs_}", hres[hs_][:], job["x"][c * 128:(c + 1) * 128, :], B(f"hres{hs_}"))
                    blocks = []
                    if c - 1 >= 0:
                        blocks.append(("prev", c - 1, job["kcol"](c - 1)))
                    blocks.append(("cur", c, job["kcol"](c)))
                    if c + 1 < n:
                        blocks.append(("next", c + 1, job["kcol"](c + 1)))
                    interleave(s2_mlstm(job, c, job["fwd_zero"] and c == lo2, cs), s2_attn(job, c, blocks))
                    s2_merge_gates(c, 0)
                    s2_mlstm_T()
                    s2_attn_T()
                    ckpt(8)
                    s2_merge(job, c, hs_)
                    ckpt(9)
                    if DBG_STOP == (job["tab0"], c):
                        raise _Stop()
                    targets = []
                    if lo3 <= c < hi3:
                        u = (c - lo3) // 2
                        pos = (c - lo3) % 2
                        targets.append((unit_slot(u), 1 + pos * 128, 0, 128, None))
                        if pos == 0 and c - 1 >= lo3:
                            targets.append((unit_slot(u - 1), 257, 0, 1, None))
                        if pos == 1 and c + 1 < hi3:
                            targets.append((unit_slot(u + 1), 0, 127, 1, None))
                    elif c == lo3 - 1:
                        targets.append((unit_slot(0), 0, 127, 1, 2 if dyn else None))
                    elif c == hi3:
                        targets.append((unit_slot((hi3 - lo3) // 2 - 1), 257, 0, 1, 3 if dyn else None))
                    if targets:
                        pend["n"] = s2_norm2(job, c, hs_, targets)
                    if lo3 <= c < hi3 and not dyn:
                        if c == lo3:
                            P.op("dve", lambda e, us=unit_slot(0): e.memset(unit[us][:, :, 0:1], 0.0), pwrites=[B(f"unit{unit_slot(0)}")])
                        if c == hi3 - 1:
                            ul = unit_slot((hi3 - lo3) // 2 - 1)
                            P.op("dve", lambda e, us=ul: e.memset(unit[us][:, :, 257:258], 0.0), pwrites=[B(f"unit{ul}")])
                    if c >= lo3 and c - 1 >= lo3 and (c - 1 - lo3) % 2 == 1 and c - 1 < hi3:
                        pend["f"] = (c - 1 - lo3) // 2
                interleave(pend["n"])
                pend["n"] = None
                run_ffn()
                if hi3 == n:
                    pend["f"] = (hi3 - lo3) // 2 - 1
                    run_ffn()

            zero_state()
            fsave = hres[2][:, 0:516].rearrange("p (h c) -> p h c", c=129)
            P.op("pool", lambda e: e.memset(hres[2][:, 0:516], 0.0), writes=[B("hres2")])
            scan(x_ctx, [i * 128 for i in range(LCTX)], 0, ctx_flags=c_ctxf)
            ckpt(2)
            rows = [c * 128 for c in range(NPB - 2, 0, -1)]
            scan(x_own, rows, 1, store_slots=list(range(NPB - 2, 0, -1)))
            cp("pool", Cst[:], fsave, reads=[B("hres2")], writes=[B("Cst")])
            ckpt(3)
            if state.get("bg") is not None:
                for _ in state["bg"]:
                    pass
                state["bg"] = None
            pj = dict(x=x_own, n=NPB, s2=(1, NPB - 1), s3=(2, NPB - 2), dyn=True, tab0=0, y=y_own, fwd_zero=False,
                      has_cdb=lambda c: True,
                      kcol=lambda c: 0 if c < 2 else (1 if c >= NPB - 2 else None))
            passB(pj)
            ckpt(4)
            for s in range(NS):
                xs_ = x_s[s * SC * 128:(s + 1) * SC * 128, :]
                ys_ = y_s[s * SC * 128:(s + 1) * SC * 128, :]
                zero_state()
                scan(xs_, [c * 128 for c in range(SC - 1, -1, -1)], 1, store_slots=list(range(SC - 1, -1, -1)))
                sj = dict(x=xs_, n=SC, s2=(0, SC), s3=(0, SC), dyn=False, tab0=NPB, y=ys_, fwd_zero=True,
                          has_cdb=lambda c: True, kcol=lambda c: None)
                zero_state()
                passB(sj)

        except _Stop:
            pass
        P.finish()
        with nc.Block() as block:
            P.replay(block)
    return nc


def _consts():
    s = np.arange(128)[:, None]
    t = np.arange(128)[None, :]
    U = (s <= t).astype(np.float32)
    UT = (s >= t).astype(np.float32)
    tri = np.concatenate([U, UT, np.ones((128, 128), np.float32), np.eye(128, dtype=np.float32)], axis=1)
    mprev = np.where(s >= t, 0.0, NEG).astype(np.float32)
    mnext = np.where(s <= t, 0.0, NEG).astype(np.float32)
    mb = np.concatenate([mprev, mprev, mnext, mnext], axis=1).astype(ml_dtypes.bfloat16)
    ident = np.eye(128, dtype=np.float32).astype(ml_dtypes.bfloat16)
    return ident, tri, mb


def _rope_rows(pos):
    inv = (np.float32(ROPE_THETA) ** (-np.arange(8, dtype=np.float32) / np.float32(8))).astype(np.float32)
    ang = (pos.astype(np.float32)[:, None] * inv[None, :]).astype(np.float32)
    return np.concatenate([np.cos(ang.astype(np.float64)), np.sin(ang.astype(np.float64))], axis=1).astype(np.float32)


_PROGS = {}


def run_layer(inputs, PC, SC, NS):
    xp = np.ascontiguousarray(inputs["x_prompt"][0], dtype=np.float32)
    xsamp = np.ascontiguousarray(inputs["x_sample"], dtype=np.float32)
    NPB, LCTX = PC + 4, 7 * PC - 1
    nchunks = xp.shape[0] // 128
    assert nchunks == NCORES * PC and xsamp.shape[0] == NCORES * NS and xsamp.shape[1] == SC * 128

    def take(g0, n):
        out = np.zeros((n * 128, D), np.float32)
        lo, hi = max(g0, 0), min(g0 + n, nchunks)
        if hi > lo:
            out[(lo - g0) * 128:(hi - g0) * 128] = xp[lo * 128:hi * 128]
        return out

    ident, tri, mb = _consts()
    f32 = lambda a: np.ascontiguousarray(a, dtype=np.float32)
    shared = dict(
        w_in=f32(inputs["w_in"][0]), w_pm=f32(inputs["w_proj_m"][0]), w_pa=f32(inputs["w_proj_a"][0]),
        w_out=f32(inputs["w_out"][0]), w_up=f32(inputs["w_up"][0]), w_dn=f32(inputs["w_down"][0]),
        norm1=f32(inputs["norm1_w"][0]), norm2=f32(inputs["norm2_w"][0]), normf=f32(inputs["norm_f_w"]),
        mhw=f32(inputs["mh_norm_w"][0]),
        gbias=f32(np.concatenate([inputs["i_bias"][0].reshape(-1), inputs["f_bias"][0].reshape(-1)])),
        sink=f32(inputs["attn_sink"][0]), convw=f32(inputs["conv_w"][0]), convb=f32(inputs["conv_b"][0]),
        c_ident=ident, c_tri=tri, c_mb=mb)
    p = np.arange(128)
    in_maps = []
    for i in range(NCORES):
        g0 = i * PC - 2
        rope = np.stack([_rope_rows((g0 + l) * 128 + p) for l in range(NPB)] +
                        [_rope_rows(c * 128 + p) for c in range(SC)], axis=0)
        fl = np.zeros((128, 4), np.float32)
        fl[:, 0] = NEG if i == 0 else 0.0
        fl[:, 1] = NEG if i == NCORES - 1 else 0.0
        fl[:, 2] = 0.0 if i == 0 else 1.0
        fl[:, 3] = 0.0 if i == NCORES - 1 else 1.0
        m = dict(shared)
        nF, nB = max(i * PC - 1, 0), max((NCORES - 1 - i) * PC - 1, 0)
        npad = LCTX - nF - nB
        order = list(range(0, nF)) + list(range(nchunks - 1, nchunks - 1 - nB, -1))
        x_ctx = np.zeros((LCTX * 128, D), np.float32)
        for k, gch in enumerate(order):
            x_ctx[(npad + k) * 128:(npad + k + 1) * 128] = xp[gch * 128:(gch + 1) * 128]
        f0 = np.array([1.0] * (npad + nF) + [0.0] * nB, np.float32)
        cf = np.zeros((LCTX, 128, 16), np.float32)
        cf[:, :, 0:4] = f0[:, None, None]
        cf[:, :, 4:8] = 1.0 - f0[:, None, None]
        cf[:, :, 8:12] = f0[:, None, None]
        cf[:, :, 12:16] = 1.0
        m.update(x_own=take(g0, NPB), x_ctx=x_ctx, c_ctxf=cf,
                 x_s=np.ascontiguousarray(xsamp[i * NS:(i + 1) * NS].reshape(NS * SC * 128, D)),
                 c_rope=rope, c_flags=fl)
        in_maps.append(m)
    key = (PC, SC, NS)
    if key not in _PROGS:
        _PROGS[key] = build_program(PC, SC, NS)
    res = run_bass_kernel_spmd(_PROGS[key], in_maps, core_ids=list(range(NCORES)))
    y_p = np.concatenate([r["y_own"] for r in res.results], axis=0)[None]
    y_s = np.stack([r["y_s"].reshape(NS, SC * 128, D) for r in res.results], axis=0).reshape(NCORES * NS, SC * 128, D)
    return y_p.astype(np.float32), y_s.astype(np.float32)


def kernel(**inputs):
    return run_layer(inputs, PC=16, SC=32, NS=2)
```

```python
import contextlib
import numpy as np
import ml_dtypes
import concourse.bass as bass
import concourse.mybir as mybir
from concourse.bass_utils import run_bass_kernel_spmd

F32 = mybir.dt.float32
BF16 = mybir.dt.bfloat16
AF = mybir.ActivationFunctionType
ALU = mybir.AluOpType

NCORES = 8
D = 1024
KC = 8
DIN = 5136
DFF = 2816
NFC = 22
MQ, MK, MV, MO, GI, AQ, AK, AV, GM, GA = 0, 512, 1024, 1536, 2048, 2064, 2576, 2832, 3088, 4112
EPS = 1e-6
NEG = -30000.0
DBG_STOP = None
DBG_PHASE = None


class _Stop(Exception):
    pass
ROPE_THETA = 500000.0


class Buf:
    __slots__ = ("name", "writers", "readers", "psum")

    def __init__(self, name, psum=False):
        self.name = name
        self.writers = []
        self.readers = []
        self.psum = psum


class Prog:
    ENG = ("pe", "act", "dve", "pool", "sp")
    HANDLE = {"pe": "tensor", "act": "scalar", "dve": "vector", "pool": "gpsimd", "sp": "sync"}

    def __init__(self, nc, stack):
        self.nc = nc
        self.stack = stack
        self.streams = {e: [] for e in self.ENG}
        self.sems = {}
        self.count = {}
        self.known = {e: {} for e in self.ENG}
        for e in self.ENG:
            self._sem(e)

    def _sem(self, key):
        if key not in self.sems:
            self.sems[key] = self.stack.enter_context(self.nc.semaphore("s_" + key))
            self.count[key] = 0
        return self.sems[key]

    @staticmethod
    def _deps(reads, writes, pwrites, eng=None):
        deps = {}

        def add(kv):
            if deps.get(kv[0], 0) < kv[1]:
                deps[kv[0]] = kv[1]
        ps = [b for b in list(reads) + list(writes) + list(pwrites) if b.psum]
        for b in ps:
            for w in b.writers:
                if w[0] != eng:
                    add(w)
        reads = [b for b in reads if not b.psum]
        writes = [b for b in writes if not b.psum]
        pwrites = [b for b in pwrites if not b.psum]
        for b in reads:
            for w in b.writers:
                add(w)
        for b in writes:
            for w in b.writers:
                add(w)
            for r in b.readers:
                add(r)
        for b in pwrites:
            for r in b.readers:
                add(r)
            if b.readers:
                for w in b.writers:
                    add(w)
        return deps

    def _waits(self, eng, deps):
        kn = self.known[eng]
        for k, v in sorted(deps.items()):
            if k == eng and eng == "pe":
                continue
            if kn.get(k, 0) >= v:
                continue
            kn[k] = v
            self.streams[eng].append(("wait", k, v))

    @staticmethod
    def _update(me, reads, writes, pwrites):
        for b in list(reads) + list(writes) + list(pwrites):
            if b.psum:
                b.writers = [me]
        reads = [b for b in reads if not b.psum]
        writes = [b for b in writes if not b.psum]
        pwrites = [b for b in pwrites if not b.psum]
        for b in reads:
            b.readers.append(me)
        for b in writes:
            b.writers = [me]
            b.readers = []
        for b in pwrites:
            if b.readers:
                b.writers = [me]
                b.readers = []
            else:
                b.writers.append(me)

    def op(self, eng, fn, reads=(), writes=(), pwrites=()):
        self._waits(eng, self._deps(reads, writes, pwrites, eng))
        self.count[eng] += 1
        self.streams[eng].append(("op", fn, eng, 1))
        self._update((eng, self.count[eng]), reads, writes, pwrites)

    def dma(self, slot, fns, reads=(), writes=(), queue="sp"):
        self._sem(slot)
        self._waits(queue, self._deps(reads, writes, (), queue))
        for fn in fns:
            self.count[slot] += 16
            self.streams[queue].append(("op", fn, slot, 16))
        self._update((slot, self.count[slot]), reads, writes, ())

    def finish(self, queue="sp"):
        for k, v in self.count.items():
            if v > 0:
                self.streams[queue].append(("wait", k, v))

    def replay(self, block):
        for e in self.ENG:
            stream = self.streams[e]
            if not stream:
                continue

            def body(h, stream=stream):
                for item in stream:
                    if item[0] == "wait":
                        h.wait_ge(self.sems[item[1]], item[2])
                    else:
                        item[1](h).then_inc(self.sems[item[2]], item[3])
            getattr(block, self.HANDLE[e])(body)


def build_program(PC, SC, NS):
    NPB = PC + 4
    LCTX = 7 * PC - 1
    NTAB = NPB + SC
    nc = bass.Bass("TRN2", target_bir_lowering=False)

    def din(name, shape, dt=F32):
        return nc.dram_tensor(name, list(shape), dt, kind="ExternalInput").ap()

    x_own = din("x_own", [NPB * 128, D])
    x_ctx = din("x_ctx", [LCTX * 128, D])
    c_ctxf = din("c_ctxf", [LCTX, 128, 16])
    x_s = din("x_s", [NS * SC * 128, D])
    w_in = din("w_in", [D, DIN])
    w_pm = din("w_pm", [512, D])
    w_pa = din("w_pa", [512, D])
    w_out = din("w_out", [D, D])
    w_up = din("w_up", [D, 2 * DFF])
    w_dn = din("w_dn", [DFF, D])
    norm1 = din("norm1", [D])
    norm2 = din("norm2", [D])
    normf = din("normf", [D])
    mhw = din("mhw", [512])
    gbias = din("gbias", [16])
    sink = din("sink", [8])
    convw = din("convw", [3, 2 * DFF])
    convb = din("convb", [2 * DFF])
    c_ident = din("c_ident", [128, 128], BF16)
    c_tri = din("c_tri", [128, 512])
    c_mb = din("c_mb", [128, 512], BF16)
    c_rope = din("c_rope", [NTAB, 128, 16])
    c_flags = din("c_flags", [128, 4])
    y_own = nc.dram_tensor("y_own", [PC * 128, D], F32, kind="ExternalOutput").ap()
    y_s = nc.dram_tensor("y_s", [NS * SC * 128, D], F32, kind="ExternalOutput").ap()
    wup_s = nc.dram_tensor("wup_s", [NFC, 128, KC, 2, 128], BF16).ap()
    wdn_s = nc.dram_tensor("wdn_s", [NFC, 128, D], BF16).ap()
    NCD = max(NPB, SC)
    cdb_s = nc.dram_tensor("cdb_s", [NCD, 128, 516], BF16).ap()

    with contextlib.ExitStack() as st:
        P = Prog(nc, st)
        bufs = {}

        def B(name):
            if name not in bufs:
                bufs[name] = Buf(name)
            return bufs[name]

        def sb(name, shape, dt=F32):
            return st.enter_context(nc.sbuf_tensor(name, list(shape), dt))

        win = sb("win", [128, KC, DIN], BF16)
        wpm = sb("wpm", [128, 4, D], BF16)
        wpa = sb("wpa", [128, 4, D], BF16)
        wout = sb("wout", [128, KC, D], BF16)
        ident = sb("ident", [128, 128], BF16)
        tri = sb("tri", [128, 512])
        mbias = sb("mbias", [128, 512], BF16)
        flags = sb("flags", [128, 4])
        gb = sb("gb", [128, 16])
        esink = sb("esink", [128, 8])
        n1c = sb("n1c", [128, KC])
        n2c = sb("n2c", [128, KC])
        mhc = sb("mhc", [128, 4])
        cw = sb("cw", [128, 3, 2 * NFC])
        cb = sb("cb", [128, 2 * NFC])
        nfw = sb("nfw", [128, D])
        ropes = [sb(f"rope{i}", [128, 16]) for i in range(3)]
        flt = [sb(f"flt{i}", [128, 16]) for i in range(2)]
        xs = [sb(f"x{i}", [128, D]) for i in range(1)]
        xn = sb("xn", [128, D], BF16)
        xnT = [sb(f"xnT{i}", [128, KC, 128], BF16) for i in range(2)]
        sm = sb("sm", [128, 64])
        kb = sb("kb", [128, 512], BF16)
        kw = sb("kw", [128, 512], BF16)
        vp = sb("vp", [128, 4, 129], BF16)
        sgo = sb("sgo", [128, 512])
        rowst = sgo
        G = sb("G", [128, 16])
        E8 = sb("E8", [128, 8])
        LF = sb("LF", [128, 8])
        NB = sb("NB", [128, 16])
        RW = sb("RW", [128, 16])
        RW2 = sb("RW2", [128, 16])
        Dd = sb("Dd", [128, 8])
        Dd2 = sb("Dd2", [128, 8])
        qT = sb("qT", [128, 4, 128], BF16)
        kT = sb("kT", [128, 4, 128], BF16)
        aqr = sb("aqr", [128, 512], BF16)
        akd = sb("akd", [128, 4, 64], BF16)
        rt = sb("rt", [128, 4, 8, 8])
        aqT = sb("aqT", [128, 8, 128], BF16)
        akT = [sb(f"akT{i}", [128, 4, 128], BF16) for i in range(3)]
        avp = [sb(f"avp{i}", [128, 4, 65], BF16) for i in range(3)]
        PT = sb("PT", [128, 8, 128], BF16)
        Cst = sb("Cst", [128, 4, 129])
        Cd = sb("Cd", [128, 4, 129], BF16)
        Cdb = [sb(f"Cdb{i}", [128, 4, 129], BF16) for i in range(1)]
        hh = sb("hh", [128, 512])
        bst = sb("bst", [128, 4, 6])
        mv = sb("mv", [128, 4, 2])
        hn = sb("hn", [128, 512], BF16)
        oa = kw
        qb = hn
        hT = sb("hT", [128, 4, 128], BF16)
        PTa = [sb(f"PTa{i}", [128, 256], BF16) for i in range(4)]
        oaT = sb("oaT", [128, 4, 128], BF16)
        sgm = sb("sgm", [128, 512], BF16)
        sga = sb("sga", [128, 512], BF16)
        NH = 3
        hres = [sb(f"hres{i}", [128, D]) for i in range(NH)]
        xn2 = sb("xn2", [128, D], BF16)
        ybf = xn2
        yT = xn[:].rearrange("p (k t) -> p k t", t=128)
        NU = 2
        unit = [sb(f"unit{i}", [128, KC, 258], BF16) for i in range(NU)]
        NR = 3
        wupr = [sb(f"wupr{i}", [128, KC, 2, 128], BF16) for i in range(NR)]
        wdnr = [sb(f"wdnr{i}", [128, D], BF16) for i in range(NR)]
        cab = [sb(f"cab{i}", [128, 2, 256]) for i in range(2)]
        gt = [sb(f"gt{i}", [128, 256], BF16) for i in range(2)]
        stg = [hres[0], hres[1]]
        stgb = [xn2, xn]
        pb = [st.enter_context(nc.psum_tensor(f"pb{i}", [128, 512], F32)) for i in range(8)]
        pbb = [p[:].bitcast(BF16) for p in pb]
        BP = [Buf(f"pb{i}", psum=True) for i in range(8)]

        U = tri[:, 0:128]
        UT = tri[:, 128:256]
        ONES = tri[:, 256:384]
        IDF = tri[:, 384:512]

        def load(slot, out, in_, wbuf, reads=()):
            P.dma(slot, [lambda e: e.dma_start(out=out, in_=in_)], reads=list(reads), writes=[wbuf])

        def act(out, in_, func, reads, writes=(), pwrites=(), **kw_):
            P.op("act", lambda e: e.activation(out=out, in_=in_, func=func, **kw_), reads=reads,
                 writes=writes, pwrites=pwrites)

        def ts(eng, out, in0, s1, s2, op0, op1=None, reads=(), writes=(), pwrites=()):
            if op1 is None:
                P.op(eng, lambda e: e.tensor_scalar(out=out, in0=in0, scalar1=s1, scalar2=None, op0=op0),
                     reads=reads, writes=writes, pwrites=pwrites)
            else:
                P.op(eng, lambda e: e.tensor_scalar(out=out, in0=in0, scalar1=s1, scalar2=s2, op0=op0, op1=op1),
                     reads=reads, writes=writes, pwrites=pwrites)

        def tt(eng, out, in0, in1, op, reads=(), writes=(), pwrites=()):
            P.op(eng, lambda e: e.tensor_tensor(out=out, in0=in0, in1=in1, op=op), reads=reads,
                 writes=writes, pwrites=pwrites)

        def stt(eng, out, in0, scalar, in1, op0, op1, reads=(), writes=(), pwrites=()):
            P.op(eng, lambda e: e.scalar_tensor_tensor(out=out, in0=in0, scalar=scalar, in1=in1, op0=op0, op1=op1),
                 reads=reads, writes=writes, pwrites=pwrites)

        def cp(eng, out, in_, reads=(), writes=(), pwrites=()):
            if eng == "act":
                act(out, in_, AF.Copy, reads, writes, pwrites)
            else:
                P.op(eng, lambda e: e.tensor_copy(out=out, in_=in_), reads=reads, writes=writes, pwrites=pwrites)

        def mm_group(mms, reads, writes=(), pwrites=()):
            def fn(e):
                ins = None
                for m in mms:
                    o, l, r, s0, s1 = m[:5]
                    if len(m) > 5:
                        ins = e.matmul(o, lhsT=l, rhs=r, start=s0, stop=s1, skip_group_check=True)
                    else:
                        ins = e.matmul(o, lhsT=l, rhs=r, start=s0, stop=s1)
                return ins
            P.op("pe", fn, reads=reads, writes=writes, pwrites=pwrites)

        def transposes(outs_ins, reads, writes=(), pwrites=()):
            def fn(e):
                ins = None
                for (o, i) in outs_ins:
                    ins = e.transpose(out=o, in_=i, identity=ident[:])
                return ins
            P.op("pe", fn, reads=list(reads) + [B("ident")], writes=writes, pwrites=pwrites)

        try:
            def ckpt(n):
                if DBG_PHASE == n:
                    raise _Stop()

            load("c0", ident[:], c_ident, B("ident"))
            load("c1", tri[:], c_tri, B("tri"))
            load("c2", mbias[:], c_mb, B("mbias"))
            load("c3", flags[:], c_flags, B("flags"))
            load("c4", gb[:], gbias.partition_broadcast(128), B("gb"))
            load("c5", esink[:], sink.partition_broadcast(128), B("esink"))
            load("c6", nfw[:], normf.partition_broadcast(128), B("nfw"))

            def load_T(dst, src_rows, n, wbuf, first=True):
                P.dma("c7", [lambda e: e.dma_start(out=rowst[0:n, 0:128], in_=src_rows)], writes=[B("sgo")])
                mm_group([(pb[0][:, 0:n], rowst[0:n, 0:128], IDF[0:n, 0:n], True, True)], [B("sgo"), B("tri")], writes=[BP[0]])
                cp("dve", dst, pb[0][:, 0:n], reads=[BP[0]], writes=[wbuf] if first else (), pwrites=() if first else [wbuf])
            load_T(n1c[:], norm1.rearrange("(k p) -> k p", p=128), KC, B("n1c"))
            load_T(n2c[:], norm2.rearrange("(k p) -> k p", p=128), KC, B("n2c"))
            load_T(mhc[:], mhw.rearrange("(k p) -> k p", p=128), 4, B("mhc"))
            for j in range(3):
                load_T(cw[:, j, :], convw[j].rearrange("(c p) -> c p", p=128), 2 * NFC, B("cw"), first=(j == 0))
            load_T(cb[:], convb.rearrange("(c p) -> c p", p=128), 2 * NFC, B("cb"))
            act(esink[:], esink[:], AF.Exp, [B("esink")], [B("esink")])
            P.op("pool", lambda e: e.memset(vp[:], 1.0), writes=[B("vp")])
            for i in range(3):
                P.op("pool", lambda e, i=i: e.memset(avp[i][:], 1.0), writes=[B(f"avp{i}")])

            PIECE = 1408
            wcount = [0]

            bcount = [0]

            def prep2(src_rows, ncols, normcol, consume, scale_ranges=(), direct=None, bdirect=None, bg=False):
                c0 = 0
                while c0 < ncols:
                    c1 = min(ncols, c0 + D)
                    i = 1 if bg else wcount[0] % 3
                    wcount[0] += 1
                    sf, bsf = hres[i], B(f"hres{i}")
                    w = c1 - c0
                    load(f"wl{i}", sf[:, 0:w], src_rows[:, c0:c1], bsf)
                    if direct is None:
                        if bg:
                            ib, sbf, bsb = 0, PT[:].rearrange("p k t -> p (k t)"), B("PT")
                        else:
                            ib = bcount[0] % 2
                            bcount[0] += 1
                            sbf, bsb = stgb[ib], B("xn2" if ib == 0 else "xn")
                    cuts = sorted(set([c0, c1] + [c for (a, b, s) in scale_ranges for c in (a, b) if c0 < c < c1]))
                    eng = "pool" if bg else ("dve" if wcount[0] % 2 == 0 else "pool")
                    first = True
                    for a, b in zip(cuts[:-1], cuts[1:]):
                        s = 1.0
                        for (ra, rb, rs) in scale_ranges:
                            if ra <= a and b <= rb:
                                s = rs
                        sc1 = normcol if normcol is not None else 1.0
                        rd = [bsf] + ([] if normcol is None else [B("ncols")])
                        if direct is not None:
                            ts(eng, direct(a, b), sf[:, a - c0:b - c0], sc1, float(s), ALU.mult, ALU.mult, reads=rd,
                               pwrites=[bdirect])
                        else:
                            ts(eng, sbf[:, a - c0:b - c0], sf[:, a - c0:b - c0], sc1, float(s), ALU.mult, ALU.mult, reads=rd,
                               writes=[bsb] if first else (), pwrites=() if first else [bsb])
                        first = False
                    if direct is None:
                        consume(c0, c1, sbf, bsb, ib)
                    c0 = c1
                    if bg:
                        yield

            bufs["ncols"] = Buf("ncols")
            for nm in ("n1c", "n2c", "mhc"):
                for w_ in B(nm).writers:
                    bufs["ncols"].writers.append(w_)

            sr_in = [(MK, MK + 512, 128.0 ** -0.5), (AQ, AQ + 512, 0.125)]
            for kc in range(KC):
                for _ in prep2(w_in[kc * 128:(kc + 1) * 128, :], DIN, n1c[:, kc:kc + 1], None, sr_in,
                      direct=lambda a, b, kc=kc: win[:, kc, a:b], bdirect=B("win")):
                    pass
            for kc in range(4):
                for _ in prep2(w_pm[kc * 128:(kc + 1) * 128, :], D, mhc[:, kc:kc + 1], None,
                      direct=lambda a, b, kc=kc: wpm[:, kc, a:b], bdirect=B("wpm")):
                    pass
            for kc in range(4):
                for _ in prep2(w_pa[kc * 128:(kc + 1) * 128, :], D, None, None,
                      direct=lambda a, b, kc=kc: wpa[:, kc, a:b], bdirect=B("wpa")):
                    pass
            for kc in range(KC):
                for _ in prep2(w_out[kc * 128:(kc + 1) * 128, :], D, None, None,
                      direct=lambda a, b, kc=kc: wout[:, kc, a:b], bdirect=B("wout")):
                    pass
            def ffn_weight_prep():
                wup_v = wup_s.rearrange("j p k a c -> p j k a c")
                for kc in range(KC):
                    def consume(c0, c1, sbf, bsb, i, kc=kc):
                        fns = []
                        c = c0
                        while c < c1:
                            ab = 0 if c < DFF else 1
                            lim = DFF if ab == 0 else 2 * DFF
                            ce = min(c1, lim)
                            j0 = (c - ab * DFF) // 128
                            nj = (ce - c) // 128
                            src = sbf[:, c - c0:ce - c0].rearrange("p (j c) -> p j c", c=128)
                            dst = wup_v[:, j0:j0 + nj, kc, ab, :]
                            fns.append(lambda e, src=src, dst=dst: e.dma_start(out=dst, in_=src))
                            c = ce
                        P.dma(f"ws{i}", fns, reads=[bsb], writes=[B("wup_s")] if False else [], )
                        B("wup_s").writers.append((f"ws{i}", P.count[f"ws{i}"]))
                        bsb.readers.append((f"ws{i}", P.count[f"ws{i}"]))
                    yield from prep2(w_up[kc * 128:(kc + 1) * 128, :], 2 * DFF, n2c[:, kc:kc + 1], consume, bg=True)
                for j in range(NFC):
                    def consume(c0, c1, sbf, bsb, i, j=j):
                        P.dma(f"ws{i}", [lambda e: e.dma_start(out=wdn_s[j], in_=sbf[:, 0:D])], reads=[bsb])
                        B("wdn_s").writers.append((f"ws{i}", P.count[f"ws{i}"]))
                    yield from prep2(w_dn[j * 128:(j + 1) * 128, :], D, None, consume, bg=True)


            state = {"bg": ffn_weight_prep()}
            ckpt(1)
            state.update({"xi": 0, "ri": 0})

            def rmsnorm_T(xt, bx, dstT, bdst, psum_i=3, xn_=None, bxn=None):
                if xn_ is None:
                    xn_, bxn = xn, B("xn")
                act(xn_[:], xt, AF.Square, [bx], [bxn, B("sm0")], accum_out=sm[:, 0:1])
                act(sm[:, 1:2], sm[:, 0:1], AF.Ln, [B("sm0")], [B("sm1")], scale=1.0 / D, bias=EPS)
                act(sm[:, 2:3], sm[:, 1:2], AF.Exp, [B("sm1")], [B("sm2")], scale=-0.5)
                act(xn_[:], xt, AF.Copy, [bx, B("sm2")], [bxn], scale=sm[:, 2:3])
                transposes([(pbb[psum_i][:, k * 128:(k + 1) * 128], xn_[:, k * 128:(k + 1) * 128]) for k in range(KC)],
                           [bxn], writes=[BP[psum_i]])
                cp("dve", dstT.rearrange("p k t -> p (k t)"), pbb[psum_i][:, 0:D], reads=[BP[psum_i]], writes=[bdst])

            def proj(xT, bxT, c0, n, bank, lo=0):
                mm_group([(pb[bank][:, lo:lo + n], xT[:, k, :], win[:, k, c0:c0 + n], k == 0, k == KC - 1) for k in range(KC)],
                         [bxT, B("win")], writes=[BP[bank]])

            SL = lambda: None
            def slot(i):
                if i == 0:
                    return dict(RW=RW, bRW=B("RW"), Dd=Dd, bDd=B("Dd"), kb=kb, bkb=B("kb"), vp=vp, bvp=B("vp"))
                return dict(RW=RW2, bRW=B("RW2"), Dd=Dd2, bDd=B("Dd2"), kb=aqr, bkb=B("aqr"), vp=Cdb[0], bvp=B("Cdb0"))

            def gates(si=0, cb_=3, part=None):
                S_ = slot(si)
                RW_, bRW, Dd_, bDd = S_["RW"], S_["bRW"], S_["Dd"], S_["bDd"]
                if part in (None, "a"):
                    act(E8[:], G[:, 8:16], AF.Exp, [B("G")], [B("E8")], scale=-1.0)
                    act(LF[:], E8[:], AF.Ln, [B("E8")], [B("LF")], bias=1.0)
                if part in (None, "b"):
                    mm_group([(pb[cb_][:, 0:4], U, LF[:, 0:4], True, True),
                              (pb[cb_][:, 4:8], UT, LF[:, 4:8], True, True),
                              (pb[cb_][:, 8:16], ONES, LF[:, 0:8], True, True)], [B("tri"), B("LF")], writes=[BP[cb_]])
                if part in (None, "c"):
                    cp("dve", NB[:], pb[cb_][:, 0:16], reads=[BP[cb_]], writes=[B("NB")])
                    tt("dve", RW_[:, 0:8], NB[:, 0:8], NB[:, 8:16], ALU.subtract, reads=[B("NB")], writes=[bRW])
                    tt("dve", RW_[:, 8:16], RW_[:, 0:8], G[:, 0:8], ALU.add, reads=[bRW, B("G")], writes=[bRW])
                    act(RW_[:], RW_[:], AF.Exp, [bRW], [bRW])
                    act(Dd_[:], NB[:, 8:16], AF.Exp, [B("NB")], [bDd], scale=-1.0)

            def state_update(d, si=0):
                S_ = slot(si)
                for h in range(4):
                    act(Cd[:, h, :], Cst[:, h, :], AF.Copy, [B("Cst"), S_["bDd"]], writes=[B("Cd")] if h == 0 else (),
                        pwrites=() if h == 0 else [B("Cd")], scale=S_["Dd"][:, d * 4 + h:d * 4 + h + 1])

            def cloc_and_update(d, si=0):
                S_ = slot(si)
                RW_, Dd_, kb_, vp_ = S_["RW"], S_["Dd"], S_["kb"], S_["vp"]
                for h in range(4):
                    ts("dve", kw[:, h * 128:(h + 1) * 128], kb_[:, h * 128:(h + 1) * 128], RW_[:, 8 + d * 4 + h:9 + d * 4 + h],
                       None, ALU.mult, reads=[S_["bkb"], S_["bRW"]], writes=[B("kw")] if h == 0 else (),
                       pwrites=() if h == 0 else [B("kw")])
                regs = [(7, 0), (7, 129), (7, 258), (6, 258)]
                mm_group([(pb[regs[h][0]][:, regs[h][1]:regs[h][1] + 129], kw[:, h * 128:(h + 1) * 128], vp_[:, h, :], True, True)
                          for h in range(3)], [B("kw"), S_["bvp"]], writes=[BP[7]])
                mm_group([(pb[6][:, 258:387], kw[:, 384:512], vp_[:, 3, :], True, True)], [B("kw"), S_["bvp"]], pwrites=[BP[6]])
                for h in range(4):
                    bk, c0 = regs[h]
                    stt("dve", Cst[:, h, :], Cst[:, h, :], Dd_[:, d * 4 + h:d * 4 + h + 1], pb[bk][:, c0:c0 + 129],
                        ALU.mult, ALU.add, reads=[B("Cst"), S_["bDd"], BP[bk]], writes=[B("Cst")] if h == 0 else (),
                        pwrites=() if h == 0 else [B("Cst")])

            def load_x(src, row0):
                i = 0
                state["xi"] += 1
                load(f"xl{i}", xs[i][:], src[row0:row0 + 128, :], B(f"x{i}"))
                return i

            def zero_state():
                P.op("pool", lambda e: e.memset(Cst[:], 0.0), writes=[B("Cst")])

            XS = [(xs[0], "x0"), (hres[0], "hres0")]

            XN = [(xn, "xn"), (xn2, "xn2")]

            def scan(src, chunk_rows, d, store_slots=None, ctx_flags=None):
                P.op("dve", lambda e: e.memset(Cdb[0][:, :, 128:129], 1.0), writes=[B("Cdb0")])
                n = len(chunk_rows)

                def s0_act(i):
                    si = i % 2
                    xt, bxn = XS[si]
                    bx = B(bxn)
                    xn_, bn_ = XN[si][0], B(XN[si][1])
                    load(f"xl{si}", xt[:], src[chunk_rows[i]:chunk_rows[i] + 128, :], bx)
                    if ctx_flags is not None:
                        load(f"fl{si}", flt[si][:], ctx_flags[i], B(f"flt{si}"))
                    act(xn_[:], xt[:], AF.Square, [bx], [bn_, B("sm0")], accum_out=sm[:, 0:1])
                    act(sm[:, 1:2], sm[:, 0:1], AF.Ln, [B("sm0")], [B("sm1")], scale=1.0 / D, bias=EPS)
                    act(sm[:, 2:3], sm[:, 1:2], AF.Exp, [B("sm1")], [B("sm2")], scale=-0.5)
                    act(xn_[:], xt[:], AF.Copy, [bx, B("sm2")], [bn_], scale=sm[:, 2:3])

                def s0_pe(i):
                    si = i % 2
                    xn_, bn_ = XN[si][0], B(XN[si][1])
                    transposes([(pbb[3][:, k * 128:(k + 1) * 128], xn_[:, k * 128:(k + 1) * 128]) for k in range(KC)],
                               [bn_], writes=[BP[3]])
                    cp("dve", xnT[si][:].rearrange("p k t -> p (k t)"), pbb[3][:, 0:D], reads=[BP[3]], writes=[B(f"xnT{si}")])

                def s1_pe(i):
                    si = i % 2
                    bxT = B(f"xnT{si}")
                    proj(xnT[si], bxT, GI, 16, 2)
                    proj(xnT[si], bxT, MK, 512, 0)
                    proj(xnT[si], bxT, MV, 512, 1)

                def s1_ev(i):
                    S_ = slot(i % 2)
                    tt("dve", G[:], pb[2][:, 0:16], gb[:], ALU.add, reads=[BP[2], B("gb")], writes=[B("G")])
                    cp("act", S_["kb"][:], pb[0][:, :], reads=[BP[0]], writes=[S_["bkb"]])
                    cp("act", S_["vp"][:, :, 0:128], pb[1][:, :].rearrange("p (h c) -> p h c", h=4), reads=[BP[1]],
                       pwrites=[S_["bvp"]])

                def s1_g1(i):
                    act(E8[:], G[:, 8:16], AF.Exp, [B("G")], [B("E8")], scale=-1.0)
                    act(LF[:], E8[:], AF.Ln, [B("E8")], [B("LF")], bias=1.0)
                    mm_group([(pb[5][:, 0:4], U, LF[:, 0:4], True, True),
                              (pb[5][:, 4:8], UT, LF[:, 4:8], True, True),
                              (pb[5][:, 8:16], ONES, LF[:, 0:8], True, True)], [B("tri"), B("LF")], writes=[BP[5]])

                def s1_g2(i):
                    S_ = slot(i % 2)
                    RW_, bRW, Dd_, bDd = S_["RW"], S_["bRW"], S_["Dd"], S_["bDd"]
                    cp("dve", NB[:], pb[5][:, 0:16], reads=[BP[5]], writes=[B("NB")])
                    tt("dve", RW_[:, 0:8], NB[:, 0:8], NB[:, 8:16], ALU.subtract, reads=[B("NB")], writes=[bRW])
                    tt("dve", RW_[:, 8:16], RW_[:, 0:8], G[:, 0:8], ALU.add, reads=[bRW, B("G")], writes=[bRW])
                    act(RW_[:], RW_[:], AF.Exp, [bRW], [bRW])
                    act(Dd_[:], NB[:, 8:16], AF.Exp, [B("NB")], [bDd], scale=-1.0)
                    if ctx_flags is not None:
                        fi = i % 2
                        bfl = B(f"flt{fi}")
                        tt("dve", RW_[:, 8:16], RW_[:, 8:16], flt[fi][:, 0:8], ALU.mult, reads=[bRW, bfl], writes=[bRW])
                        stt("dve", Dd_[:], Dd_[:], -1.0, flt[fi][:, 8:16], ALU.add, ALU.mult, reads=[bDd, bfl], writes=[bDd])
                        ts("dve", Dd_[:], Dd_[:], 1.0, None, ALU.add, reads=[bDd], writes=[bDd])

                REGS = [(7, 0), (7, 129), (7, 258), (6, 258)]
                REGS_B = [(4, 0), (4, 129), (4, 258), (6, 0)]
                fstate = hres[2][:, 0:516].rearrange("p (h c) -> p h c", c=129)

                def s2_a(i):
                    si = i % 2
                    S_ = slot(si)
                    if ctx_flags is not None:
                        for dd, kwt, bkw, regs in ((0, kw, B("kw"), REGS), (1, hn, B("hn"), REGS_B)):
                            for h in range(4):
                                ts("pool", kwt[:, h * 128:(h + 1) * 128], S_["kb"][:, h * 128:(h + 1) * 128],
                                   S_["RW"][:, 8 + dd * 4 + h:9 + dd * 4 + h], 1.0, ALU.mult, ALU.mult,
                                   reads=[S_["bkb"], S_["bRW"]], writes=[bkw] if h == 0 else (), pwrites=() if h == 0 else [bkw])
                            mm_group([(pb[regs[h][0]][:, regs[h][1]:regs[h][1] + 129], kwt[:, h * 128:(h + 1) * 128],
                                       S_["vp"][:, h, :], True, True) for h in range(3)], [bkw, S_["bvp"]], writes=[BP[regs[0][0]]])
                            mm_group([(pb[6][:, regs[3][1]:regs[3][1] + 129], kwt[:, 384:512], S_["vp"][:, 3, :], True, True)],
                                     [bkw, S_["bvp"]], writes=[BP[6]])
                        return
                    if store_slots is not None:
                        state_update(d, si)
                        ss = store_slots[i]
                        P.dma("cds", [lambda e: e.dma_start(out=cdb_s[ss], in_=Cd[:].rearrange("p h c -> p (h c)"))],
                              reads=[B("Cd")], writes=[B(f"cdb_s{ss}")])
                    for h in range(4):
                        ts("dve", kw[:, h * 128:(h + 1) * 128], S_["kb"][:, h * 128:(h + 1) * 128],
                           S_["RW"][:, 8 + d * 4 + h:9 + d * 4 + h], None, ALU.mult, reads=[S_["bkb"], S_["bRW"]],
                           writes=[B("kw")] if h == 0 else (), pwrites=() if h == 0 else [B("kw")])
                    mm_group([(pb[REGS[h][0]][:, REGS[h][1]:REGS[h][1] + 129], kw[:, h * 128:(h + 1) * 128], S_["vp"][:, h, :],
                               True, True) for h in range(3)], [B("kw"), S_["bvp"]], writes=[BP[7]])
                    mm_group([(pb[6][:, 258:387], kw[:, 384:512], S_["vp"][:, 3, :], True, True)], [B("kw"), S_["bvp"]],
                             writes=[BP[6]])

                def s2_b(i):
                    S_ = slot(i % 2)
                    if ctx_flags is not None:
                        for dd, stt_, bst_, regs in ((0, fstate, B("hres2"), REGS), (1, Cst, B("Cst"), REGS_B)):
                            for h in range(4):
                                bk, c0 = regs[h]
                                stt("dve", stt_[:, h, :], stt_[:, h, :], S_["Dd"][:, dd * 4 + h:dd * 4 + h + 1],
                                    pb[bk][:, c0:c0 + 129], ALU.mult, ALU.add, reads=[bst_, S_["bDd"], BP[bk]],
                                    writes=[bst_] if h == 0 else (), pwrites=() if h == 0 else [bst_])
                        return
                    for h in range(4):
                        bk, c0 = REGS[h]
                        stt("dve", Cst[:, h, :], Cst[:, h, :], S_["Dd"][:, d * 4 + h:d * 4 + h + 1], pb[bk][:, c0:c0 + 129],
                            ALU.mult, ALU.add, reads=[B("Cst"), S_["bDd"], BP[bk]], writes=[B("Cst")] if h == 0 else (),
                            pwrites=() if h == 0 else [B("Cst")])

                s0_act(0)
                s0_pe(0)
                if n > 1:
                    s0_act(1)
                    s0_pe(1)
                s1_pe(0)
                s1_ev(0)
                s1_g1(0)
                s1_g2(0)
                for idx in range(n):
                    if idx + 1 < n:
                        s1_pe(idx + 1)
                    s2_a(idx)
                    if idx + 2 < n:
                        s0_act(idx + 2)
                    if idx + 1 < n:
                        s1_ev(idx + 1)
                        s1_g1(idx + 1)
                    s2_b(idx)
                    if idx + 1 < n:
                        s1_g2(idx + 1)
                    if idx + 2 < n:
                        s0_pe(idx + 2)
                    if state.get("bg") is not None:
                        if next(state["bg"], "done") == "done":
                            state["bg"] = None

            def rope_apply(src_psum, bsrc, nh, dst_view, bdst, rope_t, brope, first_write):
                sv = src_psum.rearrange("p (h c) -> p h c", c=64)
                cos = rope_t[:, 0:8].unsqueeze(1).to_broadcast([128, nh, 8])
                sin = rope_t[:, 8:16].unsqueeze(1).to_broadcast([128, nh, 8])
                x1, x2 = sv[:, :, 0:8], sv[:, :, 8:16]
                ta, tb, tc, td = (rt[:, i, 0:nh, :] for i in range(4))
                tt("dve", ta, x1, cos, ALU.mult, reads=[bsrc, brope], writes=[B("rt")])
                tt("dve", tb, x2, sin, ALU.mult, reads=[bsrc, brope], pwrites=[B("rt")])
                tt("dve", tc, x2, cos, ALU.mult, reads=[bsrc, brope], pwrites=[B("rt")])
                tt("dve", td, x1, sin, ALU.mult, reads=[bsrc, brope], pwrites=[B("rt")])
                w0 = [bdst] if first_write else ()
                p0 = () if first_write else [bdst]
                tt("dve", dst_view(0, 8), ta, tb, ALU.subtract, reads=[B("rt")], writes=w0, pwrites=p0)
                tt("dve", dst_view(8, 16), tc, td, ALU.add, reads=[B("rt")], pwrites=[bdst])
                cp("act", dst_view(16, 64), sv[:, :, 16:64], reads=[bsrc], pwrites=[bdst])

            def s1_kv(job, c):
                xi = load_x(job["x"], c * 128)
                r = c % 3
                load(f"rl{r}", ropes[r][:], c_rope[job["tab0"] + c], B(f"rope{r}"))
                j = c % 2
                xt, bx = xs[xi][:], B(f"x{xi}")
                act(xn[:], xt, AF.Square, [bx], [B("xn"), B("sm0")], accum_out=sm[:, 0:1])
                act(sm[:, 1:2], sm[:, 0:1], AF.Ln, [B("sm0")], [B("sm1")], scale=1.0 / D, bias=EPS)
                act(sm[:, 2:3], sm[:, 1:2], AF.Exp, [B("sm1")], [B("sm2")], scale=-0.5)
                ts("pool", xn[:], xt, sm[:, 2:3], 1.0, ALU.mult, ALU.mult, reads=[bx, B("sm2")], writes=[B("xn")])
                yield
                transposes([(pbb[3][:, k * 128:(k + 1) * 128], xn[:, k * 128:(k + 1) * 128]) for k in range(KC)],
                           [B("xn")], writes=[BP[3]])
                cp("dve", xnT[j][:].rearrange("p k t -> p (k t)"), pbb[3][:, 0:D], reads=[BP[3]], writes=[B(f"xnT{j}")])
                yield
                proj(xnT[j], B(f"xnT{j}"), AK, 512, 1)
                s_ = c % 3
                rope_apply(pb[1][:, 0:256], BP[1], 4, lambda a, b_: akd[:, :, a:b_], B("akd"), ropes[r], B(f"rope{r}"), True)
                cp("act", avp[s_][:, :, 0:64], pb[1][:, 256:512].rearrange("p (h c) -> p h c", h=4), reads=[BP[1]],
                   pwrites=[B(f"avp{s_}")])
                yield
                transposes([(pbb[3][0:64, k * 128:(k + 1) * 128], akd[:, k, :]) for k in range(4)],
                           [B("akd")], writes=[BP[3]])
                cp("dve", akT[s_][0:64, :, :].rearrange("p k t -> p (k t)"), pbb[3][0:64, 0:512], reads=[BP[3]], writes=[B(f"akT{s_}")])
                yield

            def s1_rest(job, c):
                j = c % 2
                r = c % 3
                bxT = B(f"xnT{j}")
                proj(xnT[j], bxT, GI, 16, 2)
                proj(xnT[j], bxT, MQ, 512, 0)
                proj(xnT[j], bxT, MK, 512, 1)
                tt("dve", G[:], pb[2][:, 0:16], gb[:], ALU.add, reads=[BP[2], B("gb")], writes=[B("G")])
                gates(0, part="a")
                cp("dve", qb[:], pb[0][:, :], reads=[BP[0]], writes=[B("hn")])
                cp("dve", kb[:], pb[1][:, :], reads=[BP[1]], writes=[B("kb")])
                yield
                proj(xnT[j], bxT, MV, 512, 2)
                proj(xnT[j], bxT, MO, 512, 0)
                proj(xnT[j], bxT, AQ, 512, 1)
                gates(0, cb_=6, part="b")
                gates(0, cb_=6, part="c")
                cp("act", vp[:, :, 0:128], pb[2][:, :].rearrange("p (h c) -> p h c", h=4), reads=[BP[2]], pwrites=[B("vp")])
                act(sgo[:], pb[0][:, :], AF.Sigmoid, [BP[0]], [B("sgo")])
                yield
                rope_apply(pb[1][:, :], BP[1], 8, lambda a, b_: aqr[:].rearrange("p (h c) -> p h c", c=64)[:, :, a:b_],
                           B("aqr"), ropes[r], B(f"rope{r}"), True)
                yield
                transposes([(pbb[2][:, k * 128:(k + 1) * 128], qb[:, k * 128:(k + 1) * 128]) for k in range(4)] +
                           [(pbb[2][:, 512 + k * 128:512 + (k + 1) * 128], kb[:, k * 128:(k + 1) * 128]) for k in range(4)],
                           [B("hn"), B("kb")], writes=[BP[2]])
                cp("dve", qT[:].rearrange("p k t -> p (k t)"), pbb[2][:, 0:512], reads=[BP[2]], writes=[B("qT")])
                cp("dve", kT[:].rearrange("p k t -> p (k t)"), pbb[2][:, 512:1024], reads=[BP[2]], writes=[B("kT")])
                yield
                transposes([(pbb[0][0:64, k * 128:(k + 1) * 128], aqr[:, k * 64:(k + 1) * 64]) for k in range(8)],
                           [B("aqr")], writes=[BP[0]])
                cp("dve", aqT[0:64, :, :].rearrange("p k t -> p (k t)"), pbb[0][0:64, 0:1024], reads=[BP[0]], writes=[B("aqT")])
                yield

            def interleave(*gens):
                gens = [g for g in gens if g is not None]
                while gens:
                    for g in list(gens):
                        try:
                            next(g)
                        except StopIteration:
                            gens.remove(g)

            ACC = [(4 + (i // 3), (i % 3) * 129) for i in range(8)]

            def s2_mlstm(job, c, first, cdb_slot):
                mm_group([(pb[3][:, h * 128:(h + 1) * 128], kT[:, h, :], qT[:, h, :], True, True) for h in range(4)],
                         [B("kT"), B("qT")], writes=[BP[3]])
                for h in range(4):
                    for d in range(2):
                        i = h * 2 + d
                        stt("dve", PT[:, i, :], pb[3][:, h * 128:(h + 1) * 128], RW[:, 8 + d * 4 + h:9 + d * 4 + h],
                            U if d == 0 else UT, ALU.mult, ALU.mult, reads=[BP[3], B("RW"), B("tri")],
                            writes=[B("PT")] if i == 0 else (), pwrites=() if i == 0 else [B("PT")])
                if first:
                    P.op("pool", lambda e: e.memset(Cd[:], 0.0), writes=[B("Cd")])
                else:
                    state_update(0)
                bcd = B(f"Cdb{cdb_slot}") if cdb_slot is not None else None
                for bank in (4, 5, 6):
                    mms = []
                    rd = [B("PT"), B("vp"), B("qT"), B("Cd")] + ([bcd] if bcd is not None else [])
                    for i in range(8):
                        if ACC[i][0] != bank:
                            continue
                        h, d = i // 2, i % 2
                        o = pb[bank][:, ACC[i][1]:ACC[i][1] + 129]
                        inter = Cd[:, h, :] if d == 0 else (Cdb[cdb_slot][:, h, :] if cdb_slot is not None else None)
                        if inter is None:
                            mms.append((o, PT[:, i, :], vp[:, h, :], True, True))
                        else:
                            mms.append((o, PT[:, i, :], vp[:, h, :], True, False))
                            mms.append((o, qT[:, h, :], inter, False, True))
                    mm_group(mms, rd, writes=[BP[bank]])
                cloc_and_update(0)
                yield
                for bank in (4, 5, 6):
                    n = 3 if bank < 6 else 2
                    i0 = (bank - 4) * 3
                    act(sm[:, 8 + i0:8 + i0 + n], pb[bank][:, 128:128 + 129 * (n - 1) + 1:129], AF.Abs, [BP[bank]],
                        writes=[B("sm8")] if i0 == 0 else (), pwrites=() if i0 == 0 else [B("sm8")])
                rinv = RW[:, 0:8].rearrange("p (d h) -> p h d", d=2)
                tt("dve", sm[:, 16:24].rearrange("p (h d) -> p h d", d=2), sm[:, 8:16].rearrange("p (h d) -> p h d", d=2),
                   rinv, ALU.max, reads=[B("sm8"), B("RW")], writes=[B("sm16")])
                P.op("dve", lambda e: e.reciprocal(out=sm[:, 24:32], in_=sm[:, 16:24]), reads=[B("sm16")], writes=[B("sm24")])
                yield
                for h in range(4):
                    bf_, cf_ = ACC[2 * h]
                    bb_, cb_ = ACC[2 * h + 1]
                    act(hh[:, h * 128:(h + 1) * 128], pb[bf_][:, cf_:cf_ + 128], AF.Copy, [BP[bf_], B("sm24")],
                        writes=[B("hh")] if h == 0 else (), pwrites=() if h == 0 else [B("hh")], scale=sm[:, 24 + 2 * h:25 + 2 * h])
                    if h % 2 == 1:
                        yield
                for h in range(4):
                    bb_, cb_ = ACC[2 * h + 1]
                    stt("dve", hh[:, h * 128:(h + 1) * 128], pb[bb_][:, cb_:cb_ + 128], sm[:, 25 + 2 * h:26 + 2 * h],
                        hh[:, h * 128:(h + 1) * 128], ALU.mult, ALU.add, reads=[BP[bb_], B("sm24"), B("hh")],
                        writes=[B("hh")] if h == 0 else (), pwrites=() if h == 0 else [B("hh")])
                    if h % 2 == 1:
                        yield
                for h in range(4):
                    P.op("dve", lambda e, h=h: e.bn_stats(out=bst[:, h, :], in_=hh[:, h * 128:(h + 1) * 128]), reads=[B("hh")],
                         writes=[B("bst")] if h == 0 else (), pwrites=() if h == 0 else [B("bst")])
                    if h % 2 == 1:
                        yield
                for h in range(4):
                    P.op("dve", lambda e, h=h: e.bn_aggr(out=mv[:, h, :], in_=bst[:, h, :]), reads=[B("bst")],
                         writes=[B("mv")] if h == 0 else (), pwrites=() if h == 0 else [B("mv")])
                act(sm[:, 32:36], mv[:, :, 1], AF.Ln, [B("mv")], [B("sm32")], bias=EPS)
                act(sm[:, 36:40], sm[:, 32:36], AF.Exp, [B("sm32")], [B("sm36")], scale=-0.5)
                yield
                for h in range(4):
                    stt("dve", hh[:, h * 128:(h + 1) * 128], hh[:, h * 128:(h + 1) * 128], mv[:, h, 0:1],
                        sgo[:, h * 128:(h + 1) * 128], ALU.subtract, ALU.mult, reads=[B("hh"), B("mv"), B("sgo")],
                        writes=[B("hh")] if h == 0 else (), pwrites=() if h == 0 else [B("hh")])
                    if h % 2 == 1:
                        yield
                for h in range(4):
                    act(hn[:, h * 128:(h + 1) * 128], hh[:, h * 128:(h + 1) * 128], AF.Copy, [B("hh"), B("sm36")],
                        writes=[B("hn")] if h == 0 else (), pwrites=() if h == 0 else [B("hn")], scale=sm[:, 36 + h:37 + h])
                yield

            def s2_mlstm_T():
                transposes([(pbb[3][:, k * 128:(k + 1) * 128], hn[:, k * 128:(k + 1) * 128]) for k in range(4)],
                           [B("hn")], writes=[BP[3]])
                cp("dve", hT[:].rearrange("p k t -> p (k t)"), pbb[3][:, 0:512], reads=[BP[3]], writes=[B("hT")])

            def s2_attn(job, c, blocks):
                nb = len(blocks)
                steps = [(j, bi) for j in range(4) for bi in range(nb)]

                def st(k):
                    j, bi = steps[k]
                    which, cc, kcol = blocks[bi]
                    s_ = cc % 3
                    reg = k % 4
                    bank, lo = reg % 2, (reg // 2) * 256
                    o = pb[bank][:, lo:lo + 256]
                    mms = []
                    if which != "cur":
                        mb = mbias[:, 0:256] if which == "prev" else mbias[:, 256:512]
                        mms.append((o, ident[:], mb, True, False))
                    nm = which == "cur"
                    mms.append((o, akT[s_][0:64, j, :], aqT[0:64, 2 * j:2 * j + 2, :].rearrange("p h t -> p (h t)"), nm, True))
                    mm_group(mms, [B(f"akT{s_}"), B("aqT"), B("mbias"), B("ident")], pwrites=[BP[bank]])

                def ex_pv(k):
                    j, bi = steps[k]
                    which, cc, kcol = blocks[bi]
                    s_ = cc % 3
                    reg = k % 4
                    bank, lo = reg % 2, (reg // 2) * 256
                    o = pb[bank][:, lo:lo + 256]
                    pt = PTa[reg]
                    if kcol is None:
                        act(pt[:], o, AF.Exp, [BP[bank]], [B(f"PTa{reg}")])
                    else:
                        act(pt[:], o, AF.Exp, [BP[bank], B("flags")], [B(f"PTa{reg}")], bias=flags[:, kcol:kcol + 1])
                    for hp in range(2):
                        hd = 2 * j + hp
                        ob, oc = (2, hd * 65) if hd < 4 else (3, (hd - 4) * 65)
                        first_in_bank = (bi == 0 and hp == 0 and j in (0, 2))
                        mm_group([(pb[ob][:, oc:oc + 65], pt[:, hp * 128:(hp + 1) * 128], avp[s_][:, j, :], first_in_bank, False, True)],
                                 [B(f"PTa{reg}"), B(f"avp{s_}")], pwrites=[BP[ob]])

                st(0)
                for k in range(len(steps)):
                    if k + 1 < len(steps):
                        st(k + 1)
                    ex_pv(k)
                    if k % 2 == 1:
                        yield
                for half in range(2):
                    ob = 2 + half
                    den = pb[ob][:, 0:260].rearrange("p (h c) -> p h c", c=65)[:, :, 64]
                    tt("dve", sm[:, 40 + 4 * half:44 + 4 * half], den, esink[:, 4 * half:4 * half + 4], ALU.add,
                       reads=[BP[ob], B("esink")], writes=[B(f"sm40{half}")])
                    P.op("dve", lambda e, half=half: e.reciprocal(out=sm[:, 48 + 4 * half:52 + 4 * half],
                                                                   in_=sm[:, 40 + 4 * half:44 + 4 * half]),
                         reads=[B(f"sm40{half}")], writes=[B(f"sm48{half}")])
                    for hq in range(4):
                        hd = half * 4 + hq
                        act(oa[:, hd * 64:(hd + 1) * 64], pb[ob][:, hq * 65:hq * 65 + 64], AF.Copy, [BP[ob], B(f"sm48{half}")],
                            writes=[B("kw")] if hd == 0 else (), pwrites=() if hd == 0 else [B("kw")],
                            scale=sm[:, 48 + hd:49 + hd])
                yield

            def s2_attn_T():
                transposes([(pbb[0][:, k * 128:(k + 1) * 128], oa[:, k * 128:(k + 1) * 128]) for k in range(4)],
                           [B("kw")], writes=[BP[0]])
                cp("dve", oaT[:].rearrange("p k t -> p (k t)"), pbb[0][:, 0:512], reads=[BP[0]], writes=[B("oaT")])

            def s2_merge_gates(c, half):
                j = c % 2
                bxT = B(f"xnT{j}")
                proj(xnT[j], bxT, GM + half * 512, 512, 4)
                act(sgm[:], pb[4][:, :], AF.Sigmoid, [BP[4]], [B("sgm")])
                proj(xnT[j], bxT, GA + half * 512, 512, 5)
                act(sga[:], pb[5][:, :], AF.Sigmoid, [BP[5]], [B("sga")])

            def s2_merge(job, c, hslot):
                j = c % 2
                bxT = B(f"xnT{j}")
                bh = B(f"hres{hslot}")
                for half in range(2):
                    hs = slice(half * 512, (half + 1) * 512)
                    if half == 1:
                        s2_merge_gates(c, 1)
                    mm_group([(pb[6][:, :], hT[:, k, :], wpm[:, k, hs], k == 0, k == 3) for k in range(4)],
                             [B("hT"), B("wpm")], writes=[BP[6]])
                    mm_group([(pb[7][:, :], oaT[:, k, :], wpa[:, k, hs], k == 0, k == 3) for k in range(4)],
                             [B("oaT"), B("wpa")], writes=[BP[7]])
                    tt("dve", hh[:], pb[6][:, :], sgm[:], ALU.mult, reads=[BP[6], B("sgm")], writes=[B("hh")])
                    tt("dve", sgo[:], pb[7][:, :], sga[:], ALU.mult, reads=[BP[7], B("sga")], writes=[B("sgo")])
                    tt("dve", ybf[:, hs], hh[:], sgo[:], ALU.add, reads=[B("hh"), B("sgo")],
                       writes=[B("xn2")] if half == 0 else (), pwrites=() if half == 0 else [B("xn2")])
                transposes([(pbb[0][:, k * 128:(k + 1) * 128], ybf[:, k * 128:(k + 1) * 128]) for k in range(KC)],
                           [B("xn2")], writes=[BP[0]])
                cp("act", xn[:], pbb[0][:, 0:D], reads=[BP[0]], writes=[B("xn")])
                for half in range(2):
                    hs = slice(half * 512, (half + 1) * 512)
                    bank = 1 + half
                    mm_group([(pb[bank][:, :], yT[:, k, :], wout[:, k, hs], k == 0, k == KC - 1) for k in range(KC)],
                             [B("xn"), B("wout")], writes=[BP[bank]])
                    tt("dve", hres[hslot][:, hs], hres[hslot][:, hs], pb[bank][:, :], ALU.add, reads=[bh, BP[bank]],
                       writes=[bh] if half == 0 else (), pwrites=() if half == 0 else [bh])

            def s2_norm2(job, c, hslot, targets):
                bh = B(f"hres{hslot}")
                act(xn2[:], hres[hslot][:], AF.Square, [bh], [B("xn2"), B("sm4")], accum_out=sm[:, 4:5])
                act(sm[:, 5:6], sm[:, 4:5], AF.Ln, [B("sm4")], [B("sm5")], scale=1.0 / D, bias=EPS)
                act(sm[:, 6:7], sm[:, 5:6], AF.Exp, [B("sm5")], [B("sm6")], scale=-0.5)
                yield
                ts("pool", xn2[:], hres[hslot][:], sm[:, 6:7], 1.0, ALU.mult, ALU.mult, reads=[bh, B("sm6")], writes=[B("xn2")])
                yield
                transposes([(pbb[7][:, k * 128:(k + 1) * 128], xn2[:, k * 128:(k + 1) * 128]) for k in range(KC)],
                           [B("xn2")], writes=[BP[7]])
                src = pbb[7][:, 0:D].rearrange("p (k t) -> p k t", t=128)
                for (us, d0, s0, n, fcol) in targets:
                    bu = B(f"unit{us}")
                    if fcol is None:
                        cp("dve", unit[us][:, :, d0:d0 + n], src[:, :, s0:s0 + n], reads=[BP[7]], pwrites=[bu])
                    else:
                        ts("dve", unit[us][:, :, d0:d0 + n], src[:, :, s0:s0 + n], flags[:, fcol:fcol + 1], None, ALU.mult,
                           reads=[BP[7], B("flags")], pwrites=[bu])
                yield

            def s3_ffn(job, us, hslots, out_rows):
                bu = B(f"unit{us}")

                def load_wu(j):
                    r = j % NR
                    P.dma(f"fwu{r}", [lambda e: e.dma_start(out=wupr[r][:], in_=wup_s[j])],
                          reads=[B("wup_s")], writes=[B(f"fwu{r}")])

                def load_wd(j):
                    r = j % NR
                    P.dma(f"fwd{r}", [lambda e: e.dma_start(out=wdnr[r][:], in_=wdn_s[j])],
                          reads=[B("wdn_s")], writes=[B(f"fwd{r}")])

                def up(j):
                    r = j % NR
                    for ab in range(2):
                        bank = ab + 2 * (j % 2)
                        mm_group([(pb[bank][:, 0:258], wupr[r][:, k, ab, :], unit[us][:, k, :], k == 0, k == KC - 1)
                                  for k in range(KC)], [B(f"fwu{r}"), bu], writes=[BP[bank]])

                def conv(j):
                    cslot = j % 2
                    bc = B(f"cab{cslot}")
                    for ab in range(2):
                        bank = ab + 2 * (j % 2)
                        fidx = ab * NFC + j
                        cv = cab[cslot][:, ab, :]
                        act(cv, pb[bank][:, 1:257], AF.Identity, [BP[bank], B("cw"), B("cb")],
                            writes=[bc] if ab == 0 else (), pwrites=() if ab == 0 else [bc],
                            scale=cw[:, 1, fidx:fidx + 1], bias=cb[:, fidx:fidx + 1])
                        stt("dve", cv, pb[bank][:, 0:256], cw[:, 0, fidx:fidx + 1], cv, ALU.mult, ALU.add,
                            reads=[BP[bank], B("cw"), bc], pwrites=[bc])
                        stt("dve", cv, pb[bank][:, 2:258], cw[:, 2, fidx:fidx + 1], cv, ALU.mult, ALU.add,
                            reads=[BP[bank], B("cw"), bc], pwrites=[bc])
                    act(cab[cslot][:, 0, :], cab[cslot][:, 0, :], AF.Silu, [bc], [bc])
                    gs = j % 2
                    tt("pool", gt[gs][:], cab[cslot][:, 0, :], cab[cslot][:, 1, :], ALU.mult, reads=[bc], writes=[B(f"gt{gs}")])

                def down(j):
                    r = j % NR
                    gs = j % 2
                    for ti in range(2):
                        for half in range(2):
                            bank = 4 + ti * 2 + half
                            mm_group([(pb[bank][:, :], gt[gs][:, ti * 128:(ti + 1) * 128], wdnr[r][:, half * 512:(half + 1) * 512],
                                       j == 0, j == NFC - 1)], [B(f"gt{gs}"), B(f"fwd{r}")], pwrites=[BP[bank]])

                if not state.get("ffn_pre"):
                    for j in range(min(NR, NFC)):
                        load_wu(j)
                        load_wd(j)
                    state["ffn_pre"] = True
                up(0)
                for j in range(NFC):
                    if j + 1 < NFC:
                        up(j + 1)
                    if j + NR < NFC:
                        load_wu(j + NR)
                    conv(j)
                    down(j)
                    if j + NR < NFC:
                        load_wd(j + NR)
                for j in range(min(NR, NFC)):
                    load_wu(j)
                    load_wd(j)
                ckpt(12)
                for ti in range(2):
                    hs_ = hslots[ti]
                    bh = B(f"hres{hs_}")
                    for half in range(2):
                        bank = 4 + ti * 2 + half
                        hsl = slice(half * 512, (half + 1) * 512)
                        tt("dve", hres[hs_][:, hsl], hres[hs_][:, hsl], pb[bank][:, :], ALU.add, reads=[bh, BP[bank]],
                           writes=[bh] if half == 0 else (), pwrites=() if half == 0 else [bh])
                for ti in range(2):
                    hs_ = hslots[ti]
                    bh = B(f"hres{hs_}")
                    c0_ = 56 + 3 * ti
                    act(xn2[:], hres[hs_][:], AF.Square, [bh], [B("xn2"), B(f"smf{ti}a")], accum_out=sm[:, c0_:c0_ + 1])
                    act(sm[:, c0_ + 1:c0_ + 2], sm[:, c0_:c0_ + 1], AF.Ln, [B(f"smf{ti}a")], [B(f"smf{ti}b")], scale=1.0 / D, bias=EPS)
                    act(sm[:, c0_ + 2:c0_ + 3], sm[:, c0_ + 1:c0_ + 2], AF.Exp, [B(f"smf{ti}b")], [B(f"smf{ti}c")], scale=-0.5)
                    stt("dve", hres[hs_][:], hres[hs_][:], sm[:, c0_ + 2:c0_ + 3], nfw[:], ALU.mult, ALU.mult,
                        reads=[bh, B(f"smf{ti}c"), B("nfw")], writes=[bh])
                    dst, r0 = out_rows[ti]
                    P.dma(f"st{hs_}", [lambda e, dst=dst, r0=r0, hs_=hs_: e.dma_start(out=dst[r0:r0 + 128, :], in_=hres[hs_][:])],
                          reads=[bh], queue="pool")

            def passB(job):
                n = job["n"]
                lo2, hi2 = job["s2"]
                lo3, hi3 = job["s3"]
                dyn = job["dyn"]
                hctr = [0]
                hslot_of = {}
                unit_of = {}

                def unit_slot(u):
                    return u % NU

                interleave(s1_kv(job, 0))
                ckpt(5)
                pend = {"n": None, "f": None}

                def run_ffn():
                    if pend["f"] is not None:
                        u = pend["f"]
                        pend["f"] = None
                        ca_, cb_ = lo3 + 2 * u, lo3 + 2 * u + 1
                        s3_ffn(job, unit_slot(u), [hslot_of[ca_], hslot_of[cb_]],
                               [(job["y"], (ca_ - lo3) * 128), (job["y"], (cb_ - lo3) * 128)])
                        ckpt(11)

                for c in range(n):
                    gD = s1_kv(job, c + 1) if c + 1 < n else None
                    gN, pend["n"] = pend["n"], None
                    if not (lo2 <= c < hi2):
                        interleave(gD, gN)
                        run_ffn()
                        continue
                    interleave(s1_rest(job, c), gD, gN)
                    ckpt(6)
                    if job["has_cdb"](c):
                        cs = 0
                        load(f"cl{cs}", Cdb[cs][:].rearrange("p h c -> p (h c)"), cdb_s[c], B(f"Cdb{cs}"),
                             reads=[B(f"cdb_s{c}")])
                    else:
                        cs = None
                    run_ffn()
                    hs_ = hctr[0] % NH
                    hctr[0] += 1
                    hslot_of[c] = hs_
                    load(f"hl{hs_}", hres[hs_][:], job["x"][c * 128:(c + 1) * 128, :], B(f"hres{hs_}"))
                    blocks = []
                    if c - 1 >= 0:
                        blocks.append(("prev", c - 1, job["kcol"](c - 1)))
                    blocks.append(("cur", c, job["kcol"](c)))
                    if c + 1 < n:
                        blocks.append(("next", c + 1, job["kcol"](c + 1)))
                    interleave(s2_mlstm(job, c, job["fwd_zero"] and c == lo2, cs), s2_attn(job, c, blocks))
                    s2_merge_gates(c, 0)
                    s2_mlstm_T()
                    s2_attn_T()
                    ckpt(8)
                    s2_merge(job, c, hs_)
                    ckpt(9)
                    if DBG_STOP == (job["tab0"], c):
                        raise _Stop()
                    targets = []
                    if lo3 <= c < hi3:
                        u = (c - lo3) // 2
                        pos = (c - lo3) % 2
                        targets.append((unit_slot(u), 1 + pos * 128, 0, 128, None))
                        if pos == 0 and c - 1 >= lo3:
                            targets.append((unit_slot(u - 1), 257, 0, 1, None))
                        if pos == 1 and c + 1 < hi3:
                            targets.append((unit_slot(u + 1), 0, 127, 1, None))
                    elif c == lo3 - 1:
                        targets.append((unit_slot(0), 0, 127, 1, 2 if dyn else None))
                    elif c == hi3:
                        targets.append((unit_slot((hi3 - lo3) // 2 - 1), 257, 0, 1, 3 if dyn else None))
                    if targets:
                        pend["n"] = s2_norm2(job, c, hs_, targets)
                    if lo3 <= c < hi3 and not dyn:
                        if c == lo3:
                            P.op("dve", lambda e, us=unit_slot(0): e.memset(unit[us][:, :, 0:1], 0.0), pwrites=[B(f"unit{unit_slot(0)}")])
                        if c == hi3 - 1:
                            ul = unit_slot((hi3 - lo3) // 2 - 1)
                            P.op("dve", lambda e, us=ul: e.memset(unit[us][:, :, 257:258], 0.0), pwrites=[B(f"unit{ul}")])
                    if c >= lo3 and c - 1 >= lo3 and (c - 1 - lo3) % 2 == 1 and c - 1 < hi3:
                        pend["f"] = (c - 1 - lo3) // 2
                interleave(pend["n"])
                pend["n"] = None
                run_ffn()
                if hi3 == n:
                    pend["f"] = (hi3 - lo3) // 2 - 1
                    run_ffn()

            zero_state()
            fsave = hres[2][:, 0:516].rearrange("p (h c) -> p h c", c=129)
            P.op("pool", lambda e: e.memset(hres[2][:, 0:516], 0.0), writes=[B("hres2")])
            scan(x_ctx, [i * 128 for i in range(LCTX)], 0, ctx_flags=c_ctxf)
            ckpt(2)
            rows = [c * 128 for c in range(NPB - 2, 0, -1)]
            scan(x_own, rows, 1, store_slots=list(range(NPB - 2, 0, -1)))
            cp("pool", Cst[:], fsave, reads=[B("hres2")], writes=[B("Cst")])
            ckpt(3)
            if state.get("bg") is not None:
                for _ in state["bg"]:
                    pass
                state["bg"] = None
            pj = dict(x=x_own, n=NPB, s2=(1, NPB - 1), s3=(2, NPB - 2), dyn=True, tab0=0, y=y_own, fwd_zero=False,
                      has_cdb=lambda c: True,
                      kcol=lambda c: 0 if c < 2 else (1 if c >= NPB - 2 else None))
            passB(pj)
            ckpt(4)
            for s in range(NS):
                xs_ = x_s[s * SC * 128:(s + 1) * SC * 128, :]
                ys_ = y_s[s * SC * 128:(s + 1) * SC * 128, :]
                zero_state()
                scan(xs_, [c * 128 for c in range(SC - 1, -1, -1)], 1, store_slots=list(range(SC - 1, -1, -1)))
                sj = dict(x=xs_, n=SC, s2=(0, SC), s3=(0, SC), dyn=False, tab0=NPB, y=ys_, fwd_zero=True,
                          has_cdb=lambda c: True, kcol=lambda c: None)
                zero_state()
                passB(sj)

        except _Stop:
            pass
        P.finish()
        with nc.Block() as block:
            P.replay(block)
    return nc


def _consts():
    s = np.arange(128)[:, None]
    t = np.arange(128)[None, :]
    U = (s <= t).astype(np.float32)
    UT = (s >= t).astype(np.float32)
    tri = np.concatenate([U, UT, np.ones((128, 128), np.float32), np.eye(128, dtype=np.float32)], axis=1)
    mprev = np.where(s >= t, 0.0, NEG).astype(np.float32)
    mnext = np.where(s <= t, 0.0, NEG).astype(np.float32)
    mb = np.concatenate([mprev, mprev, mnext, mnext], axis=1).astype(ml_dtypes.bfloat16)
    ident = np.eye(128, dtype=np.float32).astype(ml_dtypes.bfloat16)
    return ident, tri, mb


def _rope_rows(pos):
    inv = (np.float32(ROPE_THETA) ** (-np.arange(8, dtype=np.float32) / np.float32(8))).astype(np.float32)
    ang = (pos.astype(np.float32)[:, None] * inv[None, :]).astype(np.float32)
    return np.concatenate([np.cos(ang.astype(np.float64)), np.sin(ang.astype(np.float64))], axis=1).astype(np.float32)


_PROGS = {}


def run_layer(inputs, PC, SC, NS):
    xp = np.ascontiguousarray(inputs["x_prompt"][0], dtype=np.float32)
    xsamp = np.ascontiguousarray(inputs["x_sample"], dtype=np.float32)
    NPB, LCTX = PC + 4, 7 * PC - 1
    nchunks = xp.shape[0] // 128
    assert nchunks == NCORES * PC and xsamp.shape[0] == NCORES * NS and xsamp.shape[1] == SC * 128

    def take(g0, n):
        out = np.zeros((n * 128, D), np.float32)
        lo, hi = max(g0, 0), min(g0 + n, nchunks)
        if hi > lo:
            out[(lo - g0) * 128:(hi - g0) * 128] = xp[lo * 128:hi * 128]
        return out

    ident, tri, mb = _consts()
    f32 = lambda a: np.ascontiguousarray(a, dtype=np.float32)
    shared = dict(
        w_in=f32(inputs["w_in"][0]), w_pm=f32(inputs["w_proj_m"][0]), w_pa=f32(inputs["w_proj_a"][0]),
        w_out=f32(inputs["w_out"][0]), w_up=f32(inputs["w_up"][0]), w_dn=f32(inputs["w_down"][0]),
        norm1=f32(inputs["norm1_w"][0]), norm2=f32(inputs["norm2_w"][0]), normf=f32(inputs["norm_f_w"]),
        mhw=f32(inputs["mh_norm_w"][0]),
        gbias=f32(np.concatenate([inputs["i_bias"][0].reshape(-1), inputs["f_bias"][0].reshape(-1)])),
        sink=f32(inputs["attn_sink"][0]), convw=f32(inputs["conv_w"][0]), convb=f32(inputs["conv_b"][0]),
        c_ident=ident, c_tri=tri, c_mb=mb)
    p = np.arange(128)
    in_maps = []
    for i in range(NCORES):
        g0 = i * PC - 2
        rope = np.stack([_rope_rows((g0 + l) * 128 + p) for l in range(NPB)] +
                        [_rope_rows(c * 128 + p) for c in range(SC)], axis=0)
        fl = np.zeros((128, 4), np.float32)
        fl[:, 0] = NEG if i == 0 else 0.0
        fl[:, 1] = NEG if i == NCORES - 1 else 0.0
        fl[:, 2] = 0.0 if i == 0 else 1.0
        fl[:, 3] = 0.0 if i == NCORES - 1 else 1.0
        m = dict(shared)
        nF, nB = max(i * PC - 1, 0), max((NCORES - 1 - i) * PC - 1, 0)
        npad = LCTX - nF - nB
        order = list(range(0, nF)) + list(range(nchunks - 1, nchunks - 1 - nB, -1))
        x_ctx = np.zeros((LCTX * 128, D), np.float32)
        for k, gch in enumerate(order):
            x_ctx[(npad + k) * 128:(npad + k + 1) * 128] = xp[gch * 128:(gch + 1) * 128]
        f0 = np.array([1.0] * (npad + nF) + [0.0] * nB, np.float32)
        cf = np.zeros((LCTX, 128, 16), np.float32)
        cf[:, :, 0:4] = f0[:, None, None]
        cf[:, :, 4:8] = 1.0 - f0[:, None, None]
        cf[:, :, 8:12] = f0[:, None, None]
        cf[:, :, 12:16] = 1.0
        m.update(x_own=take(g0, NPB), x_ctx=x_ctx, c_ctxf=cf,
                 x_s=np.ascontiguousarray(xsamp[i * NS:(i + 1) * NS].reshape(NS * SC * 128, D)),
                 c_rope=rope, c_flags=fl)
        in_maps.append(m)
    key = (PC, SC, NS)
    if key not in _PROGS:
        _PROGS[key] = build_program(PC, SC, NS)
    res = run_bass_kernel_spmd(_PROGS[key], in_maps, core_ids=list(range(NCORES)))
    y_p = np.concatenate([r["y_own"] for r in res.results], axis=0)[None]
    y_s = np.stack([r["y_s"].reshape(NS, SC * 128, D) for r in res.results], axis=0).reshape(NCORES * NS, SC * 128, D)
    return y_p.astype(np.float32), y_s.astype(np.float32)


def kernel(**inputs):
    return run_layer(inputs, PC=16, SC=32, NS=2)
```
